# Optimizing a Trainium2 kernel written in Bass

```python
import math
import jax, jax.numpy as jnp
from jax import lax
import numpy as np

D_MODEL = 1024
BATCH = 4
SEQ = 4096
DEPTH = 4

HEAD_DIM = 64
BLOCK = 128
A_HEADS = 8
A_KV_HEADS = 2
A_WINDOW = 128
B_HEADS = 8
B_CONFIGS = ((128, 1), (512, 4), (2048, 16))
NUM_BUCKETS = 32
MAX_DISTANCE = 2048
C_HEADS = 16
C_Q_RANK = 256
C_KV_RANK = 128
C_NOPE = 64
C_ROPE = 32
C_V = 64
ROPE_THETA = 10000.0
EPS = 1e-6

A_WIDTH = A_HEADS * HEAD_DIM
A_KV_WIDTH = A_KV_HEADS * HEAD_DIM
B_WIDTH = B_HEADS * HEAD_DIM
AB_WIDTH = A_WIDTH + B_WIDTH
AB_IN = A_WIDTH + 2 * A_KV_WIDTH + 3 * B_WIDTH + AB_WIDTH
C_WIDTH = C_HEADS * C_V
C_IN = C_Q_RANK + C_KV_RANK + C_ROPE + C_WIDTH
N_AB = (DEPTH + 1) // 2
N_C = DEPTH // 2

kernel_name = "hybrid_swa_dilated_mla_gated"


def rmsnorm(x, w):
    xf = x.astype(jnp.float32)
    y = xf * lax.rsqrt(jnp.mean(xf * xf, axis=-1, keepdims=True) + EPS)
    return (y * w.astype(jnp.float32)).astype(x.dtype)


def t5_bucket(dist):
    max_exact = NUM_BUCKETS // 2
    d = jnp.maximum(dist, 1).astype(jnp.float32)
    large = max_exact + (jnp.log(d / max_exact) / math.log(MAX_DISTANCE / max_exact)
                         * (NUM_BUCKETS - max_exact)).astype(jnp.int32)
    large = jnp.minimum(large, NUM_BUCKETS - 1)
    return jnp.where(dist < max_exact, dist, large)


def band_bias(rel_bias_h, dilation, window):
    i = jnp.arange(BLOCK)[:, None]
    j = jnp.arange(2 * BLOCK)[None, :]
    dist = i + BLOCK - j
    in_band = (dist >= 0) & (dist <= window)
    bucket = t5_bucket(jnp.maximum(dist, 0) * dilation)
    bias = jnp.transpose(rel_bias_h[bucket].astype(jnp.float32), (2, 0, 1))
    return bias, in_band


def banded_attention(q, k, v, bias, in_band, sink=None):
    n, length, hq, dh = q.shape
    hkv = k.shape[2]
    g = hq // hkv
    nb = -(-length // BLOCK)
    padw = ((0, 0), (0, nb * BLOCK - length), (0, 0), (0, 0))
    qb = jnp.pad(q, padw).reshape(n, nb, BLOCK, hkv, g, dh)
    kb = jnp.pad(k, padw).reshape(n, nb, BLOCK, hkv, dh)
    vb = jnp.pad(v, padw).reshape(n, nb, BLOCK, hkv, dh)

    def with_prev(t):
        prev = jnp.pad(t, ((0, 0), (1, 0), (0, 0), (0, 0), (0, 0)))[:, :-1]
        return jnp.concatenate([prev, t], axis=2)

    kk, vv = with_prev(kb), with_prev(vb)
    s = jnp.einsum('nbqhgd,nbkhd->nbhgqk', qb, kk).astype(jnp.float32) * (dh ** -0.5)
    s = s + bias.reshape(hkv, g, BLOCK, 2 * BLOCK)
    kpos = (jnp.arange(nb)[:, None] - 1) * BLOCK + jnp.arange(2 * BLOCK)[None, :]
    mask = in_band[None] & (kpos >= 0)[:, None, :]
    s = jnp.where(mask[None, :, None, None], s, -jnp.inf)
    m = jnp.max(s, axis=-1, keepdims=True)
    if sink is not None:
        sk = sink.astype(jnp.float32).reshape(1, 1, hkv, g, 1, 1)
        m = jnp.maximum(m, sk)
    p = jnp.exp(s - m)
    denom = jnp.sum(p, axis=-1, keepdims=True)
    if sink is not None:
        denom = denom + jnp.exp(sk - m)
    o = jnp.einsum('nbhgqk,nbkhd->nbqhgd', (p / denom).astype(v.dtype), vv)
    o = o.reshape(n, nb * BLOCK, hq, dh)[:, :length]
    lse = (m + jnp.log(denom))[..., 0]
    lse = jnp.transpose(lse, (0, 1, 4, 2, 3)).reshape(n, nb * BLOCK, hq)[:, :length]
    return o, lse


def to_strided(t, d):
    b, s = t.shape[:2]
    sp = -(-s // d) * d
    t = jnp.pad(t, ((0, 0), (0, sp - s)) + ((0, 0),) * (t.ndim - 2))
    t = jnp.moveaxis(t.reshape((b, sp // d, d) + t.shape[2:]), 2, 1)
    return t.reshape((b * d, sp // d) + t.shape[3:])


def from_strided(t, b, s, d):
    length = t.shape[1]
    t = jnp.moveaxis(t.reshape((b, d, length) + t.shape[2:]), 1, 2)
    return t.reshape((b, length * d) + t.shape[3:])[:, :s]


def swa_dilated_layer(h, w_in, sinks, w_out, rel_bias):
    b, s, _ = h.shape
    splits = list(np.cumsum([A_WIDTH, A_KV_WIDTH, A_KV_WIDTH, B_WIDTH, B_WIDTH, B_WIDTH]))
    qa, ka, va, qb, kb, vb, z = jnp.split(h @ w_in, splits, axis=-1)
    qa = qa.reshape(b, s, A_HEADS, HEAD_DIM)
    ka = ka.reshape(b, s, A_KV_HEADS, HEAD_DIM)
    va = va.reshape(b, s, A_KV_HEADS, HEAD_DIM)
    qb, kb, vb = (t.reshape(b, s, B_HEADS, HEAD_DIM) for t in (qb, kb, vb))

    bias_a, band_a = band_bias(rel_bias[:, :A_HEADS], 1, A_WINDOW)
    oa, _ = banded_attention(qa, ka, va, bias_a, band_a, sinks)

    outs, lses = [], []
    for window, dil in B_CONFIGS:
        bias_b, band_b = band_bias(rel_bias[:, A_HEADS:], dil, window // dil)
        o, lse = banded_attention(to_strided(qb, dil), to_strided(kb, dil),
                                  to_strided(vb, dil), bias_b, band_b)
        outs.append(from_strided(o, b, s, dil))
        lses.append(from_strided(lse, b, s, dil))
    wts = jax.nn.softmax(jnp.stack(lses), axis=0)
    ob = jnp.sum(wts[..., None] * jnp.stack(outs).astype(jnp.float32), axis=0).astype(h.dtype)

    o = jnp.concatenate([oa.reshape(b, s, A_WIDTH), ob.reshape(b, s, B_WIDTH)], axis=-1)
    return (o * jax.nn.silu(z)) @ w_out


def rope(t, cos, sin):
    half = t.shape[-1] // 2
    t1, t2 = t[..., :half], t[..., half:]
    return jnp.concatenate([t1 * cos - t2 * sin, t1 * sin + t2 * cos], axis=-1).astype(t.dtype)


def mla_layer(h, positions, w_in, q_norm, w_qb, kv_norm, w_kvb, w_out):
    b, s, _ = h.shape
    splits = list(np.cumsum([C_Q_RANK, C_KV_RANK, C_ROPE]))
    c_q, c_kv, k_pe, z = jnp.split(h @ w_in, splits, axis=-1)
    q = (rmsnorm(c_q, q_norm) @ w_qb).reshape(b, s, C_HEADS, C_NOPE + C_ROPE)
    q_nope, q_pe = q[..., :C_NOPE], q[..., C_NOPE:]
    kv = (rmsnorm(c_kv, kv_norm) @ w_kvb).reshape(b, s, C_HEADS, C_NOPE + C_V)
    k_nope, v = kv[..., :C_NOPE], kv[..., C_NOPE:]

    inv_freq = ROPE_THETA ** (-jnp.arange(0, C_ROPE, 2, dtype=jnp.float32) / C_ROPE)
    ang = positions.astype(jnp.float32)[..., None] * inv_freq
    cos, sin = jnp.cos(ang)[:, :, None], jnp.sin(ang)[:, :, None]
    q_pe = rope(q_pe, cos, sin)
    k_pe = rope(k_pe[:, :, None], cos, sin)[:, :, 0]

    scale = (C_NOPE + C_ROPE) ** -0.5
    nb = s // BLOCK
    kpos = jnp.arange(s)

    def to_blocks(t):
        return jnp.moveaxis(t.reshape((b, nb, BLOCK) + t.shape[2:]), 1, 0)

    def attend_block(args):
        qn, qp, blk = args
        sc = (jnp.einsum('bqhd,bkhd->bhqk', qn, k_nope)
              + jnp.einsum('bqhr,bkr->bhqk', qp, k_pe)).astype(jnp.float32) * scale
        qpos = blk * BLOCK + jnp.arange(BLOCK)
        sc = jnp.where(kpos[None, :] <= qpos[:, None], sc, -jnp.inf)
        p = jax.nn.softmax(sc, axis=-1).astype(v.dtype)
        return jnp.einsum('bhqk,bkhd->bqhd', p, v)

    o = lax.map(attend_block, (to_blocks(q_nope), to_blocks(q_pe), jnp.arange(nb)))
    o = jnp.moveaxis(o, 0, 1).reshape(b, s, C_WIDTH)
    return (o * jax.nn.silu(z)) @ w_out


def setup_inputs(seed: int = 0) -> dict:
    key = jax.random.key(seed)
    ks = jax.random.split(key, 16)
    f32 = jnp.float32

    def dense(k, shape):
        return jax.random.normal(k, shape, f32) * shape[-2] ** -0.5

    def gain(k, shape):
        return 1.0 + 0.02 * jax.random.normal(k, shape, f32)

    return {
        "x": jax.random.normal(ks[0], (BATCH, SEQ, D_MODEL), f32),
        "positions": jnp.tile(jnp.arange(SEQ, dtype=jnp.int32)[None, :], (BATCH, 1)),
        "norm_w": gain(ks[1], (DEPTH, D_MODEL)),
        "rel_bias": 0.5 * jax.random.normal(ks[2], (NUM_BUCKETS, A_HEADS + B_HEADS), f32),
        "ab_w_in": dense(ks[3], (N_AB, D_MODEL, AB_IN)),
        "ab_sinks": jax.random.normal(ks[4], (N_AB, A_HEADS), f32),
        "ab_w_out": dense(ks[5], (N_AB, AB_WIDTH, D_MODEL)),
        "c_w_in": dense(ks[6], (N_C, D_MODEL, C_IN)),
        "c_q_norm": gain(ks[7], (N_C, C_Q_RANK)),
        "c_w_qb": dense(ks[8], (N_C, C_Q_RANK, C_HEADS * (C_NOPE + C_ROPE))),
        "c_kv_norm": gain(ks[9], (N_C, C_KV_RANK)),
        "c_w_kvb": dense(ks[10], (N_C, C_KV_RANK, C_HEADS * (C_NOPE + C_V))),
        "c_w_out": dense(ks[11], (N_C, C_WIDTH, D_MODEL)),
        "final_norm": gain(ks[12], (D_MODEL,)),
    }


def reference(x, positions, norm_w, rel_bias, ab_w_in, ab_sinks, ab_w_out,
              c_w_in, c_q_norm, c_w_qb, c_kv_norm, c_w_kvb, c_w_out, final_norm):
    for layer in range(DEPTH):
        h = rmsnorm(x, norm_w[layer])
        i = layer // 2
        if layer % 2 == 0:
            y = swa_dilated_layer(h, ab_w_in[i], ab_sinks[i], ab_w_out[i], rel_bias)
        else:
            y = mla_layer(h, positions, c_w_in[i], c_q_norm[i], c_w_qb[i],
                          c_kv_norm[i], c_w_kvb[i], c_w_out[i])
        x = x + y
    return rmsnorm(x, final_norm)
```

```python
import contextlib
import math
import numpy as np
import ml_dtypes
import concourse.bass as bass
import concourse.mybir as mybir
from concourse.bass_utils import run_bass_kernel_spmd

F32 = mybir.dt.float32
BF16 = mybir.dt.bfloat16
I32 = mybir.dt.int32
AF = mybir.ActivationFunctionType
ALU = mybir.AluOpType

T = 4096
D = 1024
NB = 32
NT = 8
NEG = -30000.0
PAIRS = [[0, 1], [2, 3], [4, 5], [6, 7]]
B_CFG = (1, 4, 16)


class Tok:
    __slots__ = ("w", "r")

    def __init__(self):
        self.w = None
        self.r = []


class Prog:
    ENG = ("pe", "act", "dve", "pool", "sp")

    def __init__(self, nc, stack, n_dma_sems=20):
        self.nc = nc
        self.q = {e: [] for e in self.ENG}
        self.cnt = {e: 0 for e in self.ENG}
        self.waited = {e: {} for e in self.ENG}
        self.sem = {e: stack.enter_context(nc.semaphore("s_" + e)) for e in self.ENG}
        self.semid = {id(self.sem[e]): e for e in self.ENG}
        self.dsem, self.dval, self.drr = {}, {}, {}
        for qe in ("sp", "pool", "act"):
            self.dsem[qe] = [stack.enter_context(nc.semaphore(f"d_{qe}_{i}")) for i in range(n_dma_sems)]
            self.dval[qe] = [0] * n_dma_sems
            self.drr[qe] = 0
        self.cc_sem = stack.enter_context(nc.semaphore("cc"))
        self.cc_val = 0
        self.n_inst = 0

    def _collect(self, e, reads, writes, extra=()):
        deps = {}

        def add(ev):
            if ev is None:
                return
            s, v = ev
            k = id(s)
            if k not in deps or deps[k][1] < v:
                deps[k] = (s, v)

        for t in reads:
            add(t.w)
        for t in writes:
            add(t.w)
            for ev in t.r:
                add(ev)
        for ev in extra:
            add(ev)
        out = []
        for k, (s, v) in deps.items():
            if self.semid.get(k) == e and e == "pe":
                continue
            if self.waited[e].get(k, 0) >= v:
                continue
            self.waited[e][k] = v
            out.append((s, v))
        return out

    def _mark(self, ev, reads, writes):
        for t in reads:
            t.r.append(ev)
            if len(t.r) > 64:
                t.r = _compress(t.r)
        for t in writes:
            t.w = ev
            t.r = []

    def op(self, e, fn, reads=(), writes=(), extra=()):
        waits = self._collect(e, reads, writes, extra)
        self.cnt[e] += 1
        sem = self.sem[e]
        ev = (sem, self.cnt[e])

        def emit(eng, fn=fn, waits=waits, sem=sem):
            for (s, v) in waits:
                eng.wait_ge(s, v)
            fn(eng).then_inc(sem, 1)

        self.q[e].append(emit)
        self._mark(ev, reads, writes)
        self.n_inst += 1
        return ev

    def dma(self, qe, out, in_, reads=(), writes=(), extra=()):
        k = self.drr[qe]
        self.drr[qe] = (k + 1) % len(self.dsem[qe])
        s = self.dsem[qe][k]
        prev = self.dval[qe][k]
        self.dval[qe][k] = prev + 16
        ev = (s, prev + 16)
        ex = list(extra)
        if prev > 0:
            ex.append((s, prev))
        waits = self._collect(qe, reads, writes, ex)

        def emit(eng, waits=waits, s=s, out=out, in_=in_):
            for (ws, v) in waits:
                eng.wait_ge(ws, v)
            eng.dma_start(out=out, in_=in_).then_inc(s, 16)

        self.q[qe].append(emit)
        self._mark(ev, reads, writes)
        self.n_inst += 1
        return ev

    def collective(self, fn, reads, writes):
        waits = self._collect("pool", reads, writes)
        self.cc_val += 1
        ev = (self.cc_sem, self.cc_val)

        def emit(eng, waits=waits, fn=fn):
            for (s, v) in waits:
                eng.wait_ge(s, v)
            fn(eng).then_inc(self.cc_sem, 1)

        self.q["pool"].append(emit)
        self._mark(ev, reads, writes)
        return ev

    def wait_all(self, e, evs):
        waits = self._collect(e, (), (), evs)

        def emit(eng, waits=waits):
            for (s, v) in waits:
                eng.wait_ge(s, v)

        self.q[e].append(emit)

    def all_events(self):
        evs = [(self.sem[e], self.cnt[e]) for e in self.ENG if self.cnt[e] > 0]
        for qe in self.dsem:
            for s, v in zip(self.dsem[qe], self.dval[qe]):
                if v > 0:
                    evs.append((s, v))
        if self.cc_val > 0:
            evs.append((self.cc_sem, self.cc_val))
        return evs

    def barrier(self):
        evs = self.all_events()
        for e in self.ENG:
            self.wait_all(e, evs)

    def emit_all(self):
        nc = self.nc
        with nc.Block() as block:
            @block.tensor
            def _(eng):
                for f in self.q["pe"]:
                    f(eng)

            @block.scalar
            def _(eng):
                for f in self.q["act"]:
                    f(eng)

            @block.vector
            def _(eng):
                for f in self.q["dve"]:
                    f(eng)

            @block.gpsimd
            def _(eng):
                for f in self.q["pool"]:
                    f(eng)

            @block.sync
            def _(eng):
                for f in self.q["sp"]:
                    f(eng)


def _compress(evs):
    best = {}
    for s, v in evs:
        k = id(s)
        if k not in best or best[k][1] < v:
            best[k] = (s, v)
    return list(best.values())


class Rot:
    def __init__(self, items):
        self.items = [(it, Tok()) for it in items]
        self.i = 0

    def next(self):
        it = self.items[self.i]
        self.i = (self.i + 1) % len(self.items)
        return it


def build_program(nlayers=4, raw=False, stage=99):
    nc = bass.Bass("TRN2", target_bir_lowering=False)

    def din(name, shape, dt):
        return nc.dram_tensor(name, list(shape), dt, kind="ExternalInput").ap()

    x_in = din("x", [T, D], F32)
    pos_in = din("pos", [1, T], I32)
    nw_in = din("nw", [5, D], F32)
    relb_in = din("relb", [33, 8], F32)
    oneh_in = din("oneh", [33, 3 * 383], F32)
    sinks_in = din("sinks", [1, 8], F32)
    ident_in = din("ident", [128, 128], BF16)
    tri_in = din("tri", [128, 128], BF16)
    ropec_in = din("ropec", [128, 2], F32)
    cn_in = din("cn", [1, 2 * 384], F32)
    abw_in = [din(f"abw{i}", [D, 1664], F32) for i in range(2)]
    abwo_in = [din(f"abwo{i}", [D, D], F32) for i in range(2)]
    cw_in = [din(f"cw{i}", [D, 1024], F32) for i in range(2)]
    cqb_in = [din(f"cqb{i}", [256, 1536], F32) for i in range(2)]
    ckvb_in = [din(f"ckvb{i}", [128, 1536], F32) for i in range(2)]
    cwo_in = [din(f"cwo{i}", [D, D], F32) for i in range(2)]
    out = nc.dram_tensor("out", [T, D], F32, kind="ExternalOutput").ap()

    xs = nc.dram_tensor("xs", [T, D], F32).ap()
    go_own = [nc.dram_tensor(f"go_own{t}", [512, 512], BF16) for t in range(NT)]
    go_full = [[nc.dram_tensor(f"go_full{i}_{t}", [1024, 512], BF16) for t in range(NT)] for i in range(2)]
    gd = nc.dram_tensor("gd", [16, 128 * 383], F32)
    vd = nc.dram_tensor("vd", [T, 4 * 65], BF16).ap()
    csd = nc.dram_tensor("csd", [2, 64, T], F32).ap()

    with contextlib.ExitStack() as st:
        def sb(name, shape, dt):
            return st.enter_context(nc.sbuf_tensor(name, list(shape), dt))

        def psum(name, shape, dt):
            return st.enter_context(nc.psum_tensor(name, list(shape), dt))

        pr = Prog(nc, st)

        HT = sb("HT", [128, 8, T], BF16)
        WO = sb("WO", [128, 8, D], BF16)
        QT = sb("QT", [128, T], BF16)
        KT = sb("KT", [128, T], BF16)
        XB = Rot([sb(f"XB{i}", [128, D], F32) for i in range(2)])
        HB = Rot([sb(f"HB{i}", [128, D], BF16) for i in range(2)])
        PT = Rot([sb(f"PT{i}", [128, 512], BF16) for i in range(3)])
        STMP = Rot([sb(f"STMP{i}", [128, 512], F32) for i in range(2)])
        GI = Rot([sb(f"GI{i}", [128, 8, 128], BF16) for i in range(2)])
        GOUT = Rot([sb(f"GOUT{i}", [64, 512], BF16) for i in range(2)])
        WSTG = Rot([sb(f"WSTG{i}", [128, 8, 64], F32) for i in range(2)])
        NW = sb("NW", [128, D], F32)
        ST = Rot([sb(f"ST{i}", [128, 8], F32) for i in range(4)])
        IDENT = sb("IDENT", [128, 128], BF16)
        TRI = sb("TRI", [128, 128], BF16)
        ONES = sb("ONES", [128, 64], F32)
        EPS = sb("EPS", [128, 1], F32)
        ES = sb("ES", [128, 8], F32)
        EPA = Rot([sb(f"EPA{i}", [64, 512], F32) for i in range(2)])
        EPB = Rot([sb(f"EPB{i}", [64, 512], F32) for i in range(2)])
        RD = Rot([sb(f"RD{i}", [65, 512], F32) for i in range(1)])
        ARENA = sb("ARENA", [128, 16384], F32)

        t_HT = [Tok() for _ in range(NB)]
        t_WO, t_QT, t_KT, t_NW = Tok(), Tok(), Tok(), Tok()
        t_const = Tok()
        t_arena = Tok()

        SBK = Rot([psum(f"S{i}", [128, 512], F32) for i in range(2)])
        OBK = Rot([psum(f"O{i}", [128, 512], F32) for i in range(2)])
        PJ = Rot([psum(f"PJ{i}", [128, 512], F32) for i in range(3)])
        TR = psum("TR", [128, 1024], BF16)
        t_TR = Tok()

        def mm(out_, lhsT, rhs, start, stop, reads, writes):
            return pr.op("pe", lambda e: e.matmul(out_, lhsT=lhsT, rhs=rhs, start=start, stop=stop,
                                                  skip_group_check=True), reads=reads, writes=writes)

        def act(out_, in_, func, reads, writes, **kw):
            return pr.op("act", lambda e: e.activation(out=out_, in_=in_, func=func, **kw), reads=reads, writes=writes)

        def copy_any(eng, out_, in_, reads, writes):
            if eng == "act":
                return pr.op("act", lambda e: e.copy(out=out_, in_=in_), reads=reads, writes=writes)
            return pr.op(eng, lambda e: e.tensor_copy(out=out_, in_=in_), reads=reads, writes=writes)

        def tt(eng, out_, in0, in1, op, reads, writes):
            return pr.op(eng, lambda e: e.tensor_tensor(out=out_, in0=in0, in1=in1, op=op), reads=reads, writes=writes)

        def ts(eng, out_, in0, s1, s2, op0, op1, reads, writes):
            if s2 is None:
                return pr.op(eng, lambda e: e.tensor_scalar(out=out_, in0=in0, scalar1=s1, scalar2=None, op0=op0),
                             reads=reads, writes=writes)
            return pr.op(eng, lambda e: e.tensor_scalar(out=out_, in0=in0, scalar1=s1, scalar2=s2, op0=op0, op1=op1),
                         reads=reads, writes=writes)

        def stt(eng, out_, in0, scalar, in1, op0, op1, reads, writes):
            return pr.op(eng, lambda e: e.scalar_tensor_tensor(out=out_, in0=in0, scalar=scalar, in1=in1, op0=op0, op1=op1),
                         reads=reads, writes=writes)

        def recip(out_, in_, reads, writes):
            return pr.op("dve", lambda e: e.reciprocal(out=out_, in_=in_), reads=reads, writes=writes)

        def load_w(dst, src, ncols, t_dst):
            kch = dst.shape[1]
            c0 = 0
            while c0 < ncols:
                cw = min(64, ncols - c0)
                stg, t_stg = WSTG.next()
                pr.dma("sp", stg[:, 0:kch, 0:cw], src[:, c0:c0 + cw].rearrange("(kc p) n -> p kc n", p=128),
                       writes=[t_stg])
                copy_any("pool", dst[:, :, c0:c0 + cw], stg[:, 0:kch, 0:cw], [t_stg], [t_dst])
                c0 += cw

        pr.dma("sp", IDENT[:], ident_in[:, :], writes=[t_const])
        pr.dma("sp", TRI[:], tri_in[:, :], writes=[t_const])
        pr.op("pool", lambda e: e.memset(ONES[:], 1.0), writes=[t_const])
        pr.op("pool", lambda e: e.memset(EPS[:], 1e-6), writes=[t_const])
        pr.dma("sp", ES[:], sinks_in[0:1, :].partition_broadcast(128), writes=[t_const])
        act(ES[:], ES[:], AF.Exp, [t_const], [t_const])

        A32 = ARENA
        RELB = A32[0:33, 0:8]
        ONEH = A32[0:33, 8:8 + 3 * 383]
        LB = A32[0:33, 1200:1328]
        GSB = [A32[:, 1400:1783], A32[:, 1800:2183]]
        t_gsb = [Tok(), Tok()]
        t_gd = Tok()
        t_lb = Tok()
        pr.dma("sp", RELB, relb_in[:, :], writes=[t_arena])
        pr.dma("sp", ONEH, oneh_in[:, :], writes=[t_arena])
        for hc in range(16):
            col = hc if hc < 4 else 4 + (hc - 4) % 4
            cfg = 0 if hc < 8 else (1 if hc < 12 else 2)
            copy_any("dve", LB, RELB[:, col:col + 1].to_broadcast([33, 128]), [t_arena], [t_lb])
            pj, t_pj = PJ.next()
            mm(pj[:, 0:383], LB, ONEH[:, cfg * 383:(cfg + 1) * 383], True, True, [t_lb, t_arena], [t_pj])
            copy_any("dve", GSB[hc % 2], pj[:, 0:383], [t_pj], [t_gsb[hc % 2]])
            pr.dma("sp", gd[hc:hc + 1, :].rearrange("o (p n) -> (o p) n", p=128), GSB[hc % 2], reads=[t_gsb[hc % 2]], writes=[t_gd])

        def skew_ap(hc, prev):
            return bass.AP(tensor=gd.ap().tensor, offset=hc * 128 * 383 + (255 if prev else 127), ap=[[382, 128], [1, 128]])

        ROPEC = A32[:, 2200:2202]
        pr.dma("sp", ROPEC, ropec_in[:, :], writes=[t_arena])
        t_cs = Tok()
        RC = 1024
        r_pi = A32[0:64, 4096:4096 + RC].bitcast(I32)
        r_a = A32[0:64, 5120:5120 + RC]
        r_b = A32[0:64, 6144:6144 + RC]
        r_ti = A32[0:64, 7168:7168 + RC].bitcast(I32)
        r_tf = A32[0:64, 8192:8192 + RC]
        r_r = A32[0:64, 9216:9216 + RC]
        r_m = A32[0:64, 10240:10240 + RC]
        r_o = [A32[0:64, 11264:11264 + RC], A32[0:64, 12288:12288 + RC]]
        tr_ = Tok()
        t_ro = [Tok(), Tok()]
        C1 = 6.28125
        C2 = 2.0 * math.pi - 6.28125
        for ch in range(T // RC):
            pr.dma("sp", r_pi, pos_in[0:1, ch * RC:(ch + 1) * RC].partition_broadcast(64), writes=[tr_])
            copy_any("dve", r_a, r_pi, [tr_], [tr_])
            ts("dve", r_a, r_a, ROPEC[0:64, 0:1], None, ALU.mult, None, [tr_, t_arena], [tr_])
            for which in (0, 1):
                shift = math.pi / 2 if which == 0 else 0.0
                ts("dve", r_b, r_a, shift, 1.0 / (2 * math.pi), ALU.add, ALU.mult, [tr_], [tr_])
                copy_any("dve", r_ti, r_b, [tr_], [tr_])
                copy_any("dve", r_tf, r_ti, [tr_], [tr_])
                ts("dve", r_b, r_a, shift, None, ALU.add, None, [tr_], [tr_])
                stt("dve", r_r, r_tf, -C1, r_b, ALU.mult, ALU.add, [tr_], [tr_])
                stt("dve", r_r, r_tf, -C2, r_r, ALU.mult, ALU.add, [tr_], [tr_])
                ts("dve", r_m, r_r, math.pi, None, ALU.is_gt, None, [tr_], [tr_])
                stt("dve", r_r, r_m, -2 * math.pi, r_r, ALU.mult, ALU.add, [tr_], [tr_])
                ts("dve", r_m, r_r, -math.pi, None, ALU.is_lt, None, [tr_], [tr_])
                stt("dve", r_r, r_m, 2 * math.pi, r_r, ALU.mult, ALU.add, [tr_], [tr_])
                ts("dve", r_r, r_r, -3.14159, 3.14159, ALU.max, ALU.min, [tr_], [tr_])
                ro, tro = r_o[which], t_ro[which]
                act(ro, r_r, AF.Sin, [tr_], [tro])
                if which == 1:
                    ts("dve", ro, ro, ROPEC[0:64, 1:2], None, ALU.mult, None, [tro, t_arena], [tro])
                pr.dma("sp", csd[which, :, ch * RC:(ch + 1) * RC], ro, reads=[tro], writes=[t_cs])

        def norm_block(tb, xb, t_xb, final=False):
            stt_, t_st = ST.next()
            hb, t_hb = HB.next()
            act(hb[:], xb[:], AF.Square, [t_xb], [t_hb, t_st], accum_out=stt_[:, 0:1])
            act(stt_[:, 1:2], stt_[:, 0:1], AF.Ln, [t_st, t_const], [t_st], scale=1.0 / D, bias=EPS[:, 0:1])
            act(stt_[:, 2:3], stt_[:, 1:2], AF.Exp, [t_st], [t_st], scale=-0.5)
            if final:
                if not raw:
                    stt("dve", xb[:], xb[:], stt_[:, 2:3], NW[:], ALU.mult, ALU.mult, [t_xb, t_st, t_NW], [t_xb])
                return pr.dma("pool", out[tb * 128:(tb + 1) * 128, :], xb[:], reads=[t_xb])
            stt("dve", hb[:], xb[:], stt_[:, 2:3], NW[:], ALU.mult, ALU.mult, [t_xb, t_st, t_NW], [t_hb])
            for kc in range(8):
                pr.op("pe", lambda e, kc=kc: e.transpose(out=TR[:, kc * 128:(kc + 1) * 128], in_=hb[:, kc * 128:(kc + 1) * 128],
                                                         identity=IDENT[:]),
                      reads=[t_hb, t_const], writes=[t_TR])
            copy_any("act", HT[:, :, tb * 128:(tb + 1) * 128], TR[:].rearrange("p (k t) -> p k t", k=8), [t_TR], [t_HT[tb]])
            return None

        def load_nw(row):
            pr.dma("sp", NW[:], nw_in[row:row + 1, :].partition_broadcast(128), writes=[t_NW])

        def epilogue(num, den, src_reads, ti, wz_lhsT, wz_tok, row0, sink_col=None):
            pj, t_pj = PJ.next()
            for kc in range(8):
                mm(pj[0:64, :], wz_lhsT(kc), HT[:, kc, ti * 512:(ti + 1) * 512], kc == 0, kc == 7,
                   [wz_tok] + t_HT[ti * 4:ti * 4 + 4], [t_pj])
            ea, t_ea = EPA.next()
            eb, t_eb = EPB.next()
            rd, t_rd = RD.next()
            act(ea[:], pj[0:64, :], AF.Exp, [t_pj], [t_ea], scale=-1.0)
            ts("dve", ea[:], ea[:], 1.0, None, ALU.add, None, [t_ea], [t_ea])
            recip(ea[:], ea[:], [t_ea], [t_ea])
            tt("dve", ea[:], ea[:], pj[0:64, :], ALU.mult, [t_ea, t_pj], [t_ea])
            if sink_col is not None:
                ts("dve", rd[64:65, :], den, ES[64:65, sink_col:sink_col + 1], None, ALU.add, None, src_reads + [t_const], [t_rd])
                recip(rd[64:65, :], rd[64:65, :], [t_rd], [t_rd])
            else:
                recip(rd[64:65, :], den, src_reads, [t_rd])
            bc, t_bc = PJ.next()
            mm(bc[0:64, :], ONES[64:65, 0:64], rd[64:65, :], True, True, [t_rd, t_const], [t_bc])
            tt("dve", eb[:], num, ea[:], ALU.mult, src_reads + [t_ea], [t_eb])
            go, t_go = GOUT.next()
            tt("dve", go[:], eb[:], bc[0:64, :], ALU.mult, [t_eb, t_bc], [t_go])
            pr.dma("pool", go_own[ti][row0:row0 + 64, :], go[:], reads=[t_go], writes=[t_go_own[ti]])

        t_go_own = [Tok() for _ in range(NT)]
        t_go_full = [[Tok() for _ in range(NT)] for _ in range(2)]
        t_xs = [Tok() for _ in range(NB)]

        def ab_layer(li):
            i = li // 2
            AB16 = ARENA.bitcast(BF16)
            o = 0
            WH = AB16[:, o:o + 8 * 1664].rearrange("p (k n) -> p k n", k=8); o += 8 * 1664
            VA = AB16[:, o:o + 32 * 65].rearrange("p (m e) -> p m e", m=32); o += 32 * 65
            VS = AB16[:, o:o + 3 * 32 * 65].rearrange("p (c m e) -> p c m e", c=3, m=32); o += 3 * 32 * 65
            VST = [AB16[:, o + k * 260:o + (k + 1) * 260].rearrange("p (h e) -> p h e", h=4) for k in range(2)]; o += 520
            o32 = (o + 1) // 2
            U = ARENA[0:65, o32:o32 + T]; o32 += T
            BT = ARENA[:, o32:o32 + 3 * 256].rearrange("p (c n) -> p c n", c=3); o32 += 768
            assert o32 <= 16384, o32
            vst = Rot(VST)
            t_WH, t_VN, t_VS, t_U, t_BT, t_vd = Tok(), Tok(), Tok(), Tok(), Tok(), Tok()

            load_w(WH, abw_in[i], 1664, t_WH)
            load_w(WO, abwo_in[i], D, t_WO)
            pr.op("pool", lambda e: e.memset(VA[:, :, 64:65], 1.0), writes=[t_VN])
            pr.op("pool", lambda e: e.memset(VS[:, :, :, 64:65], 1.0), writes=[t_VS])
            for (v_, tv_) in vst.items:
                pr.op("pool", lambda e, v_=v_: e.memset(v_[:, :, 64:65], 1.0), writes=[tv_])

            for blk in range(NB):
                pj, t_pj = PJ.next()
                for kc in range(8):
                    mm(pj[:, 0:320], HT[:, kc, blk * 128:(blk + 1) * 128], WH[:, kc, 1344:1664], kc == 0, kc == 7,
                       [t_HT[blk], t_WH], [t_pj])
                copy_any("dve", VA[:, blk, 0:64], pj[:, 0:64], [t_pj], [t_VN])
                v_, tv_ = vst.next()
                copy_any("dve", v_[:, :, 0:64], pj[:, 64:320].rearrange("p (h e) -> p h e", h=4), [t_pj], [tv_])
                pr.dma("pool", vd[blk * 128:(blk + 1) * 128, :], v_, reads=[tv_], writes=[t_vd])

            if stage < 3:
                return
            def proj_fm(dst, col0, alt):
                for ti in range(NT):
                    pj, t_pj = PJ.next()
                    for kc in range(8):
                        mm(pj[0:64, :], WH[:, kc, col0:col0 + 64], HT[:, kc, ti * 512:(ti + 1) * 512], kc == 0, kc == 7,
                           [t_WH] + t_HT[ti * 4:ti * 4 + 4], [t_pj])
                    copy_any("act" if (ti + alt) % 2 == 0 else "dve", dst[0:64, ti * 512:(ti + 1) * 512], pj[0:64, :], [t_pj],
                             [t_QT if dst is QT else t_KT])

            def banded(d, vget, bt, consume):
                L = T // d
                nblk = L // 128
                tl = min(512, L)
                for r in range(d):
                    for t0 in range(0, L, tl):
                        ob, t_ob = OBK.next()
                        m0 = t0 // 128
                        nb_t = tl // 128
                        groups = []
                        mm_ = m0
                        while mm_ < m0 + nb_t:
                            if mm_ == 0 or mm_ == m0 + nb_t - 1:
                                groups.append([mm_]); mm_ += 1
                            else:
                                groups.append([mm_, mm_ + 1]); mm_ += 2
                        for g in groups:
                            sbk, t_s = SBK.next()
                            stmp, t_stmp = STMP.next()
                            pt, t_pt = PT.next()
                            ng = len(g)
                            first = (g[0] == 0)
                            for gi, m in enumerate(g):
                                qa = QT[0:64, r + d * 128 * m: r + d * 128 * m + d * 127 + 1: d]
                                for half in (0, 1):
                                    if m == 0 and half == 0:
                                        continue
                                    kb = m - 1 + half
                                    ka = KT[0:64, r + d * 128 * kb: r + d * 128 * kb + d * 127 + 1: d]
                                    mm(sbk[:, gi * 256 + half * 128: gi * 256 + half * 128 + 128], ka, qa, True, True,
                                       [t_QT, t_KT], [t_s])
                            c0 = 128 if first else 0
                            c1 = ng * 256
                            if first:
                                stt("dve", stmp[:, c0:c1], sbk[:, c0:c1], 0.125, bt[:, 128:256], ALU.mult, ALU.add,
                                    [t_s, t_BT], [t_stmp])
                            else:
                                stt("dve", stmp[:, c0:c1].rearrange("p (g n) -> p g n", g=ng),
                                    sbk[:, c0:c1].rearrange("p (g n) -> p g n", g=ng), 0.125,
                                    bt.unsqueeze(1).to_broadcast([128, ng, 256]), ALU.mult, ALU.add, [t_s, t_BT], [t_stmp])
                            act(pt[:, c0:c1], stmp[:, c0:c1], AF.Exp, [t_stmp], [t_pt])
                            for gi, m in enumerate(g):
                                oc = (m - m0) * 128
                                for half in (0, 1):
                                    if m == 0 and half == 0:
                                        continue
                                    kb = m - 1 + half
                                    mm(ob[0:65, oc:oc + 128], vget(r, kb), pt[:, gi * 256 + half * 128: gi * 256 + half * 128 + 128],
                                       (half == 0) or (m == 0), half == 1, [t_pt, t_VN, t_VS], [t_ob])
                        consume(ob, t_ob, r + d * t0, tl, d)

            def load_bt(hcs):
                for ci, hc in enumerate(hcs):
                    pr.dma("sp", BT[:, ci, 0:128], skew_ap(hc, True), reads=[t_gd], writes=[t_BT])
                    pr.dma("sp", BT[:, ci, 128:256], skew_ap(hc, False), reads=[t_gd], writes=[t_BT])

            proj_fm(KT, 512, 1)
            for j in range(4):
                proj_fm(QT, j * 128, 0)
                load_bt([j])

                def consume_a(ob, t_ob, tok0, ntok, stride, j=j):
                    ti = tok0 // 512
                    epilogue(ob[0:64, :], ob[64:65, :], [t_ob], ti, lambda kc, j=j: WH[:, kc, j * 128 + 64:j * 128 + 128], t_WH,
                             j * 64, sink_col=i * 4 + j)
                banded(1, lambda r, kb: VA[:, kb, :], BT[:, 0, :], consume_a)

            if stage < 4:
                return
            for j in range(4):
                cb = 576 + j * 192
                proj_fm(QT, cb, 0)
                proj_fm(KT, cb + 64, 1)
                load_bt([4 + j, 8 + j, 12 + j])
                for ci, d in enumerate(B_CFG):
                    src = bass.AP(tensor=vd.tensor, offset=j * 65,
                                  ap=[[d * 260, 128], [260, d], [128 * d * 260, 32 // d], [1, 65]])
                    pr.dma("sp", VS[:, ci, :, :].rearrange("p (r m) e -> p r m e", r=d), src, reads=[t_vd], writes=[t_VS])
                for ci, d in enumerate(B_CFG):
                    def consume_b(ob, t_ob, tok0, ntok, stride, ci=ci):
                        ua = U[:, tok0: tok0 + stride * (ntok - 1) + 1: stride]
                        if ci == 0:
                            copy_any("dve", ua, ob[0:65, 0:ntok], [t_ob], [t_U])
                        else:
                            tt("dve", ua, ua, ob[0:65, 0:ntok], ALU.add, [t_ob, t_U], [t_U])
                    vget = lambda r, kb, d=d, ci=ci: VS[:, ci, r * (32 // d) + kb, :]
                    banded(d, vget, BT[:, ci, :], consume_b)
                for ti in range(NT):
                    epilogue(U[0:64, ti * 512:(ti + 1) * 512], U[64:65, ti * 512:(ti + 1) * 512], [t_U], ti,
                             lambda kc, cb=cb: WH[:, kc, cb + 128:cb + 192], t_WH, 256 + j * 64)

        def c_layer(li):
            i = li // 2
            C16 = ARENA.bitcast(BF16)
            o = 0
            WC = C16[:, o:o + 8 * 1024].rearrange("p (k n) -> p k n", k=8); o += 8 * 1024
            WQB = C16[:, o:o + 2 * 1536].rearrange("p (k n) -> p k n", k=2); o += 2 * 1536
            WKVB = C16[:, o:o + 1536].rearrange("p (k n) -> p k n", k=1); o += 1536
            LAT = C16[:, o:o + 3 * T].rearrange("p (c t) -> p c t", c=3); o += 3 * T
            VM = C16[:, o:o + 32 * 65].rearrange("p (m e) -> p m e", m=32); o += 32 * 65
            CNB = C16[:, o:o + 384]; o += 384
            o32 = (o + 1) // 2
            CS = [ARENA[0:64, o32 + k * 512: o32 + (k + 1) * 512] for k in range(4)]; o32 += 2048
            CNW = ARENA[:, o32:o32 + 384]; o32 += 384
            assert o32 <= 16384, o32
            t_WC, t_WQB, t_WKVB, t_LAT, t_KPE, t_VM, t_CNB, t_CNW = (Tok() for _ in range(8))
            csr = Rot([(CS[0], CS[1]), (CS[2], CS[3])])

            load_w(WC, cw_in[i], 1024, t_WC)
            load_w(WQB, cqb_in[i], 1536, t_WQB)
            load_w(WKVB, ckvb_in[i], 1536, t_WKVB)
            load_w(WO, cwo_in[i], D, t_WO)
            pr.dma("sp", CNW, cn_in[0:1, i * 384:(i + 1) * 384].partition_broadcast(128), writes=[t_CNW])
            pr.op("pool", lambda e: e.memset(VM[:, :, 64:65], 1.0), writes=[t_VM])

            for tb in range(NB):
                pj, t_pj = PJ.next()
                for kc in range(8):
                    mm(pj[:, 0:384], HT[:, kc, tb * 128:(tb + 1) * 128], WC[:, kc, 0:384], kc == 0, kc == 7,
                       [t_HT[tb], t_WC], [t_pj])
                stt_, t_st = ST.next()
                stmp, t_stmp = STMP.next()
                act(stmp[:, 0:256], pj[:, 0:256], AF.Square, [t_pj], [t_stmp, t_st], accum_out=stt_[:, 0:1])
                act(stmp[:, 256:384], pj[:, 256:384], AF.Square, [t_pj], [t_stmp, t_st], accum_out=stt_[:, 1:2])
                act(stt_[:, 2:3], stt_[:, 0:1], AF.Ln, [t_st, t_const], [t_st], scale=1.0 / 256, bias=EPS[:, 0:1])
                act(stt_[:, 3:4], stt_[:, 1:2], AF.Ln, [t_st, t_const], [t_st], scale=1.0 / 128, bias=EPS[:, 0:1])
                act(stt_[:, 4:6], stt_[:, 2:4], AF.Exp, [t_st], [t_st], scale=-0.5)
                stt("dve", CNB[:, 0:256], pj[:, 0:256], stt_[:, 4:5], CNW[:, 0:256], ALU.mult, ALU.mult,
                    [t_pj, t_st, t_CNW], [t_CNB])
                stt("dve", CNB[:, 256:384], pj[:, 256:384], stt_[:, 5:6], CNW[:, 256:384], ALU.mult, ALU.mult,
                    [t_pj, t_st, t_CNW], [t_CNB])
                for c3 in range(3):
                    pr.op("pe", lambda e, c3=c3: e.transpose(out=TR[:, c3 * 128:(c3 + 1) * 128], in_=CNB[:, c3 * 128:(c3 + 1) * 128],
                                                             identity=IDENT[:]),
                          reads=[t_CNB, t_const], writes=[t_TR])
                copy_any("act", LAT[:, :, tb * 128:(tb + 1) * 128], TR[:, 0:384].rearrange("p (c t) -> p c t", c=3), [t_TR], [t_LAT])

            def rope_tile(ti, pj_main, t_main, pj_sw, t_sw, dst, t_dst):
                (cc, ss), t_cs_t = csr.next()
                pr.dma("sp", cc, csd[0, :, ti * 512:(ti + 1) * 512], reads=[t_cs], writes=[t_cs_t])
                pr.dma("sp", ss, csd[1, :, ti * 512:(ti + 1) * 512], reads=[t_cs], writes=[t_cs_t])
                ea, t_ea = EPA.next()
                eb, t_eb = EPB.next()
                tt("dve", ea[:], pj_main[0:64, :], cc, ALU.mult, [t_main, t_cs_t], [t_ea])
                tt("dve", eb[:], pj_sw[0:64, :], ss, ALU.mult, [t_sw, t_cs_t], [t_eb])
                tt("pool", dst[0:64, ti * 512:(ti + 1) * 512], ea[:], eb[:], ALU.add, [t_ea, t_eb], [t_dst])

            for ti in range(NT):
                pa, t_pa = PJ.next()
                pb, t_pb = PJ.next()
                for kc in range(8):
                    mm(pa[0:64, :], WC[:, kc, 384:448], HT[:, kc, ti * 512:(ti + 1) * 512], kc == 0, kc == 7,
                       [t_WC] + t_HT[ti * 4:ti * 4 + 4], [t_pa])
                for kc in range(8):
                    mm(pb[0:64, :], WC[:, kc, 448:512], HT[:, kc, ti * 512:(ti + 1) * 512], kc == 0, kc == 7,
                       [t_WC] + t_HT[ti * 4:ti * 4 + 4], [t_pb])
                rope_tile(ti, pa, t_pa, pb, t_pb, KT, t_KT)

            scale = (64 + 32) ** -0.5
            for h in range(8):
                for ti in range(NT):
                    pj, t_pj = PJ.next()
                    mm(pj[:, :], WKVB[:, 0, h * 128:(h + 1) * 128], LAT[:, 2, ti * 512:(ti + 1) * 512], True, True,
                       [t_WKVB, t_LAT], [t_pj])
                    copy_any("act", KT[64:128, ti * 512:(ti + 1) * 512], pj[64:128, :], [t_pj], [t_KT])
                for b0 in range(0, NB, 8):
                    pj, t_pj = PJ.next()
                    for bb in range(8):
                        mm(pj[:, bb * 64:(bb + 1) * 64], LAT[:, 2, (b0 + bb) * 128:(b0 + bb + 1) * 128],
                           WKVB[:, 0, 1024 + h * 64:1024 + (h + 1) * 64], True, True, [t_WKVB, t_LAT], [t_pj])
                    copy_any("dve", VM[:, b0:b0 + 8, 0:64], pj[:, :].rearrange("p (m e) -> p m e", m=8), [t_pj], [t_VM])
                for ti in range(NT):
                    pa, t_pa = PJ.next()
                    pb, t_pb = PJ.next()
                    for c2 in range(2):
                        mm(pa[:, :], WQB[:, c2, h * 128:(h + 1) * 128], LAT[:, c2, ti * 512:(ti + 1) * 512], c2 == 0, c2 == 1,
                           [t_WQB, t_LAT], [t_pa])
                    for c2 in range(2):
                        mm(pb[0:64, :], WQB[:, c2, 1024 + h * 64:1024 + (h + 1) * 64], LAT[:, c2, ti * 512:(ti + 1) * 512],
                           c2 == 0, c2 == 1, [t_WQB, t_LAT], [t_pb])
                    copy_any("act", QT[64:128, ti * 512:(ti + 1) * 512], pa[64:128, :], [t_pa], [t_QT])
                    rope_tile(ti, pa, t_pa, pb, t_pb, QT, t_QT)
                for j in range(NT):
                    ob, t_ob = OBK.next()
                    nkb = 4 * j + 4
                    for kb in range(nkb):
                        c0 = max(0, kb - 4 * j) * 128
                        sbk, t_s = SBK.next()
                        pt, t_pt = PT.next()
                        mm(sbk[:, c0:512], KT[:, kb * 128:(kb + 1) * 128], QT[:, j * 512 + c0:(j + 1) * 512], True, True,
                           [t_QT, t_KT], [t_s])
                        act(pt[:, c0:512], sbk[:, c0:512], AF.Exp, [t_s], [t_pt], scale=scale)
                        if kb >= 4 * j:
                            tt("pool", pt[:, c0:c0 + 128], pt[:, c0:c0 + 128], TRI[:], ALU.mult, [t_pt, t_const], [t_pt])
                        mm(ob[0:65, c0:512], VM[:, kb, :], pt[:, c0:512], kb == 0, kb == nkb - 1, [t_pt, t_VM], [t_ob])
                    epilogue(ob[0:64, :], ob[64:65, :], [t_ob], j, lambda kc, h=h: WC[:, kc, 512 + h * 64:512 + (h + 1) * 64], t_WC,
                             h * 64)

        def phase_o(li, last):
            gf = go_full[li % 2]
            tgf = t_go_full[li % 2]
            evs = []
            for tb in range(NB):
                gi, t_gi = GI.next()
                pr.dma("sp", gi[:], gf[tb // 4][:, (tb % 4) * 128:(tb % 4 + 1) * 128].rearrange("(kc p) t -> p kc t", p=128),
                       reads=[tgf[tb // 4]], writes=[t_gi])
                xb, t_xb = XB.next()
                if li == 0:
                    pr.dma("sp", xb[:], x_in[tb * 128:(tb + 1) * 128, :], writes=[t_xb])
                else:
                    pr.dma("sp", xb[:], xs[tb * 128:(tb + 1) * 128, :], reads=[t_xs[tb]], writes=[t_xb])
                for half in range(2):
                    pj, t_pj = PJ.next()
                    for kc in range(8):
                        mm(pj[:, :], gi[:, kc, :], WO[:, kc, half * 512:(half + 1) * 512], kc == 0, kc == 7, [t_gi, t_WO], [t_pj])
                    tt("dve", xb[:, half * 512:(half + 1) * 512], xb[:, half * 512:(half + 1) * 512], pj[:, :], ALU.add,
                       [t_xb, t_pj], [t_xb])
                if not last:
                    pr.dma("pool", xs[tb * 128:(tb + 1) * 128, :], xb[:], reads=[t_xb], writes=[t_xs[tb]])
                ev = norm_block(tb, xb, t_xb, final=last)
                if ev is not None:
                    evs.append(ev)
            return evs

        pr.barrier()
        load_nw(0)
        for tb in range(NB if stage >= 1 else 0):
            xb, t_xb = XB.next()
            pr.dma("sp", xb[:], x_in[tb * 128:(tb + 1) * 128, :], writes=[t_xb])
            norm_block(tb, xb, t_xb)
        final_evs = []
        for li in range(nlayers if stage >= 2 else 0):
            pr.barrier()
            if li % 2 == 0:
                ab_layer(li)
            else:
                c_layer(li)
            last = (li == nlayers - 1)
            if stage < 5:
                break
            gf = go_full[li % 2]
            for ti in range(NT):
                pr.collective(lambda e, gf=gf, ti=ti: e.collective_compute("AllGather", ALU.bypass, replica_groups=PAIRS,
                                                                           ins=[go_own[ti].ap().opt()], outs=[gf[ti].ap().opt()]),
                              reads=[t_go_own[ti]], writes=[t_go_full[li % 2][ti]])
            if stage < 6:
                break
            load_nw(4 if (last and nlayers == 4) else li + 1)
            final_evs = phase_o(li, last)
        pr.wait_all("pool", final_evs)
        pr.barrier()
        pr.emit_all()
        print("instructions:", pr.n_inst, {e: pr.cnt[e] for e in pr.ENG})
    return nc


def _t5_bucket(dist):
    dist = np.asarray(dist, dtype=np.int64)
    d = np.maximum(dist, 1).astype(np.float32)
    large = 16 + (np.log(d / np.float32(16)) / np.float32(math.log(2048 / 16)) * np.float32(16)).astype(np.int32)
    large = np.minimum(large, 31)
    return np.where(dist < 16, dist, large)


def _consts():
    oneh = np.zeros((33, 3, 383), np.float32)
    for ci, d in enumerate(B_CFG):
        for n in range(383):
            dist = n - 127
            if 0 <= dist <= 128:
                oneh[_t5_bucket(dist * d), ci, n] = 1.0
            else:
                oneh[32, ci, n] = NEG
    kk = np.arange(128)[:, None]
    qq = np.arange(128)[None, :]
    tri = (qq >= kk).astype(np.float32).astype(ml_dtypes.bfloat16)
    ident = np.eye(128, dtype=np.float32).astype(ml_dtypes.bfloat16)
    inv_freq = (10000.0 ** (-np.arange(0, 32, 2, dtype=np.float32) / np.float32(32))).astype(np.float32)
    ropec = np.zeros((128, 2), np.float32)
    ropec[0:16, 0] = inv_freq
    ropec[32:48, 0] = inv_freq
    ropec[0:16, 1] = -1.0
    ropec[32:48, 1] = 1.0
    return oneh.reshape(33, 3 * 383), tri, ident, ropec


def _core_inputs(c, inp, consts):
    b, hh = c // 2, c % 2
    oneh, tri, ident, ropec = consts
    f = np.float32
    m = {}
    m["x"] = np.ascontiguousarray(inp["x"][b], dtype=f)
    m["pos"] = np.ascontiguousarray(inp["positions"][b][None, :], dtype=np.int32)
    m["nw"] = np.ascontiguousarray(np.concatenate([inp["norm_w"], inp["final_norm"][None, :]], 0), dtype=f)
    rb = inp["rel_bias"]
    relb = np.ones((33, 8), f)
    relb[:32, 0:4] = rb[:, hh * 4:hh * 4 + 4]
    relb[:32, 4:8] = rb[:, 8 + hh * 4:8 + hh * 4 + 4]
    m["relb"] = relb
    m["oneh"] = oneh
    m["sinks"] = np.ascontiguousarray(inp["ab_sinks"][:, hh * 4:hh * 4 + 4].reshape(1, 8), dtype=f)
    m["ident"] = ident
    m["tri"] = tri
    m["ropec"] = ropec
    m["cn"] = np.ascontiguousarray(np.concatenate(
        [np.concatenate([inp["c_q_norm"][i], inp["c_kv_norm"][i]]) for i in range(2)])[None, :], dtype=f)
    for i in range(2):
        w = inp["ab_w_in"][i]
        qa, ka, va, qb, kb, vb, z = np.split(w, np.cumsum([512, 128, 128, 512, 512, 512]), axis=1)
        cols = []
        for j in range(4):
            h = hh * 4 + j
            cols += [qa[:, h * 64:(h + 1) * 64], z[:, h * 64:(h + 1) * 64]]
        cols.append(ka[:, hh * 64:(hh + 1) * 64])
        for j in range(4):
            h = hh * 4 + j
            cols += [qb[:, h * 64:(h + 1) * 64], kb[:, h * 64:(h + 1) * 64], z[:, 512 + h * 64:512 + (h + 1) * 64]]
        cols.append(va[:, hh * 64:(hh + 1) * 64])
        cols.append(vb[:, hh * 256:(hh + 1) * 256])
        m[f"abw{i}"] = np.ascontiguousarray(np.concatenate(cols, 1), dtype=f)
        assert m[f"abw{i}"].shape == (1024, 1664)
        wo = inp["ab_w_out"][i]
        m[f"abwo{i}"] = np.ascontiguousarray(np.concatenate([wo[0:256], wo[512:768], wo[256:512], wo[768:1024]], 0), dtype=f)
        w = inp["c_w_in"][i]
        kpe = w[:, 384:416]
        z16 = np.zeros((1024, 16), f)
        kp_pad = np.concatenate([kpe[:, 0:16], z16, kpe[:, 16:32], z16], 1)
        kp_sw = np.concatenate([kpe[:, 16:32], z16, kpe[:, 0:16], z16], 1)
        m[f"cw{i}"] = np.ascontiguousarray(np.concatenate([w[:, 0:384], kp_pad, kp_sw, w[:, 416 + hh * 512:416 + (hh + 1) * 512]], 1), dtype=f)
        wq = inp["c_w_qb"][i].reshape(256, 16, 96)
        z16q = np.zeros((256, 16), f)
        pads, sws = [], []
        for j in range(8):
            h = hh * 8 + j
            nope, pe = wq[:, h, 0:64], wq[:, h, 64:96]
            pads.append(np.concatenate([pe[:, 0:16], z16q, pe[:, 16:32], z16q, nope], 1))
            sws.append(np.concatenate([pe[:, 16:32], z16q, pe[:, 0:16], z16q], 1))
        m[f"cqb{i}"] = np.ascontiguousarray(np.concatenate(pads + sws, 1), dtype=f)
        wkv = inp["c_w_kvb"][i].reshape(128, 16, 128)
        z64 = np.zeros((128, 64), f)
        ks, vs = [], []
        for j in range(8):
            h = hh * 8 + j
            ks.append(np.concatenate([z64, wkv[:, h, 0:64]], 1))
            vs.append(wkv[:, h, 64:128])
        m[f"ckvb{i}"] = np.ascontiguousarray(np.concatenate(ks + vs, 1), dtype=f)
        m[f"cwo{i}"] = np.ascontiguousarray(inp["c_w_out"][i], dtype=f)
    return m


_NC_CACHE = {}


def kernel(x, positions, norm_w, rel_bias, ab_w_in, ab_sinks, ab_w_out, c_w_in, c_q_norm, c_w_qb,
           c_kv_norm, c_w_kvb, c_w_out, final_norm, _nlayers=4, _raw=False, _stage=99):
    inp = dict(x=x, positions=positions, norm_w=norm_w, rel_bias=rel_bias, ab_w_in=ab_w_in, ab_sinks=ab_sinks,
               ab_w_out=ab_w_out, c_w_in=c_w_in, c_q_norm=c_q_norm, c_w_qb=c_w_qb, c_kv_norm=c_kv_norm,
               c_w_kvb=c_w_kvb, c_w_out=c_w_out, final_norm=final_norm)
    inp = {k: np.asarray(v) for k, v in inp.items()}
    consts = _consts()
    in_maps = [_core_inputs(c, inp, consts) for c in range(8)]
    if (_nlayers, _raw, _stage) not in _NC_CACHE:
        _NC_CACHE[(_nlayers, _raw, _stage)] = build_program(_nlayers, _raw, _stage)
    nc = _NC_CACHE[(_nlayers, _raw, _stage)]
    res = run_bass_kernel_spmd(nc, in_maps, core_ids=list(range(8)))
    outs = []
    for b in range(4):
        o0 = res.results[2 * b]["out"]
        o1 = res.results[2 * b + 1]["out"]
        outs.append(np.concatenate([o0[:2048], o1[2048:]], 0))
    return np.stack(outs, 0).astype(np.float32)
```

```python
import contextlib
import math
import numpy as np
import ml_dtypes
import concourse.bass as bass
import concourse.mybir as mybir
from concourse.bass_utils import run_bass_kernel_spmd

F32 = mybir.dt.float32
BF16 = mybir.dt.bfloat16
I32 = mybir.dt.int32
AF = mybir.ActivationFunctionType
ALU = mybir.AluOpType

T = 4096
D = 1024
NB = 32
NT = 8
NEG = -30000.0
PAIRS = [[0, 1], [2, 3], [4, 5], [6, 7]]
B_CFG = (1, 4, 16)
LAG = 2


class Tok:
    __slots__ = ("w", "r")

    def __init__(self):
        self.w = None
        self.r = []


class Prog:
    ENG = ("pe", "act", "dve", "pool", "sp")

    def __init__(self, nc, stack, n_dma_sems=20):
        self.nc = nc
        self.q = {e: [] for e in self.ENG}
        self.cnt = {e: 0 for e in self.ENG}
        self.waited = {e: {} for e in self.ENG}
        self.sem = {e: stack.enter_context(nc.semaphore("s_" + e)) for e in self.ENG}
        self.semid = {id(self.sem[e]): e for e in self.ENG}
        self.dsem, self.dval, self.drr = {}, {}, {}
        for qe in ("sp", "pool", "act"):
            self.dsem[qe] = [stack.enter_context(nc.semaphore(f"d_{qe}_{i}")) for i in range(n_dma_sems)]
            self.dval[qe] = [0] * n_dma_sems
            self.drr[qe] = 0
        self.cc_sem = stack.enter_context(nc.semaphore("cc"))
        self.cc_val = 0
        self.n_inst = 0

    def _collect(self, e, reads, writes, extra=()):
        deps = {}

        def add(ev):
            if ev is None:
                return
            s, v = ev
            k = id(s)
            if k not in deps or deps[k][1] < v:
                deps[k] = (s, v)

        for t in reads:
            add(t.w)
        for t in writes:
            add(t.w)
            for ev in t.r:
                add(ev)
        for ev in extra:
            add(ev)
        out = []
        for k, (s, v) in deps.items():
            if self.semid.get(k) == e and e == "pe":
                continue
            if self.waited[e].get(k, 0) >= v:
                continue
            self.waited[e][k] = v
            out.append((s, v))
        return out

    def _mark(self, ev, reads, writes):
        for t in reads:
            t.r.append(ev)
            if len(t.r) > 64:
                t.r = _compress(t.r)
        for t in writes:
            t.w = ev
            t.r = []

    def op(self, e, fn, reads=(), writes=(), extra=()):
        waits = self._collect(e, reads, writes, extra)
        self.cnt[e] += 1
        sem = self.sem[e]
        ev = (sem, self.cnt[e])

        def emit(eng, fn=fn, waits=waits, sem=sem):
            for (s, v) in waits:
                eng.wait_ge(s, v)
            fn(eng).then_inc(sem, 1)

        self.q[e].append(emit)
        self._mark(ev, reads, writes)
        self.n_inst += 1
        return ev

    def dma(self, qe, out, in_, reads=(), writes=(), extra=()):
        k = self.drr[qe]
        self.drr[qe] = (k + 1) % len(self.dsem[qe])
        s = self.dsem[qe][k]
        prev = self.dval[qe][k]
        self.dval[qe][k] = prev + 16
        ev = (s, prev + 16)
        ex = list(extra)
        if prev > 0:
            ex.append((s, prev))
        waits = self._collect(qe, reads, writes, ex)

        def emit(eng, waits=waits, s=s, out=out, in_=in_):
            for (ws, v) in waits:
                eng.wait_ge(ws, v)
            eng.dma_start(out=out, in_=in_).then_inc(s, 16)

        self.q[qe].append(emit)
        self._mark(ev, reads, writes)
        self.n_inst += 1
        return ev

    def collective(self, fn, reads, writes):
        waits = self._collect("pool", reads, writes)
        self.cc_val += 1
        ev = (self.cc_sem, self.cc_val)

        def emit(eng, waits=waits, fn=fn):
            for (s, v) in waits:
                eng.wait_ge(s, v)
            fn(eng).then_inc(self.cc_sem, 1)

        self.q["pool"].append(emit)
        self._mark(ev, reads, writes)
        return ev

    def wait_all(self, e, evs):
        waits = self._collect(e, (), (), evs)

        def emit(eng, waits=waits):
            for (s, v) in waits:
                eng.wait_ge(s, v)

        self.q[e].append(emit)

    def all_events(self):
        evs = [(self.sem[e], self.cnt[e]) for e in self.ENG if self.cnt[e] > 0]
        for qe in self.dsem:
            for s, v in zip(self.dsem[qe], self.dval[qe]):
                if v > 0:
                    evs.append((s, v))
        if self.cc_val > 0:
            evs.append((self.cc_sem, self.cc_val))
        return evs

    def barrier(self):
        evs = self.all_events()
        for e in self.ENG:
            self.wait_all(e, evs)

    def emit_all(self):
        nc = self.nc
        with nc.Block() as block:
            @block.tensor
            def _(eng):
                for f in self.q["pe"]:
                    f(eng)

            @block.scalar
            def _(eng):
                for f in self.q["act"]:
                    f(eng)

            @block.vector
            def _(eng):
                for f in self.q["dve"]:
                    f(eng)

            @block.gpsimd
            def _(eng):
                for f in self.q["pool"]:
                    f(eng)

            @block.sync
            def _(eng):
                for f in self.q["sp"]:
                    f(eng)


def _compress(evs):
    best = {}
    for s, v in evs:
        k = id(s)
        if k not in best or best[k][1] < v:
            best[k] = (s, v)
    return list(best.values())


class Rot:
    def __init__(self, items):
        self.items = [(it, Tok()) for it in items]
        self.i = 0

    def next(self):
        it = self.items[self.i]
        self.i = (self.i + 1) % len(self.items)
        return it


def build_program(nlayers=4, raw=False, stage=99):
    nc = bass.Bass("TRN2", target_bir_lowering=False)

    def din(name, shape, dt):
        return nc.dram_tensor(name, list(shape), dt, kind="ExternalInput").ap()

    x_in = din("x", [T, D], F32)
    pos_in = din("pos", [1, T], I32)
    nw_in = din("nw", [5, D], F32)
    relb_in = din("relb", [33, 8], F32)
    oneh_in = din("oneh", [33, 3 * 383], F32)
    sinks_in = din("sinks", [1, 8], F32)
    ident_in = din("ident", [128, 128], BF16)
    tri_in = din("tri", [128, 128], BF16)
    ropec_in = din("ropec", [128, 2], F32)
    cn_in = din("cn", [1, 2 * 384], F32)
    abw_in = [din(f"abw{i}", [D, 1664], F32) for i in range(2)]
    abwo_in = [din(f"abwo{i}", [D, D], F32) for i in range(2)]
    cw_in = [din(f"cw{i}", [D, 1024], F32) for i in range(2)]
    cqb_in = [din(f"cqb{i}", [256, 1536], F32) for i in range(2)]
    ckvb_in = [din(f"ckvb{i}", [128, 1536], F32) for i in range(2)]
    cwo_in = [din(f"cwo{i}", [D, D], F32) for i in range(2)]
    out = nc.dram_tensor("out", [T, D], F32, kind="ExternalOutput").ap()

    xs = nc.dram_tensor("xs", [T, D], F32).ap()
    go_own = [nc.dram_tensor(f"go_own{t}", [512, 512], BF16) for t in range(NT)]
    go_full = [[nc.dram_tensor(f"go_full{i}_{t}", [1024, 512], BF16) for t in range(NT)] for i in range(2)]
    gd = nc.dram_tensor("gd", [16, 128 * 383], F32)
    vd = nc.dram_tensor("vd", [T, 4 * 65], BF16).ap()
    csd = nc.dram_tensor("csd", [2, 64, T], F32).ap()

    with contextlib.ExitStack() as st:
        def sb(name, shape, dt):
            return st.enter_context(nc.sbuf_tensor(name, list(shape), dt))

        def psum(name, shape, dt):
            return st.enter_context(nc.psum_tensor(name, list(shape), dt))

        pr = Prog(nc, st)

        HT = sb("HT", [128, 8, T], BF16)
        WO = sb("WO", [128, 8, D], BF16)
        QT = sb("QT", [128, T], BF16)
        KT = sb("KT", [128, T], BF16)
        XB = Rot([sb(f"XB{i}", [128, D], F32) for i in range(2)])
        HB = Rot([sb(f"HB{i}", [128, D], BF16) for i in range(2)])
        PT = Rot([sb(f"PT{i}", [128, 512], BF16) for i in range(4)])
        STMP = Rot([sb(f"STMP{i}", [128, 512], F32) for i in range(2)])
        GI = Rot([sb(f"GI{i}", [128, 8, 128], BF16) for i in range(2)])
        GOUT = Rot([sb(f"GOUT{i}", [64, 512], BF16) for i in range(2)])
        WSTG = Rot([sb(f"WSTG{i}", [128, 8, 64], F32) for i in range(2)])
        NW = sb("NW", [128, D], F32)
        ST = Rot([sb(f"ST{i}", [128, 8], F32) for i in range(4)])
        IDENT = sb("IDENT", [128, 128], BF16)
        TRI = sb("TRI", [128, 128], BF16)
        ONES = sb("ONES", [128, 64], F32)
        EPS = sb("EPS", [128, 1], F32)
        ES = sb("ES", [128, 8], F32)
        EPA = Rot([sb(f"EPA{i}", [64, 512], F32) for i in range(2)])
        EPB = Rot([sb(f"EPB{i}", [64, 512], F32) for i in range(2)])
        RD = Rot([sb(f"RD{i}", [65, 512], F32) for i in range(1)])
        ARENA = sb("ARENA", [128, 16384], F32)

        t_HT = [Tok() for _ in range(NB)]
        t_WO, t_QT, t_KT, t_NW = Tok(), Tok(), Tok(), Tok()
        t_const = Tok()
        t_arena = Tok()

        SBK = Rot([psum(f"S{i}", [128, 512], F32) for i in range(2)])
        OBK = Rot([psum(f"O{i}", [128, 512], F32) for i in range(2)])
        PJ = Rot([psum(f"PJ{i}", [128, 512], F32) for i in range(3)])
        TR = psum("TR", [128, 1024], BF16)
        t_TR = Tok()

        def mm(out_, lhsT, rhs, start, stop, reads, writes):
            return pr.op("pe", lambda e: e.matmul(out_, lhsT=lhsT, rhs=rhs, start=start, stop=stop,
                                                  skip_group_check=True), reads=reads, writes=writes)

        def act(out_, in_, func, reads, writes, **kw):
            return pr.op("act", lambda e: e.activation(out=out_, in_=in_, func=func, **kw), reads=reads, writes=writes)

        def copy_any(eng, out_, in_, reads, writes):
            if eng == "act":
                return pr.op("act", lambda e: e.copy(out=out_, in_=in_), reads=reads, writes=writes)
            return pr.op(eng, lambda e: e.tensor_copy(out=out_, in_=in_), reads=reads, writes=writes)

        def tt(eng, out_, in0, in1, op, reads, writes):
            return pr.op(eng, lambda e: e.tensor_tensor(out=out_, in0=in0, in1=in1, op=op), reads=reads, writes=writes)

        def ts(eng, out_, in0, s1, s2, op0, op1, reads, writes):
            if s2 is None:
                return pr.op(eng, lambda e: e.tensor_scalar(out=out_, in0=in0, scalar1=s1, scalar2=None, op0=op0),
                             reads=reads, writes=writes)
            return pr.op(eng, lambda e: e.tensor_scalar(out=out_, in0=in0, scalar1=s1, scalar2=s2, op0=op0, op1=op1),
                         reads=reads, writes=writes)

        def stt(eng, out_, in0, scalar, in1, op0, op1, reads, writes):
            return pr.op(eng, lambda e: e.scalar_tensor_tensor(out=out_, in0=in0, scalar=scalar, in1=in1, op0=op0, op1=op1),
                         reads=reads, writes=writes)

        def recip(out_, in_, reads, writes):
            return pr.op("dve", lambda e: e.reciprocal(out=out_, in_=in_), reads=reads, writes=writes)

        def load_w(dst, src, ncols, t_dst):
            kch = dst.shape[1]
            c0 = 0
            while c0 < ncols:
                cw = min(64, ncols - c0)
                stg, t_stg = WSTG.next()
                pr.dma("sp", stg[:, 0:kch, 0:cw], src[:, c0:c0 + cw].rearrange("(kc p) n -> p kc n", p=128),
                       writes=[t_stg])
                copy_any("pool", dst[:, :, c0:c0 + cw], stg[:, 0:kch, 0:cw], [t_stg], [t_dst])
                c0 += cw

        pr.dma("sp", IDENT[:], ident_in[:, :], writes=[t_const])
        pr.dma("sp", TRI[:], tri_in[:, :], writes=[t_const])
        pr.op("pool", lambda e: e.memset(ONES[:], 1.0), writes=[t_const])
        pr.op("pool", lambda e: e.memset(EPS[:], 1e-6), writes=[t_const])
        pr.dma("sp", ES[:], sinks_in[0:1, :].partition_broadcast(128), writes=[t_const])
        act(ES[:], ES[:], AF.Exp, [t_const], [t_const])

        A32 = ARENA
        RELB = A32[0:33, 0:8]
        ONEH = A32[0:33, 8:8 + 3 * 383]
        LB = A32[0:33, 1200:1328]
        GSB = [A32[:, 1400:1783], A32[:, 1800:2183]]
        t_gsb = [Tok(), Tok()]
        t_gd = Tok()
        t_lb = Tok()
        pr.dma("sp", RELB, relb_in[:, :], writes=[t_arena])
        pr.dma("sp", ONEH, oneh_in[:, :], writes=[t_arena])
        for hc in range(16):
            col = hc if hc < 4 else 4 + (hc - 4) % 4
            cfg = 0 if hc < 8 else (1 if hc < 12 else 2)
            copy_any("dve", LB, RELB[:, col:col + 1].to_broadcast([33, 128]), [t_arena], [t_lb])
            pj, t_pj = PJ.next()
            mm(pj[:, 0:383], LB, ONEH[:, cfg * 383:(cfg + 1) * 383], True, True, [t_lb, t_arena], [t_pj])
            copy_any("dve", GSB[hc % 2], pj[:, 0:383], [t_pj], [t_gsb[hc % 2]])
            pr.dma("sp", gd[hc:hc + 1, :].rearrange("o (p n) -> (o p) n", p=128), GSB[hc % 2], reads=[t_gsb[hc % 2]], writes=[t_gd])

        def skew_ap(hc, prev):
            return bass.AP(tensor=gd.ap().tensor, offset=hc * 128 * 383 + (255 if prev else 127), ap=[[382, 128], [1, 128]])

        ROPEC = A32[:, 2200:2202]
        pr.dma("sp", ROPEC, ropec_in[:, :], writes=[t_arena])
        t_cs = Tok()
        RC = 1024
        r_pi = A32[0:64, 4096:4096 + RC].bitcast(I32)
        r_a = A32[0:64, 5120:5120 + RC]
        r_b = A32[0:64, 6144:6144 + RC]
        r_ti = A32[0:64, 7168:7168 + RC].bitcast(I32)
        r_tf = A32[0:64, 8192:8192 + RC]
        r_r = A32[0:64, 9216:9216 + RC]
        r_m = A32[0:64, 10240:10240 + RC]
        r_o = [A32[0:64, 11264:11264 + RC], A32[0:64, 12288:12288 + RC]]
        tr_ = Tok()
        t_ro = [Tok(), Tok()]
        C1 = 6.28125
        C2 = 2.0 * math.pi - 6.28125
        for ch in range(T // RC):
            pr.dma("sp", r_pi, pos_in[0:1, ch * RC:(ch + 1) * RC].partition_broadcast(64), writes=[tr_])
            copy_any("dve", r_a, r_pi, [tr_], [tr_])
            ts("dve", r_a, r_a, ROPEC[0:64, 0:1], None, ALU.mult, None, [tr_, t_arena], [tr_])
            for which in (0, 1):
                shift = math.pi / 2 if which == 0 else 0.0
                ts("dve", r_b, r_a, shift, 1.0 / (2 * math.pi), ALU.add, ALU.mult, [tr_], [tr_])
                copy_any("dve", r_ti, r_b, [tr_], [tr_])
                copy_any("dve", r_tf, r_ti, [tr_], [tr_])
                ts("dve", r_b, r_a, shift, None, ALU.add, None, [tr_], [tr_])
                stt("dve", r_r, r_tf, -C1, r_b, ALU.mult, ALU.add, [tr_], [tr_])
                stt("dve", r_r, r_tf, -C2, r_r, ALU.mult, ALU.add, [tr_], [tr_])
                ts("dve", r_m, r_r, math.pi, None, ALU.is_gt, None, [tr_], [tr_])
                stt("dve", r_r, r_m, -2 * math.pi, r_r, ALU.mult, ALU.add, [tr_], [tr_])
                ts("dve", r_m, r_r, -math.pi, None, ALU.is_lt, None, [tr_], [tr_])
                stt("dve", r_r, r_m, 2 * math.pi, r_r, ALU.mult, ALU.add, [tr_], [tr_])
                ts("dve", r_r, r_r, -3.14159, 3.14159, ALU.max, ALU.min, [tr_], [tr_])
                ro, tro = r_o[which], t_ro[which]
                act(ro, r_r, AF.Sin, [tr_], [tro])
                if which == 1:
                    ts("dve", ro, ro, ROPEC[0:64, 1:2], None, ALU.mult, None, [tro, t_arena], [tro])
                pr.dma("sp", csd[which, :, ch * RC:(ch + 1) * RC], ro, reads=[tro], writes=[t_cs])

        def norm_block(tb, xb, t_xb, final=False):
            stt_, t_st = ST.next()
            hb, t_hb = HB.next()
            act(hb[:], xb[:], AF.Square, [t_xb], [t_hb, t_st], accum_out=stt_[:, 0:1])
            act(stt_[:, 1:2], stt_[:, 0:1], AF.Ln, [t_st, t_const], [t_st], scale=1.0 / D, bias=EPS[:, 0:1])
            act(stt_[:, 2:3], stt_[:, 1:2], AF.Exp, [t_st], [t_st], scale=-0.5)
            if final:
                if not raw:
                    stt("dve", xb[:], xb[:], stt_[:, 2:3], NW[:], ALU.mult, ALU.mult, [t_xb, t_st, t_NW], [t_xb])
                return pr.dma("pool", out[tb * 128:(tb + 1) * 128, :], xb[:], reads=[t_xb])
            stt("dve", hb[:], xb[:], stt_[:, 2:3], NW[:], ALU.mult, ALU.mult, [t_xb, t_st, t_NW], [t_hb])
            for kc in range(8):
                pr.op("pe", lambda e, kc=kc: e.transpose(out=TR[:, kc * 128:(kc + 1) * 128], in_=hb[:, kc * 128:(kc + 1) * 128],
                                                         identity=IDENT[:]),
                      reads=[t_hb, t_const], writes=[t_TR])
            copy_any("act", HT[:, :, tb * 128:(tb + 1) * 128], TR[:].rearrange("p (k t) -> p k t", k=8), [t_TR], [t_HT[tb]])
            return None

        def load_nw(row):
            pr.dma("sp", NW[:], nw_in[row:row + 1, :].partition_broadcast(128), writes=[t_NW])

        def epilogue(num, den, src_reads, ti, wz_lhsT, wz_tok, row0, sink_col=None):
            pj, t_pj = PJ.next()
            for kc in range(8):
                mm(pj[0:64, :], wz_lhsT(kc), HT[:, kc, ti * 512:(ti + 1) * 512], kc == 0, kc == 7,
                   [wz_tok] + t_HT[ti * 4:ti * 4 + 4], [t_pj])
            ea, t_ea = EPA.next()
            eb, t_eb = EPB.next()
            rd, t_rd = RD.next()
            act(ea[:], pj[0:64, :], AF.Exp, [t_pj], [t_ea], scale=-1.0)
            ts("dve", ea[:], ea[:], 1.0, None, ALU.add, None, [t_ea], [t_ea])
            recip(ea[:], ea[:], [t_ea], [t_ea])
            tt("dve", ea[:], ea[:], pj[0:64, :], ALU.mult, [t_ea, t_pj], [t_ea])
            if sink_col is not None:
                ts("dve", rd[64:65, :], den, ES[64:65, sink_col:sink_col + 1], None, ALU.add, None, src_reads + [t_const], [t_rd])
                recip(rd[64:65, :], rd[64:65, :], [t_rd], [t_rd])
            else:
                recip(rd[64:65, :], den, src_reads, [t_rd])
            bc, t_bc = PJ.next()
            mm(bc[0:64, :], ONES[64:65, 0:64], rd[64:65, :], True, True, [t_rd, t_const], [t_bc])
            tt("dve", eb[:], num, ea[:], ALU.mult, src_reads + [t_ea], [t_eb])
            go, t_go = GOUT.next()
            tt("dve", go[:], eb[:], bc[0:64, :], ALU.mult, [t_eb, t_bc], [t_go])
            pr.dma("pool", go_own[ti][row0:row0 + 64, :], go[:], reads=[t_go], writes=[t_go_own[ti]])

        t_go_own = [Tok() for _ in range(NT)]
        t_go_full = [[Tok() for _ in range(NT)] for _ in range(2)]
        t_xs = [Tok() for _ in range(NB)]

        def ab_layer(li):
            i = li // 2
            AB16 = ARENA.bitcast(BF16)
            o = 0
            WH = AB16[:, o:o + 8 * 1664].rearrange("p (k n) -> p k n", k=8); o += 8 * 1664
            VA = AB16[:, o:o + 32 * 65].rearrange("p (m e) -> p m e", m=32); o += 32 * 65
            VS = AB16[:, o:o + 3 * 32 * 65].rearrange("p (c m e) -> p c m e", c=3, m=32); o += 3 * 32 * 65
            VST = [AB16[:, o + k * 260:o + (k + 1) * 260].rearrange("p (h e) -> p h e", h=4) for k in range(2)]; o += 520
            o32 = (o + 1) // 2
            U = ARENA[0:65, o32:o32 + T]; o32 += T
            BT = ARENA[:, o32:o32 + 3 * 256].rearrange("p (c n) -> p c n", c=3); o32 += 768
            assert o32 <= 16384, o32
            vst = Rot(VST)
            t_WH, t_VN, t_VS, t_U, t_BT, t_vd = Tok(), Tok(), Tok(), Tok(), Tok(), Tok()

            load_w(WH, abw_in[i], 1664, t_WH)
            load_w(WO, abwo_in[i], D, t_WO)
            pr.op("pool", lambda e: e.memset(VA[:, :, 64:65], 1.0), writes=[t_VN])
            pr.op("pool", lambda e: e.memset(VS[:, :, :, 64:65], 1.0), writes=[t_VS])
            for (v_, tv_) in vst.items:
                pr.op("pool", lambda e, v_=v_: e.memset(v_[:, :, 64:65], 1.0), writes=[tv_])

            for blk in range(NB):
                pj, t_pj = PJ.next()
                for kc in range(8):
                    mm(pj[:, 0:320], HT[:, kc, blk * 128:(blk + 1) * 128], WH[:, kc, 1344:1664], kc == 0, kc == 7,
                       [t_HT[blk], t_WH], [t_pj])
                copy_any("dve", VA[:, blk, 0:64], pj[:, 0:64], [t_pj], [t_VN])
                v_, tv_ = vst.next()
                copy_any("dve", v_[:, :, 0:64], pj[:, 64:320].rearrange("p (h e) -> p h e", h=4), [t_pj], [tv_])
                pr.dma("pool", vd[blk * 128:(blk + 1) * 128, :], v_, reads=[tv_], writes=[t_vd])

            if stage < 3:
                return
            def proj_fm(dst, col0, alt):
                for ti in range(NT):
                    pj, t_pj = PJ.next()
                    for kc in range(8):
                        mm(pj[0:64, :], WH[:, kc, col0:col0 + 64], HT[:, kc, ti * 512:(ti + 1) * 512], kc == 0, kc == 7,
                           [t_WH] + t_HT[ti * 4:ti * 4 + 4], [t_pj])
                    copy_any("act" if (ti + alt) % 2 == 0 else "dve", dst[0:64, ti * 512:(ti + 1) * 512], pj[0:64, :], [t_pj],
                             [t_QT if dst is QT else t_KT])

            def banded(d, vget, bt, consume):
                L = T // d
                tl = min(512, L)
                steps = []
                for r in range(d):
                    for t0 in range(0, L, tl):
                        m0 = t0 // 128
                        nb_t = tl // 128
                        groups = []
                        mm_ = m0
                        while mm_ < m0 + nb_t:
                            if mm_ == 0 or mm_ == m0 + nb_t - 1:
                                groups.append([mm_]); mm_ += 1
                            else:
                                groups.append([mm_, mm_ + 1]); mm_ += 2
                        for gx, g in enumerate(groups):
                            steps.append(dict(r=r, t0=t0, m0=m0, g=g, firstg=(gx == 0), lastg=(gx == len(groups) - 1)))

                def front(stp):
                    r, g = stp["r"], stp["g"]
                    sbk, t_s = SBK.next()
                    stmp, t_stmp = STMP.next()
                    pt, t_pt = PT.next()
                    stp["pt"] = (pt, t_pt)
                    ng = len(g)
                    first = (g[0] == 0)
                    for gi, m in enumerate(g):
                        qa = QT[0:64, r + d * 128 * m: r + d * 128 * m + d * 127 + 1: d]
                        for half in (0, 1):
                            if m == 0 and half == 0:
                                continue
                            kb = m - 1 + half
                            ka = KT[0:64, r + d * 128 * kb: r + d * 128 * kb + d * 127 + 1: d]
                            mm(sbk[:, gi * 256 + half * 128: gi * 256 + half * 128 + 128], ka, qa, True, True,
                               [t_QT, t_KT], [t_s])
                    c0 = 128 if first else 0
                    c1 = ng * 256
                    if first:
                        stt("dve", stmp[:, c0:c1], sbk[:, c0:c1], 0.125, bt[:, 128:256], ALU.mult, ALU.add,
                            [t_s, t_BT], [t_stmp])
                    else:
                        stt("dve", stmp[:, c0:c1].rearrange("p (g n) -> p g n", g=ng),
                            sbk[:, c0:c1].rearrange("p (g n) -> p g n", g=ng), 0.125,
                            bt.unsqueeze(1).to_broadcast([128, ng, 256]), ALU.mult, ALU.add, [t_s, t_BT], [t_stmp])
                    act(pt[:, c0:c1], stmp[:, c0:c1], AF.Exp, [t_stmp], [t_pt])

                cur = [None]

                def back(stp):
                    r, g, m0 = stp["r"], stp["g"], stp["m0"]
                    pt, t_pt = stp["pt"]
                    if stp["firstg"]:
                        cur[0] = OBK.next()
                    ob, t_ob = cur[0]
                    for gi, m in enumerate(g):
                        oc = (m - m0) * 128
                        for half in (0, 1):
                            if m == 0 and half == 0:
                                continue
                            kb = m - 1 + half
                            mm(ob[0:65, oc:oc + 128], vget(r, kb), pt[:, gi * 256 + half * 128: gi * 256 + half * 128 + 128],
                               (half == 0) or (m == 0), half == 1, [t_pt, t_VN, t_VS], [t_ob])
                    if stp["lastg"]:
                        consume(ob, t_ob, r + d * stp["t0"], tl, d)

                for ix in range(len(steps) + LAG):
                    if ix < len(steps):
                        front(steps[ix])
                    if ix - LAG >= 0:
                        back(steps[ix - LAG])

            def load_bt(hcs):
                for ci, hc in enumerate(hcs):
                    pr.dma("sp", BT[:, ci, 0:128], skew_ap(hc, True), reads=[t_gd], writes=[t_BT])
                    pr.dma("sp", BT[:, ci, 128:256], skew_ap(hc, False), reads=[t_gd], writes=[t_BT])

            proj_fm(KT, 512, 1)
            for j in range(4):
                proj_fm(QT, j * 128, 0)
                load_bt([j])

                def consume_a(ob, t_ob, tok0, ntok, stride, j=j):
                    ti = tok0 // 512
                    epilogue(ob[0:64, :], ob[64:65, :], [t_ob], ti, lambda kc, j=j: WH[:, kc, j * 128 + 64:j * 128 + 128], t_WH,
                             j * 64, sink_col=i * 4 + j)
                banded(1, lambda r, kb: VA[:, kb, :], BT[:, 0, :], consume_a)

            if stage < 4:
                return
            for j in range(4):
                cb = 576 + j * 192
                proj_fm(QT, cb, 0)
                proj_fm(KT, cb + 64, 1)
                load_bt([4 + j, 8 + j, 12 + j])
                for ci, d in enumerate(B_CFG):
                    src = bass.AP(tensor=vd.tensor, offset=j * 65,
                                  ap=[[d * 260, 128], [260, d], [128 * d * 260, 32 // d], [1, 65]])
                    pr.dma("sp", VS[:, ci, :, :].rearrange("p (r m) e -> p r m e", r=d), src, reads=[t_vd], writes=[t_VS])
                for ci, d in enumerate(B_CFG):
                    def consume_b(ob, t_ob, tok0, ntok, stride, ci=ci):
                        ua = U[:, tok0: tok0 + stride * (ntok - 1) + 1: stride]
                        if ci == 0:
                            copy_any("dve", ua, ob[0:65, 0:ntok], [t_ob], [t_U])
                        else:
                            tt("dve", ua, ua, ob[0:65, 0:ntok], ALU.add, [t_ob, t_U], [t_U])
                    vget = lambda r, kb, d=d, ci=ci: VS[:, ci, r * (32 // d) + kb, :]
                    banded(d, vget, BT[:, ci, :], consume_b)
                for ti in range(NT):
                    epilogue(U[0:64, ti * 512:(ti + 1) * 512], U[64:65, ti * 512:(ti + 1) * 512], [t_U], ti,
                             lambda kc, cb=cb: WH[:, kc, cb + 128:cb + 192], t_WH, 256 + j * 64)

        def c_layer(li):
            i = li // 2
            C16 = ARENA.bitcast(BF16)
            o = 0
            WC = C16[:, o:o + 8 * 1024].rearrange("p (k n) -> p k n", k=8); o += 8 * 1024
            WQB = C16[:, o:o + 2 * 1536].rearrange("p (k n) -> p k n", k=2); o += 2 * 1536
            WKVB = C16[:, o:o + 1536].rearrange("p (k n) -> p k n", k=1); o += 1536
            LAT = C16[:, o:o + 3 * T].rearrange("p (c t) -> p c t", c=3); o += 3 * T
            VM = C16[:, o:o + 32 * 65].rearrange("p (m e) -> p m e", m=32); o += 32 * 65
            CNB = C16[:, o:o + 384]; o += 384
            o32 = (o + 1) // 2
            CS = [ARENA[0:64, o32 + k * 512: o32 + (k + 1) * 512] for k in range(4)]; o32 += 2048
            CNW = ARENA[:, o32:o32 + 384]; o32 += 384
            assert o32 <= 16384, o32
            t_WC, t_WQB, t_WKVB, t_LAT, t_KPE, t_VM, t_CNB, t_CNW = (Tok() for _ in range(8))
            csr = Rot([(CS[0], CS[1]), (CS[2], CS[3])])

            load_w(WC, cw_in[i], 1024, t_WC)
            load_w(WQB, cqb_in[i], 1536, t_WQB)
            load_w(WKVB, ckvb_in[i], 1536, t_WKVB)
            load_w(WO, cwo_in[i], D, t_WO)
            pr.dma("sp", CNW, cn_in[0:1, i * 384:(i + 1) * 384].partition_broadcast(128), writes=[t_CNW])
            pr.op("pool", lambda e: e.memset(VM[:, :, 64:65], 1.0), writes=[t_VM])

            for tb in range(NB):
                pj, t_pj = PJ.next()
                for kc in range(8):
                    mm(pj[:, 0:384], HT[:, kc, tb * 128:(tb + 1) * 128], WC[:, kc, 0:384], kc == 0, kc == 7,
                       [t_HT[tb], t_WC], [t_pj])
                stt_, t_st = ST.next()
                stmp, t_stmp = STMP.next()
                act(stmp[:, 0:256], pj[:, 0:256], AF.Square, [t_pj], [t_stmp, t_st], accum_out=stt_[:, 0:1])
                act(stmp[:, 256:384], pj[:, 256:384], AF.Square, [t_pj], [t_stmp, t_st], accum_out=stt_[:, 1:2])
                act(stt_[:, 2:3], stt_[:, 0:1], AF.Ln, [t_st, t_const], [t_st], scale=1.0 / 256, bias=EPS[:, 0:1])
                act(stt_[:, 3:4], stt_[:, 1:2], AF.Ln, [t_st, t_const], [t_st], scale=1.0 / 128, bias=EPS[:, 0:1])
                act(stt_[:, 4:6], stt_[:, 2:4], AF.Exp, [t_st], [t_st], scale=-0.5)
                stt("dve", CNB[:, 0:256], pj[:, 0:256], stt_[:, 4:5], CNW[:, 0:256], ALU.mult, ALU.mult,
                    [t_pj, t_st, t_CNW], [t_CNB])
                stt("dve", CNB[:, 256:384], pj[:, 256:384], stt_[:, 5:6], CNW[:, 256:384], ALU.mult, ALU.mult,
                    [t_pj, t_st, t_CNW], [t_CNB])
                for c3 in range(3):
                    pr.op("pe", lambda e, c3=c3: e.transpose(out=TR[:, c3 * 128:(c3 + 1) * 128], in_=CNB[:, c3 * 128:(c3 + 1) * 128],
                                                             identity=IDENT[:]),
                          reads=[t_CNB, t_const], writes=[t_TR])
                copy_any("act", LAT[:, :, tb * 128:(tb + 1) * 128], TR[:, 0:384].rearrange("p (c t) -> p c t", c=3), [t_TR], [t_LAT])

            def rope_tile(ti, pj_main, t_main, pj_sw, t_sw, dst, t_dst):
                (cc, ss), t_cs_t = csr.next()
                pr.dma("sp", cc, csd[0, :, ti * 512:(ti + 1) * 512], reads=[t_cs], writes=[t_cs_t])
                pr.dma("sp", ss, csd[1, :, ti * 512:(ti + 1) * 512], reads=[t_cs], writes=[t_cs_t])
                ea, t_ea = EPA.next()
                eb, t_eb = EPB.next()
                tt("dve", ea[:], pj_main[0:64, :], cc, ALU.mult, [t_main, t_cs_t], [t_ea])
                tt("dve", eb[:], pj_sw[0:64, :], ss, ALU.mult, [t_sw, t_cs_t], [t_eb])
                tt("pool", dst[0:64, ti * 512:(ti + 1) * 512], ea[:], eb[:], ALU.add, [t_ea, t_eb], [t_dst])

            for ti in range(NT):
                pa, t_pa = PJ.next()
                pb, t_pb = PJ.next()
                for kc in range(8):
                    mm(pa[0:64, :], WC[:, kc, 384:448], HT[:, kc, ti * 512:(ti + 1) * 512], kc == 0, kc == 7,
                       [t_WC] + t_HT[ti * 4:ti * 4 + 4], [t_pa])
                for kc in range(8):
                    mm(pb[0:64, :], WC[:, kc, 448:512], HT[:, kc, ti * 512:(ti + 1) * 512], kc == 0, kc == 7,
                       [t_WC] + t_HT[ti * 4:ti * 4 + 4], [t_pb])
                rope_tile(ti, pa, t_pa, pb, t_pb, KT, t_KT)

            scale = (64 + 32) ** -0.5
            for h in range(8):
                for ti in range(NT):
                    pj, t_pj = PJ.next()
                    mm(pj[:, :], WKVB[:, 0, h * 128:(h + 1) * 128], LAT[:, 2, ti * 512:(ti + 1) * 512], True, True,
                       [t_WKVB, t_LAT], [t_pj])
                    copy_any("act", KT[64:128, ti * 512:(ti + 1) * 512], pj[64:128, :], [t_pj], [t_KT])
                for b0 in range(0, NB, 8):
                    pj, t_pj = PJ.next()
                    for bb in range(8):
                        mm(pj[:, bb * 64:(bb + 1) * 64], LAT[:, 2, (b0 + bb) * 128:(b0 + bb + 1) * 128],
                           WKVB[:, 0, 1024 + h * 64:1024 + (h + 1) * 64], True, True, [t_WKVB, t_LAT], [t_pj])
                    copy_any("dve", VM[:, b0:b0 + 8, 0:64], pj[:, :].rearrange("p (m e) -> p m e", m=8), [t_pj], [t_VM])
                for ti in range(NT):
                    pa, t_pa = PJ.next()
                    pb, t_pb = PJ.next()
                    for c2 in range(2):
                        mm(pa[:, :], WQB[:, c2, h * 128:(h + 1) * 128], LAT[:, c2, ti * 512:(ti + 1) * 512], c2 == 0, c2 == 1,
                           [t_WQB, t_LAT], [t_pa])
                    for c2 in range(2):
                        mm(pb[0:64, :], WQB[:, c2, 1024 + h * 64:1024 + (h + 1) * 64], LAT[:, c2, ti * 512:(ti + 1) * 512],
                           c2 == 0, c2 == 1, [t_WQB, t_LAT], [t_pb])
                    copy_any("act", QT[64:128, ti * 512:(ti + 1) * 512], pa[64:128, :], [t_pa], [t_QT])
                    rope_tile(ti, pa, t_pa, pb, t_pb, QT, t_QT)
                steps = []
                for j in range(NT):
                    nkb = 4 * j + 4
                    for kb in range(nkb):
                        steps.append(dict(j=j, kb=kb, nkb=nkb))

                def front(stp):
                    j, kb = stp["j"], stp["kb"]
                    c0 = max(0, kb - 4 * j) * 128
                    sbk, t_s = SBK.next()
                    pt, t_pt = PT.next()
                    stp["pt"] = (pt, t_pt)
                    mm(sbk[:, c0:512], KT[:, kb * 128:(kb + 1) * 128], QT[:, j * 512 + c0:(j + 1) * 512], True, True,
                       [t_QT, t_KT], [t_s])
                    act(pt[:, c0:512], sbk[:, c0:512], AF.Exp, [t_s], [t_pt], scale=scale)
                    if kb >= 4 * j:
                        tt("pool", pt[:, c0:c0 + 128], pt[:, c0:c0 + 128], TRI[:], ALU.mult, [t_pt, t_const], [t_pt])

                cur = [None]

                def back(stp, h=h):
                    j, kb, nkb = stp["j"], stp["kb"], stp["nkb"]
                    c0 = max(0, kb - 4 * j) * 128
                    pt, t_pt = stp["pt"]
                    if kb == 0:
                        cur[0] = OBK.next()
                    ob, t_ob = cur[0]
                    mm(ob[0:65, c0:512], VM[:, kb, :], pt[:, c0:512], kb == 0, kb == nkb - 1, [t_pt, t_VM], [t_ob])
                    if kb == nkb - 1:
                        epilogue(ob[0:64, :], ob[64:65, :], [t_ob], j, lambda kc, h=h: WC[:, kc, 512 + h * 64:512 + (h + 1) * 64],
                                 t_WC, h * 64)

                for ix in range(len(steps) + LAG):
                    if ix < len(steps):
                        front(steps[ix])
                    if ix - LAG >= 0:
                        back(steps[ix - LAG])

        def phase_o(li, last):
            gf = go_full[li % 2]
            tgf = t_go_full[li % 2]
            evs = []
            for tb in range(NB):
                gi, t_gi = GI.next()
                pr.dma("sp", gi[:], gf[tb // 4][:, (tb % 4) * 128:(tb % 4 + 1) * 128].rearrange("(kc p) t -> p kc t", p=128),
                       reads=[tgf[tb // 4]], writes=[t_gi])
                xb, t_xb = XB.next()
                if li == 0:
                    pr.dma("sp", xb[:], x_in[tb * 128:(tb + 1) * 128, :], writes=[t_xb])
                else:
                    pr.dma("sp", xb[:], xs[tb * 128:(tb + 1) * 128, :], reads=[t_xs[tb]], writes=[t_xb])
                for half in range(2):
                    pj, t_pj = PJ.next()
                    for kc in range(8):
                        mm(pj[:, :], gi[:, kc, :], WO[:, kc, half * 512:(half + 1) * 512], kc == 0, kc == 7, [t_gi, t_WO], [t_pj])
                    tt("dve", xb[:, half * 512:(half + 1) * 512], xb[:, half * 512:(half + 1) * 512], pj[:, :], ALU.add,
                       [t_xb, t_pj], [t_xb])
                if not last:
                    pr.dma("pool", xs[tb * 128:(tb + 1) * 128, :], xb[:], reads=[t_xb], writes=[t_xs[tb]])
                ev = norm_block(tb, xb, t_xb, final=last)
                if ev is not None:
                    evs.append(ev)
            return evs

        pr.barrier()
        load_nw(0)
        for tb in range(NB if stage >= 1 else 0):
            xb, t_xb = XB.next()
            pr.dma("sp", xb[:], x_in[tb * 128:(tb + 1) * 128, :], writes=[t_xb])
            norm_block(tb, xb, t_xb)
        final_evs = []
        for li in range(nlayers if stage >= 2 else 0):
            pr.barrier()
            if li % 2 == 0:
                ab_layer(li)
            else:
                c_layer(li)
            last = (li == nlayers - 1)
            if stage < 5:
                break
            gf = go_full[li % 2]
            for ti in range(NT):
                pr.collective(lambda e, gf=gf, ti=ti: e.collective_compute("AllGather", ALU.bypass, replica_groups=PAIRS,
                                                                           ins=[go_own[ti].ap().opt()], outs=[gf[ti].ap().opt()]),
                              reads=[t_go_own[ti]], writes=[t_go_full[li % 2][ti]])
            if stage < 6:
                break
            load_nw(4 if (last and nlayers == 4) else li + 1)
            final_evs = phase_o(li, last)
        pr.wait_all("pool", final_evs)
        pr.barrier()
        pr.emit_all()
        print("instructions:", pr.n_inst, {e: pr.cnt[e] for e in pr.ENG}, "sbuf left", nc.sbuf_bytes_remaining)
    return nc


def _t5_bucket(dist):
    dist = np.asarray(dist, dtype=np.int64)
    d = np.maximum(dist, 1).astype(np.float32)
    large = 16 + (np.log(d / np.float32(16)) / np.float32(math.log(2048 / 16)) * np.float32(16)).astype(np.int32)
    large = np.minimum(large, 31)
    return np.where(dist < 16, dist, large)


def _consts():
    oneh = np.zeros((33, 3, 383), np.float32)
    for ci, d in enumerate(B_CFG):
        for n in range(383):
            dist = n - 127
            if 0 <= dist <= 128:
                oneh[_t5_bucket(dist * d), ci, n] = 1.0
            else:
                oneh[32, ci, n] = NEG
    kk = np.arange(128)[:, None]
    qq = np.arange(128)[None, :]
    tri = (qq >= kk).astype(np.float32).astype(ml_dtypes.bfloat16)
    ident = np.eye(128, dtype=np.float32).astype(ml_dtypes.bfloat16)
    inv_freq = (10000.0 ** (-np.arange(0, 32, 2, dtype=np.float32) / np.float32(32))).astype(np.float32)
    ropec = np.zeros((128, 2), np.float32)
    ropec[0:16, 0] = inv_freq
    ropec[32:48, 0] = inv_freq
    ropec[0:16, 1] = -1.0
    ropec[32:48, 1] = 1.0
    return oneh.reshape(33, 3 * 383), tri, ident, ropec


def _core_inputs(c, inp, consts):
    b, hh = c // 2, c % 2
    oneh, tri, ident, ropec = consts
    f = np.float32
    m = {}
    m["x"] = np.ascontiguousarray(inp["x"][b], dtype=f)
    m["pos"] = np.ascontiguousarray(inp["positions"][b][None, :], dtype=np.int32)
    m["nw"] = np.ascontiguousarray(np.concatenate([inp["norm_w"], inp["final_norm"][None, :]], 0), dtype=f)
    rb = inp["rel_bias"]
    relb = np.ones((33, 8), f)
    relb[:32, 0:4] = rb[:, hh * 4:hh * 4 + 4]
    relb[:32, 4:8] = rb[:, 8 + hh * 4:8 + hh * 4 + 4]
    m["relb"] = relb
    m["oneh"] = oneh
    m["sinks"] = np.ascontiguousarray(inp["ab_sinks"][:, hh * 4:hh * 4 + 4].reshape(1, 8), dtype=f)
    m["ident"] = ident
    m["tri"] = tri
    m["ropec"] = ropec
    m["cn"] = np.ascontiguousarray(np.concatenate(
        [np.concatenate([inp["c_q_norm"][i], inp["c_kv_norm"][i]]) for i in range(2)])[None, :], dtype=f)
    for i in range(2):
        w = inp["ab_w_in"][i]
        qa, ka, va, qb, kb, vb, z = np.split(w, np.cumsum([512, 128, 128, 512, 512, 512]), axis=1)
        cols = []
        for j in range(4):
            h = hh * 4 + j
            cols += [qa[:, h * 64:(h + 1) * 64], z[:, h * 64:(h + 1) * 64]]
        cols.append(ka[:, hh * 64:(hh + 1) * 64])
        for j in range(4):
            h = hh * 4 + j
            cols += [qb[:, h * 64:(h + 1) * 64], kb[:, h * 64:(h + 1) * 64], z[:, 512 + h * 64:512 + (h + 1) * 64]]
        cols.append(va[:, hh * 64:(hh + 1) * 64])
        cols.append(vb[:, hh * 256:(hh + 1) * 256])
        m[f"abw{i}"] = np.ascontiguousarray(np.concatenate(cols, 1), dtype=f)
        assert m[f"abw{i}"].shape == (1024, 1664)
        wo = inp["ab_w_out"][i]
        m[f"abwo{i}"] = np.ascontiguousarray(np.concatenate([wo[0:256], wo[512:768], wo[256:512], wo[768:1024]], 0), dtype=f)
        w = inp["c_w_in"][i]
        kpe = w[:, 384:416]
        z16 = np.zeros((1024, 16), f)
        kp_pad = np.concatenate([kpe[:, 0:16], z16, kpe[:, 16:32], z16], 1)
        kp_sw = np.concatenate([kpe[:, 16:32], z16, kpe[:, 0:16], z16], 1)
        m[f"cw{i}"] = np.ascontiguousarray(np.concatenate([w[:, 0:384], kp_pad, kp_sw, w[:, 416 + hh * 512:416 + (hh + 1) * 512]], 1), dtype=f)
        wq = inp["c_w_qb"][i].reshape(256, 16, 96)
        z16q = np.zeros((256, 16), f)
        pads, sws = [], []
        for j in range(8):
            h = hh * 8 + j
            nope, pe = wq[:, h, 0:64], wq[:, h, 64:96]
            pads.append(np.concatenate([pe[:, 0:16], z16q, pe[:, 16:32], z16q, nope], 1))
            sws.append(np.concatenate([pe[:, 16:32], z16q, pe[:, 0:16], z16q], 1))
        m[f"cqb{i}"] = np.ascontiguousarray(np.concatenate(pads + sws, 1), dtype=f)
        wkv = inp["c_w_kvb"][i].reshape(128, 16, 128)
        z64 = np.zeros((128, 64), f)
        ks, vs = [], []
        for j in range(8):
            h = hh * 8 + j
            ks.append(np.concatenate([z64, wkv[:, h, 0:64]], 1))
            vs.append(wkv[:, h, 64:128])
        m[f"ckvb{i}"] = np.ascontiguousarray(np.concatenate(ks + vs, 1), dtype=f)
        m[f"cwo{i}"] = np.ascontiguousarray(inp["c_w_out"][i], dtype=f)
    return m


_NC_CACHE = {}


def kernel(x, positions, norm_w, rel_bias, ab_w_in, ab_sinks, ab_w_out, c_w_in, c_q_norm, c_w_qb,
           c_kv_norm, c_w_kvb, c_w_out, final_norm, _nlayers=4, _raw=False, _stage=99):
    inp = dict(x=x, positions=positions, norm_w=norm_w, rel_bias=rel_bias, ab_w_in=ab_w_in, ab_sinks=ab_sinks,
               ab_w_out=ab_w_out, c_w_in=c_w_in, c_q_norm=c_q_norm, c_w_qb=c_w_qb, c_kv_norm=c_kv_norm,
               c_w_kvb=c_w_kvb, c_w_out=c_w_out, final_norm=final_norm)
    inp = {k: np.asarray(v) for k, v in inp.items()}
    consts = _consts()
    in_maps = [_core_inputs(c, inp, consts) for c in range(8)]
    if (_nlayers, _raw, _stage) not in _NC_CACHE:
        _NC_CACHE[(_nlayers, _raw, _stage)] = build_program(_nlayers, _raw, _stage)
    nc = _NC_CACHE[(_nlayers, _raw, _stage)]
    res = run_bass_kernel_spmd(nc, in_maps, core_ids=list(range(8)))
    outs = []
    for b in range(4):
        o0 = res.results[2 * b]["out"]
        o1 = res.results[2 * b + 1]["out"]
        outs.append(np.concatenate([o0[:2048], o1[2048:]], 0))
    return np.stack(outs, 0).astype(np.float32)
```

```python
import contextlib
import math
import numpy as np
import ml_dtypes
import concourse.bass as bass
import concourse.mybir as mybir
from concourse.bass_utils import run_bass_kernel_spmd

F32 = mybir.dt.float32
BF16 = mybir.dt.bfloat16
I32 = mybir.dt.int32
AF = mybir.ActivationFunctionType
ALU = mybir.AluOpType

T = 4096
D = 1024
NB = 32
NT = 8
NEG = -30000.0
PAIRS = [[0, 1], [2, 3], [4, 5], [6, 7]]
B_CFG = (1, 4, 16)
LAG = 2


class Tok:
    __slots__ = ("w", "r")

    def __init__(self):
        self.w = None
        self.r = []


class Prog:
    ENG = ("pe", "act", "dve", "pool", "sp")

    def __init__(self, nc, stack, n_dma_sems=20):
        self.nc = nc
        self.q = {e: [] for e in self.ENG}
        self.cnt = {e: 0 for e in self.ENG}
        self.waited = {e: {} for e in self.ENG}
        self.sem = {e: stack.enter_context(nc.semaphore("s_" + e)) for e in self.ENG}
        self.semid = {id(self.sem[e]): e for e in self.ENG}
        self.dsem, self.dval, self.drr = {}, {}, {}
        for qe in ("sp", "pool", "act"):
            self.dsem[qe] = [stack.enter_context(nc.semaphore(f"d_{qe}_{i}")) for i in range(n_dma_sems)]
            self.dval[qe] = [0] * n_dma_sems
            self.drr[qe] = 0
        self.cc_sem = stack.enter_context(nc.semaphore("cc"))
        self.cc_val = 0
        self.n_inst = 0

    def _collect(self, e, reads, writes, extra=()):
        deps = {}

        def add(ev):
            if ev is None:
                return
            s, v = ev
            k = id(s)
            if k not in deps or deps[k][1] < v:
                deps[k] = (s, v)

        for t in reads:
            add(t.w)
        for t in writes:
            add(t.w)
            for ev in t.r:
                add(ev)
        for ev in extra:
            add(ev)
        out = []
        for k, (s, v) in deps.items():
            if self.semid.get(k) == e and e == "pe":
                continue
            if self.waited[e].get(k, 0) >= v:
                continue
            self.waited[e][k] = v
            out.append((s, v))
        return out

    def _mark(self, ev, reads, writes):
        for t in reads:
            t.r.append(ev)
            if len(t.r) > 64:
                t.r = _compress(t.r)
        for t in writes:
            t.w = ev
            t.r = []

    def op(self, e, fn, reads=(), writes=(), extra=()):
        waits = self._collect(e, reads, writes, extra)
        self.cnt[e] += 1
        sem = self.sem[e]
        ev = (sem, self.cnt[e])

        def emit(eng, fn=fn, waits=waits, sem=sem):
            for (s, v) in waits:
                eng.wait_ge(s, v)
            fn(eng).then_inc(sem, 1)

        self.q[e].append(emit)
        self._mark(ev, reads, writes)
        self.n_inst += 1
        return ev

    def dma(self, qe, out, in_, reads=(), writes=(), extra=()):
        k = self.drr[qe]
        self.drr[qe] = (k + 1) % len(self.dsem[qe])
        s = self.dsem[qe][k]
        prev = self.dval[qe][k]
        self.dval[qe][k] = prev + 16
        ev = (s, prev + 16)
        ex = list(extra)
        if prev > 0:
            ex.append((s, prev))
        waits = self._collect(qe, reads, writes, ex)

        def emit(eng, waits=waits, s=s, out=out, in_=in_):
            for (ws, v) in waits:
                eng.wait_ge(ws, v)
            eng.dma_start(out=out, in_=in_).then_inc(s, 16)

        self.q[qe].append(emit)
        self._mark(ev, reads, writes)
        self.n_inst += 1
        return ev

    def collective(self, fn, reads, writes):
        waits = self._collect("pool", reads, writes)
        self.cc_val += 1
        ev = (self.cc_sem, self.cc_val)

        def emit(eng, waits=waits, fn=fn):
            for (s, v) in waits:
                eng.wait_ge(s, v)
            fn(eng).then_inc(self.cc_sem, 1)

        self.q["pool"].append(emit)
        self._mark(ev, reads, writes)
        return ev

    def wait_all(self, e, evs):
        waits = self._collect(e, (), (), evs)

        def emit(eng, waits=waits):
            for (s, v) in waits:
                eng.wait_ge(s, v)

        self.q[e].append(emit)

    def all_events(self):
        evs = [(self.sem[e], self.cnt[e]) for e in self.ENG if self.cnt[e] > 0]
        for qe in self.dsem:
            for s, v in zip(self.dsem[qe], self.dval[qe]):
                if v > 0:
                    evs.append((s, v))
        if self.cc_val > 0:
            evs.append((self.cc_sem, self.cc_val))
        return evs

    def barrier(self):
        evs = self.all_events()
        for e in self.ENG:
            self.wait_all(e, evs)

    def emit_all(self):
        nc = self.nc
        with nc.Block() as block:
            @block.tensor
            def _(eng):
                for f in self.q["pe"]:
                    f(eng)

            @block.scalar
            def _(eng):
                for f in self.q["act"]:
                    f(eng)

            @block.vector
            def _(eng):
                for f in self.q["dve"]:
                    f(eng)

            @block.gpsimd
            def _(eng):
                for f in self.q["pool"]:
                    f(eng)

            @block.sync
            def _(eng):
                for f in self.q["sp"]:
                    f(eng)


def _compress(evs):
    best = {}
    for s, v in evs:
        k = id(s)
        if k not in best or best[k][1] < v:
            best[k] = (s, v)
    return list(best.values())


class Rot:
    def __init__(self, items):
        self.items = [(it, Tok()) for it in items]
        self.i = 0

    def next(self):
        it = self.items[self.i]
        self.i = (self.i + 1) % len(self.items)
        return it


def build_program(nlayers=4, raw=False, stage=99):
    nc = bass.Bass("TRN2", target_bir_lowering=False)

    def din(name, shape, dt):
        return nc.dram_tensor(name, list(shape), dt, kind="ExternalInput").ap()

    x_in = din("x", [T, D], F32)
    pos_in = din("pos", [1, T], I32)
    nw_in = din("nw", [5, D], F32)
    relb_in = din("relb", [33, 8], F32)
    oneh_in = din("oneh", [33, 3 * 383], F32)
    sinks_in = din("sinks", [1, 8], F32)
    ident_in = din("ident", [128, 128], BF16)
    tri_in = din("tri", [128, 128], BF16)
    ropec_in = din("ropec", [128, 2], F32)
    cn_in = din("cn", [1, 2 * 384], F32)
    abw_in = [din(f"abw{i}", [D, 1664], F32) for i in range(2)]
    abwo_in = [din(f"abwo{i}", [D, D], F32) for i in range(2)]
    cw_in = [din(f"cw{i}", [D, 1024], F32) for i in range(2)]
    cqb_in = [din(f"cqb{i}", [256, 1536], F32) for i in range(2)]
    ckvb_in = [din(f"ckvb{i}", [128, 1536], F32) for i in range(2)]
    cwo_in = [din(f"cwo{i}", [D, D], F32) for i in range(2)]
    out = nc.dram_tensor("out", [T, D], F32, kind="ExternalOutput").ap()

    xs = nc.dram_tensor("xs", [T, D], F32).ap()
    go_own = [nc.dram_tensor(f"go_own{t}", [512, 512], BF16) for t in range(NT)]
    go_full = [[nc.dram_tensor(f"go_full{i}_{t}", [1024, 512], BF16) for t in range(NT)] for i in range(2)]
    gd = nc.dram_tensor("gd", [16, 128 * 383], F32)
    vd = nc.dram_tensor("vd", [T, 4 * 65], BF16).ap()
    csd = nc.dram_tensor("csd", [2, 64, T], F32).ap()

    with contextlib.ExitStack() as st:
        def sb(name, shape, dt):
            return st.enter_context(nc.sbuf_tensor(name, list(shape), dt))

        def psum(name, shape, dt):
            return st.enter_context(nc.psum_tensor(name, list(shape), dt))

        pr = Prog(nc, st)

        HT = sb("HT", [128, 8, T], BF16)
        WO = sb("WO", [128, 8, D], BF16)
        QT = sb("QT", [128, T], BF16)
        KT = sb("KT", [128, T], BF16)
        XB = Rot([sb(f"XB{i}", [128, D], F32) for i in range(2)])
        HB = Rot([sb(f"HB{i}", [128, D], BF16) for i in range(2)])
        PT = Rot([sb(f"PT{i}", [128, 512], BF16) for i in range(4)])
        STMP = Rot([sb(f"STMP{i}", [128, 512], F32) for i in range(2)])
        GI = Rot([sb(f"GI{i}", [128, 8, 128], BF16) for i in range(2)])
        GOUT = Rot([sb(f"GOUT{i}", [64, 512], BF16) for i in range(2)])
        WSTG = Rot([sb(f"WSTG{i}", [128, 8, 64], F32) for i in range(3)])
        NW = sb("NW", [128, D], F32)
        ST = Rot([sb(f"ST{i}", [128, 8], F32) for i in range(4)])
        IDENT = sb("IDENT", [128, 128], BF16)
        TRI = sb("TRI", [128, 128], BF16)
        ONES = sb("ONES", [128, 64], F32)
        EPS = sb("EPS", [128, 1], F32)
        ES = sb("ES", [128, 8], F32)
        EPA = Rot([sb(f"EPA{i}", [64, 512], F32) for i in range(2)])
        EPB = Rot([sb(f"EPB{i}", [64, 512], F32) for i in range(2)])
        RD = Rot([sb(f"RD{i}", [65, 512], F32) for i in range(1)])
        ARENA = sb("ARENA", [128, 16384], F32)

        t_HT = [Tok() for _ in range(NB)]
        t_WO, t_QT, t_KT, t_NW = Tok(), Tok(), Tok(), Tok()
        t_const = Tok()
        t_arena = Tok()

        SBK = Rot([psum(f"S{i}", [128, 512], F32) for i in range(2)])
        OBK = Rot([psum(f"O{i}", [128, 512], F32) for i in range(2)])
        PJ = Rot([psum(f"PJ{i}", [128, 512], F32) for i in range(3)])
        TR = psum("TR", [128, 1024], BF16)
        t_TR = Tok()

        def mm(out_, lhsT, rhs, start, stop, reads, writes):
            return pr.op("pe", lambda e: e.matmul(out_, lhsT=lhsT, rhs=rhs, start=start, stop=stop,
                                                  skip_group_check=True), reads=reads, writes=writes)

        def act(out_, in_, func, reads, writes, **kw):
            return pr.op("act", lambda e: e.activation(out=out_, in_=in_, func=func, **kw), reads=reads, writes=writes)

        def copy_any(eng, out_, in_, reads, writes):
            if eng == "act":
                return pr.op("act", lambda e: e.copy(out=out_, in_=in_), reads=reads, writes=writes)
            return pr.op(eng, lambda e: e.tensor_copy(out=out_, in_=in_), reads=reads, writes=writes)

        def tt(eng, out_, in0, in1, op, reads, writes):
            return pr.op(eng, lambda e: e.tensor_tensor(out=out_, in0=in0, in1=in1, op=op), reads=reads, writes=writes)

        def ts(eng, out_, in0, s1, s2, op0, op1, reads, writes):
            if s2 is None:
                return pr.op(eng, lambda e: e.tensor_scalar(out=out_, in0=in0, scalar1=s1, scalar2=None, op0=op0),
                             reads=reads, writes=writes)
            return pr.op(eng, lambda e: e.tensor_scalar(out=out_, in0=in0, scalar1=s1, scalar2=s2, op0=op0, op1=op1),
                         reads=reads, writes=writes)

        def stt(eng, out_, in0, scalar, in1, op0, op1, reads, writes):
            return pr.op(eng, lambda e: e.scalar_tensor_tensor(out=out_, in0=in0, scalar=scalar, in1=in1, op0=op0, op1=op1),
                         reads=reads, writes=writes)

        def recip(out_, in_, reads, writes):
            return pr.op("dve", lambda e: e.reciprocal(out=out_, in_=in_), reads=reads, writes=writes)

        wrr = [0]

        def load_w(dst, src, ncols, t_dst, engs=("pool", "dve", "act")):
            kch = dst.shape[1]
            c0 = 0
            while c0 < ncols:
                cw = min(64, ncols - c0)
                stg, t_stg = WSTG.next()
                pr.dma("sp", stg[:, 0:kch, 0:cw], src[:, c0:c0 + cw].rearrange("(kc p) n -> p kc n", p=128),
                       writes=[t_stg])
                copy_any(engs[wrr[0] % len(engs)], dst[:, :, c0:c0 + cw], stg[:, 0:kch, 0:cw], [t_stg], [t_dst])
                wrr[0] += 1
                c0 += cw

        pr.dma("sp", IDENT[:], ident_in[:, :], writes=[t_const])
        pr.dma("sp", TRI[:], tri_in[:, :], writes=[t_const])
        pr.op("pool", lambda e: e.memset(ONES[:], 1.0), writes=[t_const])
        pr.op("pool", lambda e: e.memset(EPS[:], 1e-6), writes=[t_const])
        pr.dma("sp", ES[:], sinks_in[0:1, :].partition_broadcast(128), writes=[t_const])
        act(ES[:], ES[:], AF.Exp, [t_const], [t_const])

        A32 = ARENA
        RELB = A32[0:33, 0:8]
        ONEH = A32[0:33, 8:8 + 3 * 383]
        LB = A32[0:33, 1200:1328]
        GSB = [A32[:, 1400:1783], A32[:, 1800:2183]]
        t_gsb = [Tok(), Tok()]
        t_gd = Tok()
        t_lb = Tok()
        pr.dma("sp", RELB, relb_in[:, :], writes=[t_arena])
        pr.dma("sp", ONEH, oneh_in[:, :], writes=[t_arena])
        for hc in range(16):
            col = hc if hc < 4 else 4 + (hc - 4) % 4
            cfg = 0 if hc < 8 else (1 if hc < 12 else 2)
            copy_any("dve", LB, RELB[:, col:col + 1].to_broadcast([33, 128]), [t_arena], [t_lb])
            pj, t_pj = PJ.next()
            mm(pj[:, 0:383], LB, ONEH[:, cfg * 383:(cfg + 1) * 383], True, True, [t_lb, t_arena], [t_pj])
            copy_any("dve", GSB[hc % 2], pj[:, 0:383], [t_pj], [t_gsb[hc % 2]])
            pr.dma("sp", gd[hc:hc + 1, :].rearrange("o (p n) -> (o p) n", p=128), GSB[hc % 2], reads=[t_gsb[hc % 2]], writes=[t_gd])

        def skew_ap(hc, prev):
            return bass.AP(tensor=gd.ap().tensor, offset=hc * 128 * 383 + (255 if prev else 127), ap=[[382, 128], [1, 128]])

        ROPEC = A32[:, 2200:2202]
        pr.dma("sp", ROPEC, ropec_in[:, :], writes=[t_arena])
        t_cs = Tok()
        RC = 1024
        r_pi = A32[0:64, 4096:4096 + RC].bitcast(I32)
        r_a = A32[0:64, 5120:5120 + RC]
        r_b = A32[0:64, 6144:6144 + RC]
        r_ti = A32[0:64, 7168:7168 + RC].bitcast(I32)
        r_tf = A32[0:64, 8192:8192 + RC]
        r_r = A32[0:64, 9216:9216 + RC]
        r_m = A32[0:64, 10240:10240 + RC]
        r_o = [A32[0:64, 11264:11264 + RC], A32[0:64, 12288:12288 + RC]]
        tr_ = Tok()
        t_ro = [Tok(), Tok()]
        C1 = 6.28125
        C2 = 2.0 * math.pi - 6.28125
        for ch in range(T // RC):
            pr.dma("sp", r_pi, pos_in[0:1, ch * RC:(ch + 1) * RC].partition_broadcast(64), writes=[tr_])
            copy_any("dve", r_a, r_pi, [tr_], [tr_])
            ts("dve", r_a, r_a, ROPEC[0:64, 0:1], None, ALU.mult, None, [tr_, t_arena], [tr_])
            for which in (0, 1):
                shift = math.pi / 2 if which == 0 else 0.0
                ts("dve", r_b, r_a, shift, 1.0 / (2 * math.pi), ALU.add, ALU.mult, [tr_], [tr_])
                copy_any("dve", r_ti, r_b, [tr_], [tr_])
                copy_any("dve", r_tf, r_ti, [tr_], [tr_])
                ts("dve", r_b, r_a, shift, None, ALU.add, None, [tr_], [tr_])
                stt("dve", r_r, r_tf, -C1, r_b, ALU.mult, ALU.add, [tr_], [tr_])
                stt("dve", r_r, r_tf, -C2, r_r, ALU.mult, ALU.add, [tr_], [tr_])
                ts("dve", r_m, r_r, math.pi, None, ALU.is_gt, None, [tr_], [tr_])
                stt("dve", r_r, r_m, -2 * math.pi, r_r, ALU.mult, ALU.add, [tr_], [tr_])
                ts("dve", r_m, r_r, -math.pi, None, ALU.is_lt, None, [tr_], [tr_])
                stt("dve", r_r, r_m, 2 * math.pi, r_r, ALU.mult, ALU.add, [tr_], [tr_])
                ts("dve", r_r, r_r, -3.14159, 3.14159, ALU.max, ALU.min, [tr_], [tr_])
                ro, tro = r_o[which], t_ro[which]
                act(ro, r_r, AF.Sin, [tr_], [tro])
                if which == 1:
                    ts("dve", ro, ro, ROPEC[0:64, 1:2], None, ALU.mult, None, [tro, t_arena], [tro])
                pr.dma("sp", csd[which, :, ch * RC:(ch + 1) * RC], ro, reads=[tro], writes=[t_cs])

        def norm_block(tb, xb, t_xb, final=False):
            stt_, t_st = ST.next()
            hb, t_hb = HB.next()
            act(hb[:], xb[:], AF.Square, [t_xb], [t_hb, t_st], accum_out=stt_[:, 0:1])
            act(stt_[:, 1:2], stt_[:, 0:1], AF.Ln, [t_st, t_const], [t_st], scale=1.0 / D, bias=EPS[:, 0:1])
            act(stt_[:, 2:3], stt_[:, 1:2], AF.Exp, [t_st], [t_st], scale=-0.5)
            if final:
                if not raw:
                    stt("dve", xb[:], xb[:], stt_[:, 2:3], NW[:], ALU.mult, ALU.mult, [t_xb, t_st, t_NW], [t_xb])
                return pr.dma("pool", out[tb * 128:(tb + 1) * 128, :], xb[:], reads=[t_xb])
            stt("dve", hb[:], xb[:], stt_[:, 2:3], NW[:], ALU.mult, ALU.mult, [t_xb, t_st, t_NW], [t_hb])
            for kc in range(8):
                pr.op("pe", lambda e, kc=kc: e.transpose(out=TR[:, kc * 128:(kc + 1) * 128], in_=hb[:, kc * 128:(kc + 1) * 128],
                                                         identity=IDENT[:]),
                      reads=[t_hb, t_const], writes=[t_TR])
            copy_any("act", HT[:, :, tb * 128:(tb + 1) * 128], TR[:].rearrange("p (k t) -> p k t", k=8), [t_TR], [t_HT[tb]])
            return None

        def load_nw(row):
            pr.dma("sp", NW[:], nw_in[row:row + 1, :].partition_broadcast(128), writes=[t_NW])

        def epilogue(num, den, src_reads, ti, wz_lhsT, wz_tok, row0, sink_col=None):
            pj, t_pj = PJ.next()
            for kc in range(8):
                mm(pj[0:64, :], wz_lhsT(kc), HT[:, kc, ti * 512:(ti + 1) * 512], kc == 0, kc == 7,
                   [wz_tok] + t_HT[ti * 4:ti * 4 + 4], [t_pj])
            ea, t_ea = EPA.next()
            eb, t_eb = EPB.next()
            rd, t_rd = RD.next()
            act(ea[:], pj[0:64, :], AF.Exp, [t_pj], [t_ea], scale=-1.0)
            if sink_col is not None:
                ts("dve", rd[64:65, :], den, ES[64:65, sink_col:sink_col + 1], None, ALU.add, None, src_reads + [t_const], [t_rd])
            else:
                copy_any("dve", rd[64:65, :], den, src_reads, [t_rd])
            bc, t_bc = PJ.next()
            mm(bc[0:64, :], ONES[64:65, 0:64], rd[64:65, :], True, True, [t_rd, t_const], [t_bc])
            stt("dve", eb[:], ea[:], 1.0, bc[0:64, :], ALU.add, ALU.mult, [t_ea, t_bc], [t_eb])
            act(eb[:], eb[:], AF.Ln, [t_eb], [t_eb])
            act(eb[:], eb[:], AF.Exp, [t_eb], [t_eb], scale=-1.0)
            tt("dve", ea[:], pj[0:64, :], eb[:], ALU.mult, [t_pj, t_eb, t_ea], [t_ea])
            go, t_go = GOUT.next()
            tt("dve", go[:], num, ea[:], ALU.mult, src_reads + [t_ea], [t_go])
            pr.dma("pool", go_own[ti][row0:row0 + 64, :], go[:], reads=[t_go], writes=[t_go_own[ti]])

        t_go_own = [Tok() for _ in range(NT)]
        t_go_full = [[Tok() for _ in range(NT)] for _ in range(2)]
        t_xs = [Tok() for _ in range(NB)]

        def ab_layer(li):
            i = li // 2
            AB16 = ARENA.bitcast(BF16)
            o = 0
            WH = AB16[:, o:o + 8 * 1664].rearrange("p (k n) -> p k n", k=8); o += 8 * 1664
            VA = AB16[:, o:o + 32 * 65].rearrange("p (m e) -> p m e", m=32); o += 32 * 65
            VS = AB16[:, o:o + 3 * 32 * 65].rearrange("p (c m e) -> p c m e", c=3, m=32); o += 3 * 32 * 65
            VST = [AB16[:, o + k * 260:o + (k + 1) * 260].rearrange("p (h e) -> p h e", h=4) for k in range(2)]; o += 520
            o32 = (o + 1) // 2
            U = ARENA[0:65, o32:o32 + T]; o32 += T
            BT = ARENA[:, o32:o32 + 3 * 256].rearrange("p (c n) -> p c n", c=3); o32 += 768
            assert o32 <= 16384, o32
            vst = Rot(VST)
            t_WH, t_VN, t_VS, t_U, t_BT, t_vd = Tok(), Tok(), Tok(), Tok(), Tok(), Tok()

            load_w(WH, abw_in[i], 1664, t_WH)
            pr.op("pool", lambda e: e.memset(VA[:, :, 64:65], 1.0), writes=[t_VN])
            pr.op("pool", lambda e: e.memset(VS[:, :, :, 64:65], 1.0), writes=[t_VS])
            for (v_, tv_) in vst.items:
                pr.op("pool", lambda e, v_=v_: e.memset(v_[:, :, 64:65], 1.0), writes=[tv_])

            for blk in range(NB):
                pj, t_pj = PJ.next()
                for kc in range(8):
                    mm(pj[:, 0:320], HT[:, kc, blk * 128:(blk + 1) * 128], WH[:, kc, 1344:1664], kc == 0, kc == 7,
                       [t_HT[blk], t_WH], [t_pj])
                copy_any("dve", VA[:, blk, 0:64], pj[:, 0:64], [t_pj], [t_VN])
                v_, tv_ = vst.next()
                copy_any("dve", v_[:, :, 0:64], pj[:, 64:320].rearrange("p (h e) -> p h e", h=4), [t_pj], [tv_])
                pr.dma("pool", vd[blk * 128:(blk + 1) * 128, :], v_, reads=[tv_], writes=[t_vd])

            if stage < 3:
                return
            def proj_fm(dst, col0, alt):
                for ti in range(NT):
                    pj, t_pj = PJ.next()
                    for kc in range(8):
                        mm(pj[0:64, :], WH[:, kc, col0:col0 + 64], HT[:, kc, ti * 512:(ti + 1) * 512], kc == 0, kc == 7,
                           [t_WH] + t_HT[ti * 4:ti * 4 + 4], [t_pj])
                    copy_any("act" if (ti + alt) % 2 == 0 else "dve", dst[0:64, ti * 512:(ti + 1) * 512], pj[0:64, :], [t_pj],
                             [t_QT if dst is QT else t_KT])

            def banded(d, vget, bt, consume):
                L = T // d
                tl = min(512, L)
                steps = []
                for r in range(d):
                    for t0 in range(0, L, tl):
                        m0 = t0 // 128
                        nb_t = tl // 128
                        groups = []
                        mm_ = m0
                        while mm_ < m0 + nb_t:
                            if mm_ == 0 or mm_ == m0 + nb_t - 1:
                                groups.append([mm_]); mm_ += 1
                            else:
                                groups.append([mm_, mm_ + 1]); mm_ += 2
                        for gx, g in enumerate(groups):
                            steps.append(dict(r=r, t0=t0, m0=m0, g=g, firstg=(gx == 0), lastg=(gx == len(groups) - 1)))

                def front(stp):
                    r, g = stp["r"], stp["g"]
                    sbk, t_s = SBK.next()
                    stmp, t_stmp = STMP.next()
                    pt, t_pt = PT.next()
                    stp["pt"] = (pt, t_pt)
                    ng = len(g)
                    first = (g[0] == 0)
                    for gi, m in enumerate(g):
                        qa = QT[0:64, r + d * 128 * m: r + d * 128 * m + d * 127 + 1: d]
                        for half in (0, 1):
                            if m == 0 and half == 0:
                                continue
                            kb = m - 1 + half
                            ka = KT[0:64, r + d * 128 * kb: r + d * 128 * kb + d * 127 + 1: d]
                            mm(sbk[:, gi * 256 + half * 128: gi * 256 + half * 128 + 128], ka, qa, True, True,
                               [t_QT, t_KT], [t_s])
                    c0 = 128 if first else 0
                    c1 = ng * 256
                    if first:
                        stt("dve", stmp[:, c0:c1], sbk[:, c0:c1], 0.125, bt[:, 128:256], ALU.mult, ALU.add,
                            [t_s, t_BT], [t_stmp])
                    else:
                        stt("dve", stmp[:, c0:c1].rearrange("p (g n) -> p g n", g=ng),
                            sbk[:, c0:c1].rearrange("p (g n) -> p g n", g=ng), 0.125,
                            bt.unsqueeze(1).to_broadcast([128, ng, 256]), ALU.mult, ALU.add, [t_s, t_BT], [t_stmp])
                    act(pt[:, c0:c1], stmp[:, c0:c1], AF.Exp, [t_stmp], [t_pt])

                cur = [None]

                def back(stp):
                    r, g, m0 = stp["r"], stp["g"], stp["m0"]
                    pt, t_pt = stp["pt"]
                    if stp["firstg"]:
                        cur[0] = OBK.next()
                    ob, t_ob = cur[0]
                    for gi, m in enumerate(g):
                        oc = (m - m0) * 128
                        for half in (0, 1):
                            if m == 0 and half == 0:
                                continue
                            kb = m - 1 + half
                            mm(ob[0:65, oc:oc + 128], vget(r, kb), pt[:, gi * 256 + half * 128: gi * 256 + half * 128 + 128],
                               (half == 0) or (m == 0), half == 1, [t_pt, t_VN, t_VS], [t_ob])
                    if stp["lastg"]:
                        consume(ob, t_ob, r + d * stp["t0"], tl, d)

                for ix in range(len(steps) + LAG):
                    if ix < len(steps):
                        front(steps[ix])
                    if ix - LAG >= 0:
                        back(steps[ix - LAG])

            def load_bt(hcs):
                for ci, hc in enumerate(hcs):
                    pr.dma("sp", BT[:, ci, 0:128], skew_ap(hc, True), reads=[t_gd], writes=[t_BT])
                    pr.dma("sp", BT[:, ci, 128:256], skew_ap(hc, False), reads=[t_gd], writes=[t_BT])

            proj_fm(KT, 512, 1)
            for j in range(4):
                proj_fm(QT, j * 128, 0)
                load_bt([j])

                def consume_a(ob, t_ob, tok0, ntok, stride, j=j):
                    ti = tok0 // 512
                    epilogue(ob[0:64, :], ob[64:65, :], [t_ob], ti, lambda kc, j=j: WH[:, kc, j * 128 + 64:j * 128 + 128], t_WH,
                             j * 64, sink_col=i * 4 + j)
                banded(1, lambda r, kb: VA[:, kb, :], BT[:, 0, :], consume_a)
                load_w(WO[:, :, j * 128:(j + 1) * 128], abwo_in[i][:, j * 128:(j + 1) * 128], 128, t_WO, engs=("pool",))

            if stage < 4:
                return
            for j in range(4):
                cb = 576 + j * 192
                proj_fm(QT, cb, 0)
                proj_fm(KT, cb + 64, 1)
                load_bt([4 + j, 8 + j, 12 + j])
                for ci, d in enumerate(B_CFG):
                    src = bass.AP(tensor=vd.tensor, offset=j * 65,
                                  ap=[[d * 260, 128], [260, d], [128 * d * 260, 32 // d], [1, 65]])
                    pr.dma("sp", VS[:, ci, :, :].rearrange("p (r m) e -> p r m e", r=d), src, reads=[t_vd], writes=[t_VS])
                for ci, d in enumerate(B_CFG):
                    def consume_b(ob, t_ob, tok0, ntok, stride, ci=ci):
                        ua = U[:, tok0: tok0 + stride * (ntok - 1) + 1: stride]
                        if ci == 0:
                            copy_any("dve", ua, ob[0:65, 0:ntok], [t_ob], [t_U])
                        else:
                            tt("dve", ua, ua, ob[0:65, 0:ntok], ALU.add, [t_ob, t_U], [t_U])
                    vget = lambda r, kb, d=d, ci=ci: VS[:, ci, r * (32 // d) + kb, :]
                    banded(d, vget, BT[:, ci, :], consume_b)
                for ti in range(NT):
                    epilogue(U[0:64, ti * 512:(ti + 1) * 512], U[64:65, ti * 512:(ti + 1) * 512], [t_U], ti,
                             lambda kc, cb=cb: WH[:, kc, cb + 128:cb + 192], t_WH, 256 + j * 64)
                load_w(WO[:, :, (4 + j) * 128:(5 + j) * 128], abwo_in[i][:, (4 + j) * 128:(5 + j) * 128], 128, t_WO, engs=("pool",))

        def c_layer(li):
            i = li // 2
            C16 = ARENA.bitcast(BF16)
            o = 0
            WC = C16[:, o:o + 8 * 1024].rearrange("p (k n) -> p k n", k=8); o += 8 * 1024
            WQB = C16[:, o:o + 2 * 1536].rearrange("p (k n) -> p k n", k=2); o += 2 * 1536
            WKVB = C16[:, o:o + 1536].rearrange("p (k n) -> p k n", k=1); o += 1536
            LAT = C16[:, o:o + 3 * T].rearrange("p (c t) -> p c t", c=3); o += 3 * T
            VM = C16[:, o:o + 32 * 65].rearrange("p (m e) -> p m e", m=32); o += 32 * 65
            CNB = C16[:, o:o + 384]; o += 384
            o32 = (o + 1) // 2
            CS = [ARENA[0:64, o32 + k * 512: o32 + (k + 1) * 512] for k in range(4)]; o32 += 2048
            CNW = ARENA[:, o32:o32 + 384]; o32 += 384
            assert o32 <= 16384, o32
            t_WC, t_WQB, t_WKVB, t_LAT, t_KPE, t_VM, t_CNB, t_CNW = (Tok() for _ in range(8))
            csr = Rot([(CS[0], CS[1]), (CS[2], CS[3])])

            load_w(WC, cw_in[i], 1024, t_WC)
            load_w(WQB, cqb_in[i], 1536, t_WQB)
            load_w(WKVB, ckvb_in[i], 1536, t_WKVB)
            pr.dma("sp", CNW, cn_in[0:1, i * 384:(i + 1) * 384].partition_broadcast(128), writes=[t_CNW])
            pr.op("pool", lambda e: e.memset(VM[:, :, 64:65], 1.0), writes=[t_VM])

            for tb in range(NB):
                pj, t_pj = PJ.next()
                for kc in range(8):
                    mm(pj[:, 0:384], HT[:, kc, tb * 128:(tb + 1) * 128], WC[:, kc, 0:384], kc == 0, kc == 7,
                       [t_HT[tb], t_WC], [t_pj])
                stt_, t_st = ST.next()
                stmp, t_stmp = STMP.next()
                act(stmp[:, 0:256], pj[:, 0:256], AF.Square, [t_pj], [t_stmp, t_st], accum_out=stt_[:, 0:1])
                act(stmp[:, 256:384], pj[:, 256:384], AF.Square, [t_pj], [t_stmp, t_st], accum_out=stt_[:, 1:2])
                act(stt_[:, 2:3], stt_[:, 0:1], AF.Ln, [t_st, t_const], [t_st], scale=1.0 / 256, bias=EPS[:, 0:1])
                act(stt_[:, 3:4], stt_[:, 1:2], AF.Ln, [t_st, t_const], [t_st], scale=1.0 / 128, bias=EPS[:, 0:1])
                act(stt_[:, 4:6], stt_[:, 2:4], AF.Exp, [t_st], [t_st], scale=-0.5)
                stt("dve", CNB[:, 0:256], pj[:, 0:256], stt_[:, 4:5], CNW[:, 0:256], ALU.mult, ALU.mult,
                    [t_pj, t_st, t_CNW], [t_CNB])
                stt("dve", CNB[:, 256:384], pj[:, 256:384], stt_[:, 5:6], CNW[:, 256:384], ALU.mult, ALU.mult,
                    [t_pj, t_st, t_CNW], [t_CNB])
                for c3 in range(3):
                    pr.op("pe", lambda e, c3=c3: e.transpose(out=TR[:, c3 * 128:(c3 + 1) * 128], in_=CNB[:, c3 * 128:(c3 + 1) * 128],
                                                             identity=IDENT[:]),
                          reads=[t_CNB, t_const], writes=[t_TR])
                copy_any("act", LAT[:, :, tb * 128:(tb + 1) * 128], TR[:, 0:384].rearrange("p (c t) -> p c t", c=3), [t_TR], [t_LAT])

            def rope_tile(ti, pj_main, t_main, pj_sw, t_sw, dst, t_dst):
                (cc, ss), t_cs_t = csr.next()
                pr.dma("sp", cc, csd[0, :, ti * 512:(ti + 1) * 512], reads=[t_cs], writes=[t_cs_t])
                pr.dma("sp", ss, csd[1, :, ti * 512:(ti + 1) * 512], reads=[t_cs], writes=[t_cs_t])
                ea, t_ea = EPA.next()
                eb, t_eb = EPB.next()
                tt("dve", ea[:], pj_main[0:64, :], cc, ALU.mult, [t_main, t_cs_t], [t_ea])
                tt("dve", eb[:], pj_sw[0:64, :], ss, ALU.mult, [t_sw, t_cs_t], [t_eb])
                tt("pool", dst[0:64, ti * 512:(ti + 1) * 512], ea[:], eb[:], ALU.add, [t_ea, t_eb], [t_dst])

            for ti in range(NT):
                pa, t_pa = PJ.next()
                pb, t_pb = PJ.next()
                for kc in range(8):
                    mm(pa[0:64, :], WC[:, kc, 384:448], HT[:, kc, ti * 512:(ti + 1) * 512], kc == 0, kc == 7,
                       [t_WC] + t_HT[ti * 4:ti * 4 + 4], [t_pa])
                for kc in range(8):
                    mm(pb[0:64, :], WC[:, kc, 448:512], HT[:, kc, ti * 512:(ti + 1) * 512], kc == 0, kc == 7,
                       [t_WC] + t_HT[ti * 4:ti * 4 + 4], [t_pb])
                rope_tile(ti, pa, t_pa, pb, t_pb, KT, t_KT)

            scale = (64 + 32) ** -0.5
            for h in range(8):
                for ti in range(NT):
                    pj, t_pj = PJ.next()
                    mm(pj[:, :], WKVB[:, 0, h * 128:(h + 1) * 128], LAT[:, 2, ti * 512:(ti + 1) * 512], True, True,
                       [t_WKVB, t_LAT], [t_pj])
                    copy_any("act", KT[64:128, ti * 512:(ti + 1) * 512], pj[64:128, :], [t_pj], [t_KT])
                for b0 in range(0, NB, 8):
                    pj, t_pj = PJ.next()
                    for bb in range(8):
                        mm(pj[:, bb * 64:(bb + 1) * 64], LAT[:, 2, (b0 + bb) * 128:(b0 + bb + 1) * 128],
                           WKVB[:, 0, 1024 + h * 64:1024 + (h + 1) * 64], True, True, [t_WKVB, t_LAT], [t_pj])
                    copy_any("dve", VM[:, b0:b0 + 8, 0:64], pj[:, :].rearrange("p (m e) -> p m e", m=8), [t_pj], [t_VM])
                for ti in range(NT):
                    pa, t_pa = PJ.next()
                    pb, t_pb = PJ.next()
                    for c2 in range(2):
                        mm(pa[:, :], WQB[:, c2, h * 128:(h + 1) * 128], LAT[:, c2, ti * 512:(ti + 1) * 512], c2 == 0, c2 == 1,
                           [t_WQB, t_LAT], [t_pa])
                    for c2 in range(2):
                        mm(pb[0:64, :], WQB[:, c2, 1024 + h * 64:1024 + (h + 1) * 64], LAT[:, c2, ti * 512:(ti + 1) * 512],
                           c2 == 0, c2 == 1, [t_WQB, t_LAT], [t_pb])
                    copy_any("act", QT[64:128, ti * 512:(ti + 1) * 512], pa[64:128, :], [t_pa], [t_QT])
                    rope_tile(ti, pa, t_pa, pb, t_pb, QT, t_QT)
                steps = []
                for j in range(NT):
                    nkb = 4 * j + 4
                    for kb in range(nkb):
                        steps.append(dict(j=j, kb=kb, nkb=nkb))

                def front(stp):
                    j, kb = stp["j"], stp["kb"]
                    c0 = max(0, kb - 4 * j) * 128
                    sbk, t_s = SBK.next()
                    pt, t_pt = PT.next()
                    stp["pt"] = (pt, t_pt)
                    mm(sbk[:, c0:512], KT[:, kb * 128:(kb + 1) * 128], QT[:, j * 512 + c0:(j + 1) * 512], True, True,
                       [t_QT, t_KT], [t_s])
                    act(pt[:, c0:512], sbk[:, c0:512], AF.Exp, [t_s], [t_pt], scale=scale)
                    if kb >= 4 * j:
                        tt("pool", pt[:, c0:c0 + 128], pt[:, c0:c0 + 128], TRI[:], ALU.mult, [t_pt, t_const], [t_pt])

                cur = [None]

                def back(stp, h=h):
                    j, kb, nkb = stp["j"], stp["kb"], stp["nkb"]
                    c0 = max(0, kb - 4 * j) * 128
                    pt, t_pt = stp["pt"]
                    if kb == 0:
                        cur[0] = OBK.next()
                    ob, t_ob = cur[0]
                    mm(ob[0:65, c0:512], VM[:, kb, :], pt[:, c0:512], kb == 0, kb == nkb - 1, [t_pt, t_VM], [t_ob])
                    if kb == nkb - 1:
                        epilogue(ob[0:64, :], ob[64:65, :], [t_ob], j, lambda kc, h=h: WC[:, kc, 512 + h * 64:512 + (h + 1) * 64],
                                 t_WC, h * 64)

                for ix in range(len(steps) + LAG):
                    if ix < len(steps):
                        front(steps[ix])
                    if ix - LAG >= 0:
                        back(steps[ix - LAG])
                load_w(WO[:, :, h * 128:(h + 1) * 128], cwo_in[i][:, h * 128:(h + 1) * 128], 128, t_WO, engs=("pool",))

        def phase_o(li, last):
            gf = go_full[li % 2]
            tgf = t_go_full[li % 2]
            evs = []
            for tb in range(NB):
                gi, t_gi = GI.next()
                pr.dma("sp", gi[:], gf[tb // 4][:, (tb % 4) * 128:(tb % 4 + 1) * 128].rearrange("(kc p) t -> p kc t", p=128),
                       reads=[tgf[tb // 4]], writes=[t_gi])
                xb, t_xb = XB.next()
                if li == 0:
                    pr.dma("sp", xb[:], x_in[tb * 128:(tb + 1) * 128, :], writes=[t_xb])
                else:
                    pr.dma("sp", xb[:], xs[tb * 128:(tb + 1) * 128, :], reads=[t_xs[tb]], writes=[t_xb])
                for half in range(2):
                    pj, t_pj = PJ.next()
                    for kc in range(8):
                        mm(pj[:, :], gi[:, kc, :], WO[:, kc, half * 512:(half + 1) * 512], kc == 0, kc == 7, [t_gi, t_WO], [t_pj])
                    tt("dve", xb[:, half * 512:(half + 1) * 512], xb[:, half * 512:(half + 1) * 512], pj[:, :], ALU.add,
                       [t_xb, t_pj], [t_xb])
                if not last:
                    pr.dma("pool", xs[tb * 128:(tb + 1) * 128, :], xb[:], reads=[t_xb], writes=[t_xs[tb]])
                ev = norm_block(tb, xb, t_xb, final=last)
                if ev is not None:
                    evs.append(ev)
            return evs

        pr.barrier()
        load_nw(0)
        for tb in range(NB if stage >= 1 else 0):
            xb, t_xb = XB.next()
            pr.dma("sp", xb[:], x_in[tb * 128:(tb + 1) * 128, :], writes=[t_xb])
            norm_block(tb, xb, t_xb)
        final_evs = []
        for li in range(nlayers if stage >= 2 else 0):
            pr.barrier()
            if li % 2 == 0:
                ab_layer(li)
            else:
                c_layer(li)
            last = (li == nlayers - 1)
            if stage < 5:
                break
            gf = go_full[li % 2]
            for ti in range(NT):
                pr.collective(lambda e, gf=gf, ti=ti: e.collective_compute("AllGather", ALU.bypass, replica_groups=PAIRS,
                                                                           ins=[go_own[ti].ap().opt()], outs=[gf[ti].ap().opt()]),
                              reads=[t_go_own[ti]], writes=[t_go_full[li % 2][ti]])
            if stage < 6:
                break
            load_nw(4 if (last and nlayers == 4) else li + 1)
            final_evs = phase_o(li, last)
        pr.wait_all("pool", final_evs)
        pr.barrier()
        pr.emit_all()
        print("instructions:", pr.n_inst, {e: pr.cnt[e] for e in pr.ENG}, "sbuf left", nc.sbuf_bytes_remaining)
    return nc


def _t5_bucket(dist):
    dist = np.asarray(dist, dtype=np.int64)
    d = np.maximum(dist, 1).astype(np.float32)
    large = 16 + (np.log(d / np.float32(16)) / np.float32(math.log(2048 / 16)) * np.float32(16)).astype(np.int32)
    large = np.minimum(large, 31)
    return np.where(dist < 16, dist, large)


def _consts():
    oneh = np.zeros((33, 3, 383), np.float32)
    for ci, d in enumerate(B_CFG):
        for n in range(383):
            dist = n - 127
            if 0 <= dist <= 128:
                oneh[_t5_bucket(dist * d), ci, n] = 1.0
            else:
                oneh[32, ci, n] = NEG
    kk = np.arange(128)[:, None]
    qq = np.arange(128)[None, :]
    tri = (qq >= kk).astype(np.float32).astype(ml_dtypes.bfloat16)
    ident = np.eye(128, dtype=np.float32).astype(ml_dtypes.bfloat16)
    inv_freq = (10000.0 ** (-np.arange(0, 32, 2, dtype=np.float32) / np.float32(32))).astype(np.float32)
    ropec = np.zeros((128, 2), np.float32)
    ropec[0:16, 0] = inv_freq
    ropec[32:48, 0] = inv_freq
    ropec[0:16, 1] = -1.0
    ropec[32:48, 1] = 1.0
    return oneh.reshape(33, 3 * 383), tri, ident, ropec


def _core_inputs(c, inp, consts):
    b, hh = c // 2, c % 2
    oneh, tri, ident, ropec = consts
    f = np.float32
    m = {}
    m["x"] = np.ascontiguousarray(inp["x"][b], dtype=f)
    m["pos"] = np.ascontiguousarray(inp["positions"][b][None, :], dtype=np.int32)
    m["nw"] = np.ascontiguousarray(np.concatenate([inp["norm_w"], inp["final_norm"][None, :]], 0), dtype=f)
    rb = inp["rel_bias"]
    relb = np.ones((33, 8), f)
    relb[:32, 0:4] = rb[:, hh * 4:hh * 4 + 4]
    relb[:32, 4:8] = rb[:, 8 + hh * 4:8 + hh * 4 + 4]
    m["relb"] = relb
    m["oneh"] = oneh
    m["sinks"] = np.ascontiguousarray(inp["ab_sinks"][:, hh * 4:hh * 4 + 4].reshape(1, 8), dtype=f)
    m["ident"] = ident
    m["tri"] = tri
    m["ropec"] = ropec
    m["cn"] = np.ascontiguousarray(np.concatenate(
        [np.concatenate([inp["c_q_norm"][i], inp["c_kv_norm"][i]]) for i in range(2)])[None, :], dtype=f)
    for i in range(2):
        w = inp["ab_w_in"][i]
        qa, ka, va, qb, kb, vb, z = np.split(w, np.cumsum([512, 128, 128, 512, 512, 512]), axis=1)
        cols = []
        for j in range(4):
            h = hh * 4 + j
            cols += [qa[:, h * 64:(h + 1) * 64], z[:, h * 64:(h + 1) * 64]]
        cols.append(ka[:, hh * 64:(hh + 1) * 64])
        for j in range(4):
            h = hh * 4 + j
            cols += [qb[:, h * 64:(h + 1) * 64], kb[:, h * 64:(h + 1) * 64], z[:, 512 + h * 64:512 + (h + 1) * 64]]
        cols.append(va[:, hh * 64:(hh + 1) * 64])
        cols.append(vb[:, hh * 256:(hh + 1) * 256])
        m[f"abw{i}"] = np.ascontiguousarray(np.concatenate(cols, 1), dtype=f)
        assert m[f"abw{i}"].shape == (1024, 1664)
        wo = inp["ab_w_out"][i]
        m[f"abwo{i}"] = np.ascontiguousarray(np.concatenate([wo[0:256], wo[512:768], wo[256:512], wo[768:1024]], 0), dtype=f)
        w = inp["c_w_in"][i]
        kpe = w[:, 384:416]
        z16 = np.zeros((1024, 16), f)
        kp_pad = np.concatenate([kpe[:, 0:16], z16, kpe[:, 16:32], z16], 1)
        kp_sw = np.concatenate([kpe[:, 16:32], z16, kpe[:, 0:16], z16], 1)
        m[f"cw{i}"] = np.ascontiguousarray(np.concatenate([w[:, 0:384], kp_pad, kp_sw, w[:, 416 + hh * 512:416 + (hh + 1) * 512]], 1), dtype=f)
        wq = inp["c_w_qb"][i].reshape(256, 16, 96)
        z16q = np.zeros((256, 16), f)
        pads, sws = [], []
        for j in range(8):
            h = hh * 8 + j
            nope, pe = wq[:, h, 0:64], wq[:, h, 64:96]
            pads.append(np.concatenate([pe[:, 0:16], z16q, pe[:, 16:32], z16q, nope], 1))
            sws.append(np.concatenate([pe[:, 16:32], z16q, pe[:, 0:16], z16q], 1))
        m[f"cqb{i}"] = np.ascontiguousarray(np.concatenate(pads + sws, 1), dtype=f)
        wkv = inp["c_w_kvb"][i].reshape(128, 16, 128)
        z64 = np.zeros((128, 64), f)
        ks, vs = [], []
        for j in range(8):
            h = hh * 8 + j
            ks.append(np.concatenate([z64, wkv[:, h, 0:64]], 1))
            vs.append(wkv[:, h, 64:128])
        m[f"ckvb{i}"] = np.ascontiguousarray(np.concatenate(ks + vs, 1), dtype=f)
        m[f"cwo{i}"] = np.ascontiguousarray(inp["c_w_out"][i], dtype=f)
    return m


_NC_CACHE = {}


def kernel(x, positions, norm_w, rel_bias, ab_w_in, ab_sinks, ab_w_out, c_w_in, c_q_norm, c_w_qb,
           c_kv_norm, c_w_kvb, c_w_out, final_norm, _nlayers=4, _raw=False, _stage=99):
    inp = dict(x=x, positions=positions, norm_w=norm_w, rel_bias=rel_bias, ab_w_in=ab_w_in, ab_sinks=ab_sinks,
               ab_w_out=ab_w_out, c_w_in=c_w_in, c_q_norm=c_q_norm, c_w_qb=c_w_qb, c_kv_norm=c_kv_norm,
               c_w_kvb=c_w_kvb, c_w_out=c_w_out, final_norm=final_norm)
    inp = {k: np.asarray(v) for k, v in inp.items()}
    consts = _consts()
    in_maps = [_core_inputs(c, inp, consts) for c in range(8)]
    if (_nlayers, _raw, _stage) not in _NC_CACHE:
        _NC_CACHE[(_nlayers, _raw, _stage)] = build_program(_nlayers, _raw, _stage)
    nc = _NC_CACHE[(_nlayers, _raw, _stage)]
    res = run_bass_kernel_spmd(nc, in_maps, core_ids=list(range(8)))
    outs = []
    for b in range(4):
        o0 = res.results[2 * b]["out"]
        o1 = res.results[2 * b + 1]["out"]
        outs.append(np.concatenate([o0[:2048], o1[2048:]], 0))
    return np.stack(outs, 0).astype(np.float32)
```

```python
import contextlib
import math
import numpy as np
import ml_dtypes
import concourse.bass as bass
import concourse.mybir as mybir
from concourse.bass_utils import run_bass_kernel_spmd

F32 = mybir.dt.float32
BF16 = mybir.dt.bfloat16
I32 = mybir.dt.int32
AF = mybir.ActivationFunctionType
ALU = mybir.AluOpType

T = 4096
D = 1024
NB = 32
NT = 8
NEG = -30000.0
PAIRS = [[0, 1], [2, 3], [4, 5], [6, 7]]
B_CFG = (1, 4, 16)
LAG = 2


class Tok:
    __slots__ = ("w", "r")

    def __init__(self):
        self.w = None
        self.r = []


class Prog:
    ENG = ("pe", "act", "dve", "pool", "sp")

    def __init__(self, nc, stack, n_dma_sems=20):
        self.nc = nc
        self.q = {e: [] for e in self.ENG}
        self.cnt = {e: 0 for e in self.ENG}
        self.waited = {e: {} for e in self.ENG}
        self.sem = {e: stack.enter_context(nc.semaphore("s_" + e)) for e in self.ENG}
        self.semid = {id(self.sem[e]): e for e in self.ENG}
        self.dsem, self.dval, self.drr = {}, {}, {}
        for qe in ("sp", "pool", "act"):
            self.dsem[qe] = [stack.enter_context(nc.semaphore(f"d_{qe}_{i}")) for i in range(n_dma_sems)]
            self.dval[qe] = [0] * n_dma_sems
            self.drr[qe] = 0
        self.cc_sem = stack.enter_context(nc.semaphore("cc"))
        self.cc_val = 0
        self.n_inst = 0

    def _collect(self, e, reads, writes, extra=()):
        deps = {}

        def add(ev):
            if ev is None:
                return
            s, v = ev
            k = id(s)
            if k not in deps or deps[k][1] < v:
                deps[k] = (s, v)

        for t in reads:
            add(t.w)
        for t in writes:
            add(t.w)
            for ev in t.r:
                add(ev)
        for ev in extra:
            add(ev)
        out = []
        for k, (s, v) in deps.items():
            if self.semid.get(k) == e and e == "pe":
                continue
            if self.waited[e].get(k, 0) >= v:
                continue
            self.waited[e][k] = v
            out.append((s, v))
        return out

    def _mark(self, ev, reads, writes):
        for t in reads:
            t.r.append(ev)
            if len(t.r) > 64:
                t.r = _compress(t.r)
        for t in writes:
            t.w = ev
            t.r = []

    def op(self, e, fn, reads=(), writes=(), extra=()):
        waits = self._collect(e, reads, writes, extra)
        self.cnt[e] += 1
        sem = self.sem[e]
        ev = (sem, self.cnt[e])

        def emit(eng, fn=fn, waits=waits, sem=sem):
            for (s, v) in waits:
                eng.wait_ge(s, v)
            fn(eng).then_inc(sem, 1)

        self.q[e].append(emit)
        self._mark(ev, reads, writes)
        self.n_inst += 1
        return ev

    def dma(self, qe, out, in_, reads=(), writes=(), extra=()):
        k = self.drr[qe]
        self.drr[qe] = (k + 1) % len(self.dsem[qe])
        s = self.dsem[qe][k]
        prev = self.dval[qe][k]
        self.dval[qe][k] = prev + 16
        ev = (s, prev + 16)
        ex = list(extra)
        if prev > 0:
            ex.append((s, prev))
        waits = self._collect(qe, reads, writes, ex)

        def emit(eng, waits=waits, s=s, out=out, in_=in_):
            for (ws, v) in waits:
                eng.wait_ge(ws, v)
            eng.dma_start(out=out, in_=in_).then_inc(s, 16)

        self.q[qe].append(emit)
        self._mark(ev, reads, writes)
        self.n_inst += 1
        return ev

    def collective(self, fn, reads, writes):
        waits = self._collect("pool", reads, writes)
        self.cc_val += 1
        ev = (self.cc_sem, self.cc_val)

        def emit(eng, waits=waits, fn=fn):
            for (s, v) in waits:
                eng.wait_ge(s, v)
            fn(eng).then_inc(self.cc_sem, 1)

        self.q["pool"].append(emit)
        self._mark(ev, reads, writes)
        return ev

    def wait_all(self, e, evs):
        waits = self._collect(e, (), (), evs)

        def emit(eng, waits=waits):
            for (s, v) in waits:
                eng.wait_ge(s, v)

        self.q[e].append(emit)

    def all_events(self):
        evs = [(self.sem[e], self.cnt[e]) for e in self.ENG if self.cnt[e] > 0]
        for qe in self.dsem:
            for s, v in zip(self.dsem[qe], self.dval[qe]):
                if v > 0:
                    evs.append((s, v))
        if self.cc_val > 0:
            evs.append((self.cc_sem, self.cc_val))
        return evs

    def barrier(self):
        evs = self.all_events()
        for e in self.ENG:
            self.wait_all(e, evs)

    def emit_all(self):
        nc = self.nc
        with nc.Block() as block:
            @block.tensor
            def _(eng):
                for f in self.q["pe"]:
                    f(eng)

            @block.scalar
            def _(eng):
                for f in self.q["act"]:
                    f(eng)

            @block.vector
            def _(eng):
                for f in self.q["dve"]:
                    f(eng)

            @block.gpsimd
            def _(eng):
                for f in self.q["pool"]:
                    f(eng)

            @block.sync
            def _(eng):
                for f in self.q["sp"]:
                    f(eng)


def _compress(evs):
    best = {}
    for s, v in evs:
        k = id(s)
        if k not in best or best[k][1] < v:
            best[k] = (s, v)
    return list(best.values())


class Rot:
    def __init__(self, items):
        self.items = [(it, Tok()) for it in items]
        self.i = 0

    def next(self):
        it = self.items[self.i]
        self.i = (self.i + 1) % len(self.items)
        return it


def build_program(nlayers=4, raw=False, stage=99):
    nc = bass.Bass("TRN2", target_bir_lowering=False)

    def din(name, shape, dt):
        return nc.dram_tensor(name, list(shape), dt, kind="ExternalInput").ap()

    x_in = din("x", [T, D], F32)
    pos_in = din("pos", [1, T], I32)
    nw_in = din("nw", [5, D], F32)
    relb_in = din("relb", [33, 8], F32)
    oneh_in = din("oneh", [33, 3 * 383], F32)
    sinks_in = din("sinks", [1, 8], F32)
    ident_in = din("ident", [128, 128], BF16)
    tri_in = din("tri", [128, 128], BF16)
    ropec_in = din("ropec", [128, 2], F32)
    cn_in = din("cn", [1, 2 * 384], F32)
    abw_in = [din(f"abw{i}", [D, 1664], F32) for i in range(2)]
    abwo_in = [din(f"abwo{i}", [D, D], F32) for i in range(2)]
    cw_in = [din(f"cw{i}", [D, 1024], F32) for i in range(2)]
    cqb_in = [din(f"cqb{i}", [256, 1536], F32) for i in range(2)]
    ckvb_in = [din(f"ckvb{i}", [128, 1536], F32) for i in range(2)]
    cwo_in = [din(f"cwo{i}", [D, D], F32) for i in range(2)]
    out = nc.dram_tensor("out", [T, D], F32, kind="ExternalOutput").ap()

    xs = nc.dram_tensor("xs", [T, D], F32).ap()
    go_own = [nc.dram_tensor(f"go_own{t}", [512, 512], BF16) for t in range(NT)]
    go_full = [[nc.dram_tensor(f"go_full{i}_{t}", [1024, 512], BF16) for t in range(NT)] for i in range(2)]
    gd = nc.dram_tensor("gd", [16, 128 * 383], F32)
    vd = nc.dram_tensor("vd", [T, 4 * 65], BF16).ap()
    csd = nc.dram_tensor("csd", [2, 64, T], F32).ap()

    with contextlib.ExitStack() as st:
        def sb(name, shape, dt):
            return st.enter_context(nc.sbuf_tensor(name, list(shape), dt))

        def psum(name, shape, dt):
            return st.enter_context(nc.psum_tensor(name, list(shape), dt))

        pr = Prog(nc, st)

        HT = sb("HT", [128, 8, T], BF16)
        WO = sb("WO", [128, 8, D], BF16)
        QT = sb("QT", [128, T], BF16)
        KT = sb("KT", [128, T], BF16)
        XB = Rot([sb(f"XB{i}", [128, D], F32) for i in range(2)])
        HB = Rot([sb(f"HB{i}", [128, D], BF16) for i in range(2)])
        PT = Rot([sb(f"PT{i}", [128, 512], BF16) for i in range(4)])
        STMP = Rot([sb(f"STMP{i}", [128, 512], F32) for i in range(2)])
        GI = Rot([sb(f"GI{i}", [128, 8, 128], BF16) for i in range(2)])
        GOUT = Rot([sb(f"GOUT{i}", [64, 512], BF16) for i in range(2)])
        WSTG = Rot([sb(f"WSTG{i}", [128, 8, 64], F32) for i in range(3)])
        NW = sb("NW", [128, D], F32)
        ST = Rot([sb(f"ST{i}", [128, 8], F32) for i in range(4)])
        IDENT = sb("IDENT", [128, 128], BF16)
        TRI = sb("TRI", [128, 128], BF16)
        ONES = sb("ONES", [128, 64], F32)
        EPS = sb("EPS", [128, 1], F32)
        ES = sb("ES", [128, 8], F32)
        EPA = Rot([sb(f"EPA{i}", [64, 512], F32) for i in range(2)])
        EPB = Rot([sb(f"EPB{i}", [64, 512], F32) for i in range(2)])
        RD = Rot([sb(f"RD{i}", [65, 512], F32) for i in range(1)])
        ARENA = sb("ARENA", [128, 16384], F32)

        t_HT = [Tok() for _ in range(NB)]
        t_WO, t_QT, t_KT, t_NW = Tok(), Tok(), Tok(), Tok()
        t_const = Tok()
        t_arena = Tok()

        SBK = Rot([psum(f"S{i}", [128, 512], F32) for i in range(2)])
        OBK = Rot([psum(f"O{i}", [128, 512], F32) for i in range(2)])
        PJ = Rot([psum(f"PJ{i}", [128, 512], F32) for i in range(3)])
        TR = psum("TR", [128, 1024], BF16)
        t_TR = Tok()

        def mm(out_, lhsT, rhs, start, stop, reads, writes):
            return pr.op("pe", lambda e: e.matmul(out_, lhsT=lhsT, rhs=rhs, start=start, stop=stop,
                                                  skip_group_check=True), reads=reads, writes=writes)

        def act(out_, in_, func, reads, writes, **kw):
            return pr.op("act", lambda e: e.activation(out=out_, in_=in_, func=func, **kw), reads=reads, writes=writes)

        def copy_any(eng, out_, in_, reads, writes):
            if eng == "act":
                return pr.op("act", lambda e: e.copy(out=out_, in_=in_), reads=reads, writes=writes)
            return pr.op(eng, lambda e: e.tensor_copy(out=out_, in_=in_), reads=reads, writes=writes)

        def tt(eng, out_, in0, in1, op, reads, writes):
            return pr.op(eng, lambda e: e.tensor_tensor(out=out_, in0=in0, in1=in1, op=op), reads=reads, writes=writes)

        def ts(eng, out_, in0, s1, s2, op0, op1, reads, writes):
            if s2 is None:
                return pr.op(eng, lambda e: e.tensor_scalar(out=out_, in0=in0, scalar1=s1, scalar2=None, op0=op0),
                             reads=reads, writes=writes)
            return pr.op(eng, lambda e: e.tensor_scalar(out=out_, in0=in0, scalar1=s1, scalar2=s2, op0=op0, op1=op1),
                         reads=reads, writes=writes)

        def stt(eng, out_, in0, scalar, in1, op0, op1, reads, writes):
            return pr.op(eng, lambda e: e.scalar_tensor_tensor(out=out_, in0=in0, scalar=scalar, in1=in1, op0=op0, op1=op1),
                         reads=reads, writes=writes)

        def recip(out_, in_, reads, writes):
            return pr.op("dve", lambda e: e.reciprocal(out=out_, in_=in_), reads=reads, writes=writes)

        wrr = [0]

        def load_w(dst, src, ncols, t_dst, engs=("pool", "dve", "act")):
            kch = dst.shape[1]
            c0 = 0
            while c0 < ncols:
                cw = min(64, ncols - c0)
                stg, t_stg = WSTG.next()
                pr.dma("sp", stg[:, 0:kch, 0:cw], src[:, c0:c0 + cw].rearrange("(kc p) n -> p kc n", p=128),
                       writes=[t_stg])
                copy_any(engs[wrr[0] % len(engs)], dst[:, :, c0:c0 + cw], stg[:, 0:kch, 0:cw], [t_stg], [t_dst])
                wrr[0] += 1
                c0 += cw

        pr.dma("sp", IDENT[:], ident_in[:, :], writes=[t_const])
        pr.dma("sp", TRI[:], tri_in[:, :], writes=[t_const])
        pr.op("pool", lambda e: e.memset(ONES[:], 1.0), writes=[t_const])
        pr.op("pool", lambda e: e.memset(EPS[:], 1e-6), writes=[t_const])
        pr.dma("sp", ES[:], sinks_in[0:1, :].partition_broadcast(128), writes=[t_const])
        act(ES[:], ES[:], AF.Exp, [t_const], [t_const])

        def build_tables():
            A32 = ARENA
            RELB = A32[0:33, 0:8]
            ONEH = A32[0:33, 8:8 + 3 * 383]
            LB = A32[0:33, 1200:1328]
            GSB = [A32[:, 1400:1783], A32[:, 1800:2183]]
            t_gsb = [Tok(), Tok()]
            t_gd = Tok()
            t_lb = Tok()
            pr.dma("sp", RELB, relb_in[:, :], writes=[t_arena])
            pr.dma("sp", ONEH, oneh_in[:, :], writes=[t_arena])
            for hc in range(16):
                col = hc if hc < 4 else 4 + (hc - 4) % 4
                cfg = 0 if hc < 8 else (1 if hc < 12 else 2)
                copy_any("dve", LB, RELB[:, col:col + 1].to_broadcast([33, 128]), [t_arena], [t_lb])
                pj, t_pj = PJ.next()
                mm(pj[:, 0:383], LB, ONEH[:, cfg * 383:(cfg + 1) * 383], True, True, [t_lb, t_arena], [t_pj])
                copy_any("dve", GSB[hc % 2], pj[:, 0:383], [t_pj], [t_gsb[hc % 2]])
                pr.dma("sp", gd[hc:hc + 1, :].rearrange("o (p n) -> (o p) n", p=128), GSB[hc % 2], reads=[t_gsb[hc % 2]], writes=[t_gd])

            def skew_ap(hc, prev):
                return bass.AP(tensor=gd.ap().tensor, offset=hc * 128 * 383 + (255 if prev else 127), ap=[[382, 128], [1, 128]])

            ROPEC = A32[:, 2200:2202]
            pr.dma("sp", ROPEC, ropec_in[:, :], writes=[t_arena])
            t_cs = Tok()
            RC = 1024
            r_pi = A32[0:64, 4096:4096 + RC].bitcast(I32)
            r_a = A32[0:64, 5120:5120 + RC]
            r_b = A32[0:64, 6144:6144 + RC]
            r_ti = A32[0:64, 7168:7168 + RC].bitcast(I32)
            r_tf = A32[0:64, 8192:8192 + RC]
            r_r = A32[0:64, 9216:9216 + RC]
            r_m = A32[0:64, 10240:10240 + RC]
            r_o = [A32[0:64, 11264:11264 + RC], A32[0:64, 12288:12288 + RC]]
            tr_ = Tok()
            t_ro = [Tok(), Tok()]
            C1 = 6.28125
            C2 = 2.0 * math.pi - 6.28125
            for ch in range(T // RC):
                pr.dma("sp", r_pi, pos_in[0:1, ch * RC:(ch + 1) * RC].partition_broadcast(64), writes=[tr_])
                copy_any("dve", r_a, r_pi, [tr_], [tr_])
                ts("dve", r_a, r_a, ROPEC[0:64, 0:1], None, ALU.mult, None, [tr_, t_arena], [tr_])
                for which in (0, 1):
                    shift = math.pi / 2 if which == 0 else 0.0
                    ts("dve", r_b, r_a, shift, 1.0 / (2 * math.pi), ALU.add, ALU.mult, [tr_], [tr_])
                    copy_any("dve", r_ti, r_b, [tr_], [tr_])
                    copy_any("dve", r_tf, r_ti, [tr_], [tr_])
                    ts("dve", r_b, r_a, shift, None, ALU.add, None, [tr_], [tr_])
                    stt("dve", r_r, r_tf, -C1, r_b, ALU.mult, ALU.add, [tr_], [tr_])
                    stt("dve", r_r, r_tf, -C2, r_r, ALU.mult, ALU.add, [tr_], [tr_])
                    ts("dve", r_m, r_r, math.pi, None, ALU.is_gt, None, [tr_], [tr_])
                    stt("dve", r_r, r_m, -2 * math.pi, r_r, ALU.mult, ALU.add, [tr_], [tr_])
                    ts("dve", r_m, r_r, -math.pi, None, ALU.is_lt, None, [tr_], [tr_])
                    stt("dve", r_r, r_m, 2 * math.pi, r_r, ALU.mult, ALU.add, [tr_], [tr_])
                    ts("dve", r_r, r_r, -3.14159, 3.14159, ALU.max, ALU.min, [tr_], [tr_])
                    ro, tro = r_o[which], t_ro[which]
                    act(ro, r_r, AF.Sin, [tr_], [tro])
                    if which == 1:
                        ts("dve", ro, ro, ROPEC[0:64, 1:2], None, ALU.mult, None, [tro, t_arena], [tro])
                    pr.dma("sp", csd[which, :, ch * RC:(ch + 1) * RC], ro, reads=[tro], writes=[t_cs])
            return skew_ap, t_gd, t_cs

        def norm_block(tb, xb, t_xb, final=False):
            stt_, t_st = ST.next()
            hb, t_hb = HB.next()
            act(hb[:], xb[:], AF.Square, [t_xb], [t_hb, t_st], accum_out=stt_[:, 0:1])
            act(stt_[:, 1:2], stt_[:, 0:1], AF.Ln, [t_st, t_const], [t_st], scale=1.0 / D, bias=EPS[:, 0:1])
            act(stt_[:, 2:3], stt_[:, 1:2], AF.Exp, [t_st], [t_st], scale=-0.5)
            if final:
                if not raw:
                    stt("dve", xb[:], xb[:], stt_[:, 2:3], NW[:], ALU.mult, ALU.mult, [t_xb, t_st, t_NW], [t_xb])
                return pr.dma("pool", out[tb * 128:(tb + 1) * 128, :], xb[:], reads=[t_xb])
            stt("dve", hb[:], xb[:], stt_[:, 2:3], NW[:], ALU.mult, ALU.mult, [t_xb, t_st, t_NW], [t_hb])
            for kc in range(8):
                pr.op("pe", lambda e, kc=kc: e.transpose(out=TR[:, kc * 128:(kc + 1) * 128], in_=hb[:, kc * 128:(kc + 1) * 128],
                                                         identity=IDENT[:]),
                      reads=[t_hb, t_const], writes=[t_TR])
            copy_any("act", HT[:, :, tb * 128:(tb + 1) * 128], TR[:].rearrange("p (k t) -> p k t", k=8), [t_TR], [t_HT[tb]])
            return None

        def load_nw(row):
            pr.dma("sp", NW[:], nw_in[row:row + 1, :].partition_broadcast(128), writes=[t_NW])

        def epilogue(num, den, src_reads, ti, wz_lhsT, wz_tok, row0, sink_col=None):
            pj, t_pj = PJ.next()
            for kc in range(8):
                mm(pj[0:64, :], wz_lhsT(kc), HT[:, kc, ti * 512:(ti + 1) * 512], kc == 0, kc == 7,
                   [wz_tok] + t_HT[ti * 4:ti * 4 + 4], [t_pj])
            ea, t_ea = EPA.next()
            eb, t_eb = EPB.next()
            rd, t_rd = RD.next()
            act(ea[:], pj[0:64, :], AF.Exp, [t_pj], [t_ea], scale=-1.0)
            if sink_col is not None:
                ts("dve", rd[64:65, :], den, ES[64:65, sink_col:sink_col + 1], None, ALU.add, None, src_reads + [t_const], [t_rd])
            else:
                copy_any("dve", rd[64:65, :], den, src_reads, [t_rd])
            bc, t_bc = PJ.next()
            mm(bc[0:64, :], ONES[64:65, 0:64], rd[64:65, :], True, True, [t_rd, t_const], [t_bc])
            stt("dve", eb[:], ea[:], 1.0, bc[0:64, :], ALU.add, ALU.mult, [t_ea, t_bc], [t_eb])
            act(eb[:], eb[:], AF.Ln, [t_eb], [t_eb])
            act(eb[:], eb[:], AF.Exp, [t_eb], [t_eb], scale=-1.0)
            tt("dve", ea[:], pj[0:64, :], eb[:], ALU.mult, [t_pj, t_eb, t_ea], [t_ea])
            go, t_go = GOUT.next()
            tt("dve", go[:], num, ea[:], ALU.mult, src_reads + [t_ea], [t_go])
            pr.dma("pool", go_own[ti][row0:row0 + 64, :], go[:], reads=[t_go], writes=[t_go_own[ti]])

        t_go_own = [Tok() for _ in range(NT)]
        t_go_full = [[Tok() for _ in range(NT)] for _ in range(2)]
        t_xs = [Tok() for _ in range(NB)]

        def ab_layer(li):
            i = li // 2
            AB16 = ARENA.bitcast(BF16)
            o = 0
            WH = AB16[:, o:o + 8 * 1664].rearrange("p (k n) -> p k n", k=8); o += 8 * 1664
            VA = AB16[:, o:o + 32 * 65].rearrange("p (m e) -> p m e", m=32); o += 32 * 65
            VS = AB16[:, o:o + 3 * 32 * 65].rearrange("p (c m e) -> p c m e", c=3, m=32); o += 3 * 32 * 65
            VST = [AB16[:, o + k * 260:o + (k + 1) * 260].rearrange("p (h e) -> p h e", h=4) for k in range(2)]; o += 520
            o32 = (o + 1) // 2
            U = ARENA[0:65, o32:o32 + T]; o32 += T
            BT = ARENA[:, o32:o32 + 3 * 256].rearrange("p (c n) -> p c n", c=3); o32 += 768
            assert o32 <= 16384, o32
            vst = Rot(VST)
            t_WH, t_VN, t_VS, t_U, t_BT, t_vd = Tok(), Tok(), Tok(), Tok(), Tok(), Tok()

            load_w(WH, abw_in[i], 1664, t_WH)
            pr.op("pool", lambda e: e.memset(VA[:, :, 64:65], 1.0), writes=[t_VN])
            pr.op("pool", lambda e: e.memset(VS[:, :, :, 64:65], 1.0), writes=[t_VS])
            for (v_, tv_) in vst.items:
                pr.op("pool", lambda e, v_=v_: e.memset(v_[:, :, 64:65], 1.0), writes=[tv_])

            for blk in range(NB):
                pj, t_pj = PJ.next()
                for kc in range(8):
                    mm(pj[:, 0:320], HT[:, kc, blk * 128:(blk + 1) * 128], WH[:, kc, 1344:1664], kc == 0, kc == 7,
                       [t_HT[blk], t_WH], [t_pj])
                copy_any("dve", VA[:, blk, 0:64], pj[:, 0:64], [t_pj], [t_VN])
                v_, tv_ = vst.next()
                copy_any("dve", v_[:, :, 0:64], pj[:, 64:320].rearrange("p (h e) -> p h e", h=4), [t_pj], [tv_])
                pr.dma("pool", vd[blk * 128:(blk + 1) * 128, :], v_, reads=[tv_], writes=[t_vd])

            if stage < 3:
                return
            def proj_fm(dst, col0, alt):
                for ti in range(NT):
                    pj, t_pj = PJ.next()
                    for kc in range(8):
                        mm(pj[0:64, :], WH[:, kc, col0:col0 + 64], HT[:, kc, ti * 512:(ti + 1) * 512], kc == 0, kc == 7,
                           [t_WH] + t_HT[ti * 4:ti * 4 + 4], [t_pj])
                    copy_any("act" if (ti + alt) % 2 == 0 else "dve", dst[0:64, ti * 512:(ti + 1) * 512], pj[0:64, :], [t_pj],
                             [t_QT if dst is QT else t_KT])

            def banded(d, vget, bt, consume):
                L = T // d
                tl = min(512, L)
                steps = []
                for r in range(d):
                    for t0 in range(0, L, tl):
                        m0 = t0 // 128
                        nb_t = tl // 128
                        groups = []
                        mm_ = m0
                        while mm_ < m0 + nb_t:
                            if mm_ == 0 or mm_ == m0 + nb_t - 1:
                                groups.append([mm_]); mm_ += 1
                            else:
                                groups.append([mm_, mm_ + 1]); mm_ += 2
                        for gx, g in enumerate(groups):
                            steps.append(dict(r=r, t0=t0, m0=m0, g=g, firstg=(gx == 0), lastg=(gx == len(groups) - 1)))

                def front(stp):
                    r, g = stp["r"], stp["g"]
                    sbk, t_s = SBK.next()
                    stmp, t_stmp = STMP.next()
                    pt, t_pt = PT.next()
                    stp["pt"] = (pt, t_pt)
                    ng = len(g)
                    first = (g[0] == 0)
                    for gi, m in enumerate(g):
                        qa = QT[0:64, r + d * 128 * m: r + d * 128 * m + d * 127 + 1: d]
                        for half in (0, 1):
                            if m == 0 and half == 0:
                                continue
                            kb = m - 1 + half
                            ka = KT[0:64, r + d * 128 * kb: r + d * 128 * kb + d * 127 + 1: d]
                            mm(sbk[:, gi * 256 + half * 128: gi * 256 + half * 128 + 128], ka, qa, True, True,
                               [t_QT, t_KT], [t_s])
                    c0 = 128 if first else 0
                    c1 = ng * 256
                    if first:
                        stt("dve", stmp[:, c0:c1], sbk[:, c0:c1], 0.125, bt[:, 128:256], ALU.mult, ALU.add,
                            [t_s, t_BT], [t_stmp])
                    else:
                        stt("dve", stmp[:, c0:c1].rearrange("p (g n) -> p g n", g=ng),
                            sbk[:, c0:c1].rearrange("p (g n) -> p g n", g=ng), 0.125,
                            bt.unsqueeze(1).to_broadcast([128, ng, 256]), ALU.mult, ALU.add, [t_s, t_BT], [t_stmp])
                    act(pt[:, c0:c1], stmp[:, c0:c1], AF.Exp, [t_stmp], [t_pt])

                cur = [None]

                def back(stp):
                    r, g, m0 = stp["r"], stp["g"], stp["m0"]
                    pt, t_pt = stp["pt"]
                    if stp["firstg"]:
                        cur[0] = OBK.next()
                    ob, t_ob = cur[0]
                    for gi, m in enumerate(g):
                        oc = (m - m0) * 128
                        for half in (0, 1):
                            if m == 0 and half == 0:
                                continue
                            kb = m - 1 + half
                            mm(ob[0:65, oc:oc + 128], vget(r, kb), pt[:, gi * 256 + half * 128: gi * 256 + half * 128 + 128],
                               (half == 0) or (m == 0), half == 1, [t_pt, t_VN, t_VS], [t_ob])
                    if stp["lastg"]:
                        consume(ob, t_ob, r + d * stp["t0"], tl, d)

                for ix in range(len(steps) + LAG):
                    if ix < len(steps):
                        front(steps[ix])
                    if ix - LAG >= 0:
                        back(steps[ix - LAG])

            def load_bt(hcs):
                for ci, hc in enumerate(hcs):
                    pr.dma("sp", BT[:, ci, 0:128], skew_ap(hc, True), reads=[t_gd], writes=[t_BT])
                    pr.dma("sp", BT[:, ci, 128:256], skew_ap(hc, False), reads=[t_gd], writes=[t_BT])

            proj_fm(KT, 512, 1)
            for j in range(4):
                proj_fm(QT, j * 128, 0)
                load_bt([j])

                def consume_a(ob, t_ob, tok0, ntok, stride, j=j):
                    ti = tok0 // 512
                    epilogue(ob[0:64, :], ob[64:65, :], [t_ob], ti, lambda kc, j=j: WH[:, kc, j * 128 + 64:j * 128 + 128], t_WH,
                             j * 64, sink_col=i * 4 + j)
                banded(1, lambda r, kb: VA[:, kb, :], BT[:, 0, :], consume_a)
                load_w(WO[:, :, j * 128:(j + 1) * 128], abwo_in[i][:, j * 128:(j + 1) * 128], 128, t_WO, engs=("pool",))

            if stage < 4:
                return
            for j in range(4):
                cb = 576 + j * 192
                proj_fm(QT, cb, 0)
                proj_fm(KT, cb + 64, 1)
                load_bt([4 + j, 8 + j, 12 + j])
                for ci, d in enumerate(B_CFG):
                    src = bass.AP(tensor=vd.tensor, offset=j * 65,
                                  ap=[[d * 260, 128], [260, d], [128 * d * 260, 32 // d], [1, 65]])
                    pr.dma("sp", VS[:, ci, :, :].rearrange("p (r m) e -> p r m e", r=d), src, reads=[t_vd], writes=[t_VS])
                for ci, d in enumerate(B_CFG):
                    def consume_b(ob, t_ob, tok0, ntok, stride, ci=ci):
                        ua = U[:, tok0: tok0 + stride * (ntok - 1) + 1: stride]
                        if ci == 0:
                            copy_any("dve", ua, ob[0:65, 0:ntok], [t_ob], [t_U])
                        else:
                            tt("dve", ua, ua, ob[0:65, 0:ntok], ALU.add, [t_ob, t_U], [t_U])
                    vget = lambda r, kb, d=d, ci=ci: VS[:, ci, r * (32 // d) + kb, :]
                    banded(d, vget, BT[:, ci, :], consume_b)
                for ti in range(NT):
                    epilogue(U[0:64, ti * 512:(ti + 1) * 512], U[64:65, ti * 512:(ti + 1) * 512], [t_U], ti,
                             lambda kc, cb=cb: WH[:, kc, cb + 128:cb + 192], t_WH, 256 + j * 64)
                load_w(WO[:, :, (4 + j) * 128:(5 + j) * 128], abwo_in[i][:, (4 + j) * 128:(5 + j) * 128], 128, t_WO, engs=("pool",))

        def c_layer(li):
            i = li // 2
            C16 = ARENA.bitcast(BF16)
            o = 0
            WC = C16[:, o:o + 8 * 1024].rearrange("p (k n) -> p k n", k=8); o += 8 * 1024
            WQB = C16[:, o:o + 2 * 1536].rearrange("p (k n) -> p k n", k=2); o += 2 * 1536
            WKVB = C16[:, o:o + 1536].rearrange("p (k n) -> p k n", k=1); o += 1536
            LAT = C16[:, o:o + 3 * T].rearrange("p (c t) -> p c t", c=3); o += 3 * T
            VM = C16[:, o:o + 32 * 65].rearrange("p (m e) -> p m e", m=32); o += 32 * 65
            CNB = C16[:, o:o + 384]; o += 384
            o32 = (o + 1) // 2
            CS = [ARENA[0:64, o32 + k * 512: o32 + (k + 1) * 512] for k in range(4)]; o32 += 2048
            CNW = ARENA[:, o32:o32 + 384]; o32 += 384
            assert o32 <= 16384, o32
            t_WC, t_WQB, t_WKVB, t_LAT, t_KPE, t_VM, t_CNB, t_CNW = (Tok() for _ in range(8))
            csr = Rot([(CS[0], CS[1]), (CS[2], CS[3])])

            load_w(WC, cw_in[i], 1024, t_WC)
            load_w(WQB, cqb_in[i], 1536, t_WQB)
            load_w(WKVB, ckvb_in[i], 1536, t_WKVB)
            pr.dma("sp", CNW, cn_in[0:1, i * 384:(i + 1) * 384].partition_broadcast(128), writes=[t_CNW])
            pr.op("pool", lambda e: e.memset(VM[:, :, 64:65], 1.0), writes=[t_VM])

            for tb in range(NB):
                pj, t_pj = PJ.next()
                for kc in range(8):
                    mm(pj[:, 0:384], HT[:, kc, tb * 128:(tb + 1) * 128], WC[:, kc, 0:384], kc == 0, kc == 7,
                       [t_HT[tb], t_WC], [t_pj])
                stt_, t_st = ST.next()
                stmp, t_stmp = STMP.next()
                act(stmp[:, 0:256], pj[:, 0:256], AF.Square, [t_pj], [t_stmp, t_st], accum_out=stt_[:, 0:1])
                act(stmp[:, 256:384], pj[:, 256:384], AF.Square, [t_pj], [t_stmp, t_st], accum_out=stt_[:, 1:2])
                act(stt_[:, 2:3], stt_[:, 0:1], AF.Ln, [t_st, t_const], [t_st], scale=1.0 / 256, bias=EPS[:, 0:1])
                act(stt_[:, 3:4], stt_[:, 1:2], AF.Ln, [t_st, t_const], [t_st], scale=1.0 / 128, bias=EPS[:, 0:1])
                act(stt_[:, 4:6], stt_[:, 2:4], AF.Exp, [t_st], [t_st], scale=-0.5)
                stt("dve", CNB[:, 0:256], pj[:, 0:256], stt_[:, 4:5], CNW[:, 0:256], ALU.mult, ALU.mult,
                    [t_pj, t_st, t_CNW], [t_CNB])
                stt("dve", CNB[:, 256:384], pj[:, 256:384], stt_[:, 5:6], CNW[:, 256:384], ALU.mult, ALU.mult,
                    [t_pj, t_st, t_CNW], [t_CNB])
                for c3 in range(3):
                    pr.op("pe", lambda e, c3=c3: e.transpose(out=TR[:, c3 * 128:(c3 + 1) * 128], in_=CNB[:, c3 * 128:(c3 + 1) * 128],
                                                             identity=IDENT[:]),
                          reads=[t_CNB, t_const], writes=[t_TR])
                copy_any("act", LAT[:, :, tb * 128:(tb + 1) * 128], TR[:, 0:384].rearrange("p (c t) -> p c t", c=3), [t_TR], [t_LAT])

            def rope_tile(ti, pj_main, t_main, pj_sw, t_sw, dst, t_dst):
                (cc, ss), t_cs_t = csr.next()
                pr.dma("sp", cc, csd[0, :, ti * 512:(ti + 1) * 512], reads=[t_cs], writes=[t_cs_t])
                pr.dma("sp", ss, csd[1, :, ti * 512:(ti + 1) * 512], reads=[t_cs], writes=[t_cs_t])
                ea, t_ea = EPA.next()
                eb, t_eb = EPB.next()
                tt("dve", ea[:], pj_main[0:64, :], cc, ALU.mult, [t_main, t_cs_t], [t_ea])
                tt("dve", eb[:], pj_sw[0:64, :], ss, ALU.mult, [t_sw, t_cs_t], [t_eb])
                tt("pool", dst[0:64, ti * 512:(ti + 1) * 512], ea[:], eb[:], ALU.add, [t_ea, t_eb], [t_dst])

            for ti in range(NT):
                pa, t_pa = PJ.next()
                pb, t_pb = PJ.next()
                for kc in range(8):
                    mm(pa[0:64, :], WC[:, kc, 384:448], HT[:, kc, ti * 512:(ti + 1) * 512], kc == 0, kc == 7,
                       [t_WC] + t_HT[ti * 4:ti * 4 + 4], [t_pa])
                for kc in range(8):
                    mm(pb[0:64, :], WC[:, kc, 448:512], HT[:, kc, ti * 512:(ti + 1) * 512], kc == 0, kc == 7,
                       [t_WC] + t_HT[ti * 4:ti * 4 + 4], [t_pb])
                rope_tile(ti, pa, t_pa, pb, t_pb, KT, t_KT)

            scale = (64 + 32) ** -0.5
            for h in range(8):
                for ti in range(NT):
                    pj, t_pj = PJ.next()
                    mm(pj[:, :], WKVB[:, 0, h * 128:(h + 1) * 128], LAT[:, 2, ti * 512:(ti + 1) * 512], True, True,
                       [t_WKVB, t_LAT], [t_pj])
                    copy_any("act", KT[64:128, ti * 512:(ti + 1) * 512], pj[64:128, :], [t_pj], [t_KT])
                for b0 in range(0, NB, 8):
                    pj, t_pj = PJ.next()
                    for bb in range(8):
                        mm(pj[:, bb * 64:(bb + 1) * 64], LAT[:, 2, (b0 + bb) * 128:(b0 + bb + 1) * 128],
                           WKVB[:, 0, 1024 + h * 64:1024 + (h + 1) * 64], True, True, [t_WKVB, t_LAT], [t_pj])
                    copy_any("dve", VM[:, b0:b0 + 8, 0:64], pj[:, :].rearrange("p (m e) -> p m e", m=8), [t_pj], [t_VM])
                for ti in range(NT):
                    pa, t_pa = PJ.next()
                    pb, t_pb = PJ.next()
                    for c2 in range(2):
                        mm(pa[:, :], WQB[:, c2, h * 128:(h + 1) * 128], LAT[:, c2, ti * 512:(ti + 1) * 512], c2 == 0, c2 == 1,
                           [t_WQB, t_LAT], [t_pa])
                    for c2 in range(2):
                        mm(pb[0:64, :], WQB[:, c2, 1024 + h * 64:1024 + (h + 1) * 64], LAT[:, c2, ti * 512:(ti + 1) * 512],
                           c2 == 0, c2 == 1, [t_WQB, t_LAT], [t_pb])
                    copy_any("act", QT[64:128, ti * 512:(ti + 1) * 512], pa[64:128, :], [t_pa], [t_QT])
                    rope_tile(ti, pa, t_pa, pb, t_pb, QT, t_QT)
                steps = []
                for j in range(NT):
                    nkb = 4 * j + 4
                    for kb in range(nkb):
                        steps.append(dict(j=j, kb=kb, nkb=nkb))

                def front(stp):
                    j, kb = stp["j"], stp["kb"]
                    c0 = max(0, kb - 4 * j) * 128
                    sbk, t_s = SBK.next()
                    pt, t_pt = PT.next()
                    stp["pt"] = (pt, t_pt)
                    mm(sbk[:, c0:512], KT[:, kb * 128:(kb + 1) * 128], QT[:, j * 512 + c0:(j + 1) * 512], True, True,
                       [t_QT, t_KT], [t_s])
                    act(pt[:, c0:512], sbk[:, c0:512], AF.Exp, [t_s], [t_pt], scale=scale)
                    if kb >= 4 * j:
                        tt("pool", pt[:, c0:c0 + 128], pt[:, c0:c0 + 128], TRI[:], ALU.mult, [t_pt, t_const], [t_pt])

                cur = [None]

                def back(stp, h=h):
                    j, kb, nkb = stp["j"], stp["kb"], stp["nkb"]
                    c0 = max(0, kb - 4 * j) * 128
                    pt, t_pt = stp["pt"]
                    if kb == 0:
                        cur[0] = OBK.next()
                    ob, t_ob = cur[0]
                    mm(ob[0:65, c0:512], VM[:, kb, :], pt[:, c0:512], kb == 0, kb == nkb - 1, [t_pt, t_VM], [t_ob])
                    if kb == nkb - 1:
                        epilogue(ob[0:64, :], ob[64:65, :], [t_ob], j, lambda kc, h=h: WC[:, kc, 512 + h * 64:512 + (h + 1) * 64],
                                 t_WC, h * 64)

                for ix in range(len(steps) + LAG):
                    if ix < len(steps):
                        front(steps[ix])
                    if ix - LAG >= 0:
                        back(steps[ix - LAG])
                load_w(WO[:, :, h * 128:(h + 1) * 128], cwo_in[i][:, h * 128:(h + 1) * 128], 128, t_WO, engs=("pool",))

        def phase_o(li, last):
            gf = go_full[li % 2]
            tgf = t_go_full[li % 2]
            evs = []
            banks = Rot([None])
            banks.items = PJ.items + SBK.items + OBK.items
            blk = {}

            def front(tb):
                gi, t_gi = GI.next()
                pr.dma("sp", gi[:], gf[tb // 4][:, (tb % 4) * 128:(tb % 4 + 1) * 128].rearrange("(kc p) t -> p kc t", p=128),
                       reads=[tgf[tb // 4]], writes=[t_gi])
                xb, t_xb = XB.next()
                if li == 0:
                    pr.dma("sp", xb[:], x_in[tb * 128:(tb + 1) * 128, :], writes=[t_xb])
                else:
                    pr.dma("sp", xb[:], xs[tb * 128:(tb + 1) * 128, :], reads=[t_xs[tb]], writes=[t_xb])
                for half in range(2):
                    pj, t_pj = banks.next()
                    for kc in range(8):
                        mm(pj[:, :], gi[:, kc, :], WO[:, kc, half * 512:(half + 1) * 512], kc == 0, kc == 7, [t_gi, t_WO], [t_pj])
                    tt("dve", xb[:, half * 512:(half + 1) * 512], xb[:, half * 512:(half + 1) * 512], pj[:, :], ALU.add,
                       [t_xb, t_pj], [t_xb])
                if not last:
                    pr.dma("pool", xs[tb * 128:(tb + 1) * 128, :], xb[:], reads=[t_xb], writes=[t_xs[tb]])
                blk[tb] = (xb, t_xb)

            def back(tb):
                xb, t_xb = blk.pop(tb)
                ev = norm_block(tb, xb, t_xb, final=last)
                if ev is not None:
                    evs.append(ev)

            for ix in range(NB + 1):
                if ix < NB:
                    front(ix)
                if ix >= 1:
                    back(ix - 1)
            return evs

        load_nw(0)
        for tb in range(NB if stage >= 1 else 0):
            xb, t_xb = XB.next()
            pr.dma("sp", xb[:], x_in[tb * 128:(tb + 1) * 128, :], writes=[t_xb])
            norm_block(tb, xb, t_xb)
        skew_ap, t_gd, t_cs = build_tables()
        final_evs = []
        for li in range(nlayers if stage >= 2 else 0):
            pr.barrier()
            if li % 2 == 0:
                ab_layer(li)
            else:
                c_layer(li)
            last = (li == nlayers - 1)
            if stage < 5:
                break
            gf = go_full[li % 2]
            for ti in range(NT):
                pr.collective(lambda e, gf=gf, ti=ti: e.collective_compute("AllGather", ALU.bypass, replica_groups=PAIRS,
                                                                           ins=[go_own[ti].ap().opt()], outs=[gf[ti].ap().opt()]),
                              reads=[t_go_own[ti]], writes=[t_go_full[li % 2][ti]])
            if stage < 6:
                break
            load_nw(4 if (last and nlayers == 4) else li + 1)
            final_evs = phase_o(li, last)
        pr.wait_all("pool", final_evs)
        pr.barrier()
        pr.emit_all()
        print("instructions:", pr.n_inst, {e: pr.cnt[e] for e in pr.ENG}, "sbuf left", nc.sbuf_bytes_remaining)
    return nc


def _t5_bucket(dist):
    dist = np.asarray(dist, dtype=np.int64)
    d = np.maximum(dist, 1).astype(np.float32)
    large = 16 + (np.log(d / np.float32(16)) / np.float32(math.log(2048 / 16)) * np.float32(16)).astype(np.int32)
    large = np.minimum(large, 31)
    return np.where(dist < 16, dist, large)


def _consts():
    oneh = np.zeros((33, 3, 383), np.float32)
    for ci, d in enumerate(B_CFG):
        for n in range(383):
            dist = n - 127
            if 0 <= dist <= 128:
                oneh[_t5_bucket(dist * d), ci, n] = 1.0
            else:
                oneh[32, ci, n] = NEG
    kk = np.arange(128)[:, None]
    qq = np.arange(128)[None, :]
    tri = (qq >= kk).astype(np.float32).astype(ml_dtypes.bfloat16)
    ident = np.eye(128, dtype=np.float32).astype(ml_dtypes.bfloat16)
    inv_freq = (10000.0 ** (-np.arange(0, 32, 2, dtype=np.float32) / np.float32(32))).astype(np.float32)
    ropec = np.zeros((128, 2), np.float32)
    ropec[0:16, 0] = inv_freq
    ropec[32:48, 0] = inv_freq
    ropec[0:16, 1] = -1.0
    ropec[32:48, 1] = 1.0
    return oneh.reshape(33, 3 * 383), tri, ident, ropec


def _core_inputs(c, inp, consts):
    b, hh = c // 2, c % 2
    oneh, tri, ident, ropec = consts
    f = np.float32
    m = {}
    m["x"] = np.ascontiguousarray(inp["x"][b], dtype=f)
    m["pos"] = np.ascontiguousarray(inp["positions"][b][None, :], dtype=np.int32)
    m["nw"] = np.ascontiguousarray(np.concatenate([inp["norm_w"], inp["final_norm"][None, :]], 0), dtype=f)
    rb = inp["rel_bias"]
    relb = np.ones((33, 8), f)
    relb[:32, 0:4] = rb[:, hh * 4:hh * 4 + 4]
    relb[:32, 4:8] = rb[:, 8 + hh * 4:8 + hh * 4 + 4]
    m["relb"] = relb
    m["oneh"] = oneh
    m["sinks"] = np.ascontiguousarray(inp["ab_sinks"][:, hh * 4:hh * 4 + 4].reshape(1, 8), dtype=f)
    m["ident"] = ident
    m["tri"] = tri
    m["ropec"] = ropec
    m["cn"] = np.ascontiguousarray(np.concatenate(
        [np.concatenate([inp["c_q_norm"][i], inp["c_kv_norm"][i]]) for i in range(2)])[None, :], dtype=f)
    for i in range(2):
        w = inp["ab_w_in"][i]
        qa, ka, va, qb, kb, vb, z = np.split(w, np.cumsum([512, 128, 128, 512, 512, 512]), axis=1)
        cols = []
        for j in range(4):
            h = hh * 4 + j
            cols += [qa[:, h * 64:(h + 1) * 64], z[:, h * 64:(h + 1) * 64]]
        cols.append(ka[:, hh * 64:(hh + 1) * 64])
        for j in range(4):
            h = hh * 4 + j
            cols += [qb[:, h * 64:(h + 1) * 64], kb[:, h * 64:(h + 1) * 64], z[:, 512 + h * 64:512 + (h + 1) * 64]]
        cols.append(va[:, hh * 64:(hh + 1) * 64])
        cols.append(vb[:, hh * 256:(hh + 1) * 256])
        m[f"abw{i}"] = np.ascontiguousarray(np.concatenate(cols, 1), dtype=f)
        assert m[f"abw{i}"].shape == (1024, 1664)
        wo = inp["ab_w_out"][i]
        m[f"abwo{i}"] = np.ascontiguousarray(np.concatenate([wo[0:256], wo[512:768], wo[256:512], wo[768:1024]], 0), dtype=f)
        w = inp["c_w_in"][i]
        kpe = w[:, 384:416]
        z16 = np.zeros((1024, 16), f)
        kp_pad = np.concatenate([kpe[:, 0:16], z16, kpe[:, 16:32], z16], 1)
        kp_sw = np.concatenate([kpe[:, 16:32], z16, kpe[:, 0:16], z16], 1)
        m[f"cw{i}"] = np.ascontiguousarray(np.concatenate([w[:, 0:384], kp_pad, kp_sw, w[:, 416 + hh * 512:416 + (hh + 1) * 512]], 1), dtype=f)
        wq = inp["c_w_qb"][i].reshape(256, 16, 96)
        z16q = np.zeros((256, 16), f)
        pads, sws = [], []
        for j in range(8):
            h = hh * 8 + j
            nope, pe = wq[:, h, 0:64], wq[:, h, 64:96]
            pads.append(np.concatenate([pe[:, 0:16], z16q, pe[:, 16:32], z16q, nope], 1))
            sws.append(np.concatenate([pe[:, 16:32], z16q, pe[:, 0:16], z16q], 1))
        m[f"cqb{i}"] = np.ascontiguousarray(np.concatenate(pads + sws, 1), dtype=f)
        wkv = inp["c_w_kvb"][i].reshape(128, 16, 128)
        z64 = np.zeros((128, 64), f)
        ks, vs = [], []
        for j in range(8):
            h = hh * 8 + j
            ks.append(np.concatenate([z64, wkv[:, h, 0:64]], 1))
            vs.append(wkv[:, h, 64:128])
        m[f"ckvb{i}"] = np.ascontiguousarray(np.concatenate(ks + vs, 1), dtype=f)
        m[f"cwo{i}"] = np.ascontiguousarray(inp["c_w_out"][i], dtype=f)
    return m


_NC_CACHE = {}


def kernel(x, positions, norm_w, rel_bias, ab_w_in, ab_sinks, ab_w_out, c_w_in, c_q_norm, c_w_qb,
           c_kv_norm, c_w_kvb, c_w_out, final_norm, _nlayers=4, _raw=False, _stage=99):
    inp = dict(x=x, positions=positions, norm_w=norm_w, rel_bias=rel_bias, ab_w_in=ab_w_in, ab_sinks=ab_sinks,
               ab_w_out=ab_w_out, c_w_in=c_w_in, c_q_norm=c_q_norm, c_w_qb=c_w_qb, c_kv_norm=c_kv_norm,
               c_w_kvb=c_w_kvb, c_w_out=c_w_out, final_norm=final_norm)
    inp = {k: np.asarray(v) for k, v in inp.items()}
    consts = _consts()
    in_maps = [_core_inputs(c, inp, consts) for c in range(8)]
    if (_nlayers, _raw, _stage) not in _NC_CACHE:
        _NC_CACHE[(_nlayers, _raw, _stage)] = build_program(_nlayers, _raw, _stage)
    nc = _NC_CACHE[(_nlayers, _raw, _stage)]
    res = run_bass_kernel_spmd(nc, in_maps, core_ids=list(range(8)))
    outs = []
    for b in range(4):
        o0 = res.results[2 * b]["out"]
        o1 = res.results[2 * b + 1]["out"]
        outs.append(np.concatenate([o0[:2048], o1[2048:]], 0))
    return np.stack(outs, 0).astype(np.float32)
```

```python
import contextlib
import math
import numpy as np
import ml_dtypes
import concourse.bass as bass
import concourse.mybir as mybir
from concourse.bass_utils import run_bass_kernel_spmd

F32 = mybir.dt.float32
BF16 = mybir.dt.bfloat16
I32 = mybir.dt.int32
AF = mybir.ActivationFunctionType
ALU = mybir.AluOpType

T = 4096
D = 1024
NB = 32
NT = 8
NEG = -30000.0
PAIRS = [[0, 1], [2, 3], [4, 5], [6, 7]]
B_CFG = (1, 4, 16)
LAG = 2


class Tok:
    __slots__ = ("w", "r")

    def __init__(self):
        self.w = None
        self.r = []


class Prog:
    ENG = ("pe", "act", "dve", "pool", "sp")

    def __init__(self, nc, stack, n_dma_sems=20):
        self.nc = nc
        self.q = {e: [] for e in self.ENG}
        self.cnt = {e: 0 for e in self.ENG}
        self.waited = {e: {} for e in self.ENG}
        self.sem = {e: stack.enter_context(nc.semaphore("s_" + e)) for e in self.ENG}
        self.semid = {id(self.sem[e]): e for e in self.ENG}
        self.dsem, self.dval, self.drr = {}, {}, {}
        for qe in ("sp", "pool", "act"):
            self.dsem[qe] = [stack.enter_context(nc.semaphore(f"d_{qe}_{i}")) for i in range(n_dma_sems)]
            self.dval[qe] = [0] * n_dma_sems
            self.drr[qe] = 0
        self.cc_sem = stack.enter_context(nc.semaphore("cc"))
        self.cc_val = 0
        self.n_inst = 0

    def _collect(self, e, reads, writes, extra=()):
        deps = {}

        def add(ev):
            if ev is None:
                return
            s, v = ev
            k = id(s)
            if k not in deps or deps[k][1] < v:
                deps[k] = (s, v)

        for t in reads:
            add(t.w)
        for t in writes:
            add(t.w)
            for ev in t.r:
                add(ev)
        for ev in extra:
            add(ev)
        out = []
        for k, (s, v) in deps.items():
            if self.semid.get(k) == e and e == "pe":
                continue
            if self.waited[e].get(k, 0) >= v:
                continue
            self.waited[e][k] = v
            out.append((s, v))
        return out

    def _mark(self, ev, reads, writes):
        for t in reads:
            t.r.append(ev)
            if len(t.r) > 64:
                t.r = _compress(t.r)
        for t in writes:
            t.w = ev
            t.r = []

    def op(self, e, fn, reads=(), writes=(), extra=()):
        waits = self._collect(e, reads, writes, extra)
        self.cnt[e] += 1
        sem = self.sem[e]
        ev = (sem, self.cnt[e])

        def emit(eng, fn=fn, waits=waits, sem=sem):
            for (s, v) in waits:
                eng.wait_ge(s, v)
            fn(eng).then_inc(sem, 1)

        self.q[e].append(emit)
        self._mark(ev, reads, writes)
        self.n_inst += 1
        return ev

    def dma(self, qe, out, in_, reads=(), writes=(), extra=()):
        k = self.drr[qe]
        self.drr[qe] = (k + 1) % len(self.dsem[qe])
        s = self.dsem[qe][k]
        prev = self.dval[qe][k]
        self.dval[qe][k] = prev + 16
        ev = (s, prev + 16)
        ex = list(extra)
        if prev > 0:
            ex.append((s, prev))
        waits = self._collect(qe, reads, writes, ex)

        def emit(eng, waits=waits, s=s, out=out, in_=in_):
            for (ws, v) in waits:
                eng.wait_ge(ws, v)
            eng.dma_start(out=out, in_=in_).then_inc(s, 16)

        self.q[qe].append(emit)
        self._mark(ev, reads, writes)
        self.n_inst += 1
        return ev

    def collective(self, fn, reads, writes):
        waits = self._collect("pool", reads, writes)
        self.cc_val += 1
        ev = (self.cc_sem, self.cc_val)

        def emit(eng, waits=waits, fn=fn):
            for (s, v) in waits:
                eng.wait_ge(s, v)
            fn(eng).then_inc(self.cc_sem, 1)

        self.q["pool"].append(emit)
        self._mark(ev, reads, writes)
        return ev

    def wait_all(self, e, evs):
        waits = self._collect(e, (), (), evs)

        def emit(eng, waits=waits):
            for (s, v) in waits:
                eng.wait_ge(s, v)

        self.q[e].append(emit)

    def all_events(self):
        evs = [(self.sem[e], self.cnt[e]) for e in self.ENG if self.cnt[e] > 0]
        for qe in self.dsem:
            for s, v in zip(self.dsem[qe], self.dval[qe]):
                if v > 0:
                    evs.append((s, v))
        if self.cc_val > 0:
            evs.append((self.cc_sem, self.cc_val))
        return evs

    def barrier(self):
        evs = self.all_events()
        for e in self.ENG:
            self.wait_all(e, evs)

    def emit_all(self):
        nc = self.nc
        with nc.Block() as block:
            @block.tensor
            def _(eng):
                for f in self.q["pe"]:
                    f(eng)

            @block.scalar
            def _(eng):
                for f in self.q["act"]:
                    f(eng)

            @block.vector
            def _(eng):
                for f in self.q["dve"]:
                    f(eng)

            @block.gpsimd
            def _(eng):
                for f in self.q["pool"]:
                    f(eng)

            @block.sync
            def _(eng):
                for f in self.q["sp"]:
                    f(eng)


def _compress(evs):
    best = {}
    for s, v in evs:
        k = id(s)
        if k not in best or best[k][1] < v:
            best[k] = (s, v)
    return list(best.values())


class Rot:
    def __init__(self, items):
        self.items = [(it, Tok()) for it in items]
        self.i = 0

    def next(self):
        it = self.items[self.i]
        self.i = (self.i + 1) % len(self.items)
        return it


def build_program(nlayers=4, raw=False, stage=99):
    nc = bass.Bass("TRN2", target_bir_lowering=False)

    def din(name, shape, dt):
        return nc.dram_tensor(name, list(shape), dt, kind="ExternalInput").ap()

    x_in = din("x", [T, D], F32)
    pos_in = din("pos", [1, T], I32)
    nw_in = din("nw", [5, D], F32)
    relb_in = din("relb", [33, 8], F32)
    oneh_in = din("oneh", [33, 3 * 383], F32)
    sinks_in = din("sinks", [1, 8], F32)
    ident_in = din("ident", [128, 128], BF16)
    tri_in = din("tri", [128, 128], BF16)
    ropec_in = din("ropec", [128, 2], F32)
    cn_in = din("cn", [1, 2 * 384], F32)
    abw_in = [din(f"abw{i}", [D, 1728], F32) for i in range(2)]
    abwo_in = [din(f"abwo{i}", [D, D], F32) for i in range(2)]
    cw_in = [din(f"cw{i}", [D, 1024], F32) for i in range(2)]
    cqb_in = [din(f"cqb{i}", [256, 1536], F32) for i in range(2)]
    ckvb_in = [din(f"ckvb{i}", [128, 1536], F32) for i in range(2)]
    cwo_in = [din(f"cwo{i}", [D, D], F32) for i in range(2)]
    out = nc.dram_tensor("out", [T, D], F32, kind="ExternalOutput").ap()

    xs = nc.dram_tensor("xs", [T, D], F32).ap()
    go_own = [nc.dram_tensor(f"go_own{t}", [512, 512], BF16) for t in range(NT)]
    go_full = [[nc.dram_tensor(f"go_full{i}_{t}", [1024, 512], BF16) for t in range(NT)] for i in range(2)]
    gd = nc.dram_tensor("gd", [16, 128 * 383], F32)
    vd = nc.dram_tensor("vd", [T, 4 * 65], BF16).ap()
    csd = nc.dram_tensor("csd", [2, 64, T], F32).ap()

    with contextlib.ExitStack() as st:
        def sb(name, shape, dt):
            return st.enter_context(nc.sbuf_tensor(name, list(shape), dt))

        def psum(name, shape, dt):
            return st.enter_context(nc.psum_tensor(name, list(shape), dt))

        pr = Prog(nc, st)

        HT = sb("HT", [128, 8, T], BF16)
        WO = sb("WO", [128, 8, D], BF16)
        QT = sb("QT", [128, T], BF16)
        KT = sb("KT", [128, T], BF16)
        XB = Rot([sb(f"XB{i}", [128, D], F32) for i in range(2)])
        HB = Rot([sb(f"HB{i}", [128, D], BF16) for i in range(2)])
        PT = Rot([sb(f"PT{i}", [128, 512], BF16) for i in range(4)])
        STMP = Rot([sb(f"STMP{i}", [128, 512], F32) for i in range(2)])
        GI = Rot([sb(f"GI{i}", [128, 8, 128], BF16) for i in range(2)])
        GOUT = Rot([sb(f"GOUT{i}", [64, 512], BF16) for i in range(2)])
        WSTG = Rot([sb(f"WSTG{i}", [128, 8, 64], F32) for i in range(3)])
        NW = sb("NW", [128, D], F32)
        ST = Rot([sb(f"ST{i}", [128, 8], F32) for i in range(4)])
        IDENT = sb("IDENT", [128, 128], BF16)
        TRI = sb("TRI", [128, 128], BF16)
        ONES = sb("ONES", [128, 64], F32)
        EPS = sb("EPS", [128, 1], F32)
        ES = sb("ES", [128, 8], F32)
        EPA = Rot([sb(f"EPA{i}", [64, 512], F32) for i in range(2)])
        EPB = Rot([sb(f"EPB{i}", [64, 512], F32) for i in range(2)])
        RD = Rot([sb(f"RD{i}", [65, 512], F32) for i in range(1)])
        ARENA = sb("ARENA", [128, 16384], F32)

        t_HT = [Tok() for _ in range(NB)]
        t_WO, t_QT, t_KT, t_NW = Tok(), Tok(), Tok(), Tok()
        xb_extra = {}
        for nm, big, tbig in (("q", QT, t_QT), ("k", KT, t_KT)):
            v32 = big[:].bitcast(F32)
            for k2 in range(2):
                ap_ = v32[:, k2 * 1024:(k2 + 1) * 1024]
                XB.items.append((ap_, Tok()))
                xb_extra[id(XB.items[-1][1])] = [tbig]
        t_const = Tok()
        t_arena = Tok()

        SBK = Rot([psum(f"S{i}", [128, 512], F32) for i in range(2)])
        OBK = Rot([psum(f"O{i}", [128, 512], F32) for i in range(2)])
        PJ = Rot([psum(f"PJ{i}", [128, 512], F32) for i in range(3)])
        TR = psum("TR", [128, 1024], BF16)
        t_TR = Tok()

        def mm(out_, lhsT, rhs, start, stop, reads, writes):
            return pr.op("pe", lambda e: e.matmul(out_, lhsT=lhsT, rhs=rhs, start=start, stop=stop,
                                                  skip_group_check=True), reads=reads, writes=writes)

        def act(out_, in_, func, reads, writes, **kw):
            return pr.op("act", lambda e: e.activation(out=out_, in_=in_, func=func, **kw), reads=reads, writes=writes)

        def copy_any(eng, out_, in_, reads, writes):
            if eng == "act":
                return pr.op("act", lambda e: e.copy(out=out_, in_=in_), reads=reads, writes=writes)
            return pr.op(eng, lambda e: e.tensor_copy(out=out_, in_=in_), reads=reads, writes=writes)

        def tt(eng, out_, in0, in1, op, reads, writes):
            return pr.op(eng, lambda e: e.tensor_tensor(out=out_, in0=in0, in1=in1, op=op), reads=reads, writes=writes)

        def ts(eng, out_, in0, s1, s2, op0, op1, reads, writes):
            if s2 is None:
                return pr.op(eng, lambda e: e.tensor_scalar(out=out_, in0=in0, scalar1=s1, scalar2=None, op0=op0),
                             reads=reads, writes=writes)
            return pr.op(eng, lambda e: e.tensor_scalar(out=out_, in0=in0, scalar1=s1, scalar2=s2, op0=op0, op1=op1),
                         reads=reads, writes=writes)

        def stt(eng, out_, in0, scalar, in1, op0, op1, reads, writes):
            return pr.op(eng, lambda e: e.scalar_tensor_tensor(out=out_, in0=in0, scalar=scalar, in1=in1, op0=op0, op1=op1),
                         reads=reads, writes=writes)

        def recip(out_, in_, reads, writes):
            return pr.op("dve", lambda e: e.reciprocal(out=out_, in_=in_), reads=reads, writes=writes)

        wrr = [0]

        def load_w(dst, src, ncols, t_dst, engs=("pool", "dve", "act")):
            kch = dst.shape[1]
            c0 = 0
            while c0 < ncols:
                cw = min(64, ncols - c0)
                stg, t_stg = WSTG.next()
                pr.dma("sp", stg[:, 0:kch, 0:cw], src[:, c0:c0 + cw].rearrange("(kc p) n -> p kc n", p=128),
                       writes=[t_stg])
                copy_any(engs[wrr[0] % len(engs)], dst[:, :, c0:c0 + cw], stg[:, 0:kch, 0:cw], [t_stg], [t_dst])
                wrr[0] += 1
                c0 += cw

        pr.dma("sp", IDENT[:], ident_in[:, :], writes=[t_const])
        pr.dma("sp", TRI[:], tri_in[:, :], writes=[t_const])
        pr.op("pool", lambda e: e.memset(ONES[:], 1.0), writes=[t_const])
        pr.op("pool", lambda e: e.memset(EPS[:], 1e-6), writes=[t_const])
        pr.dma("sp", ES[:], sinks_in[0:1, :].partition_broadcast(128), writes=[t_const])
        act(ES[:], ES[:], AF.Exp, [t_const], [t_const])

        def build_tables():
            A32 = ARENA
            RELB = A32[0:33, 0:8]
            ONEH = A32[0:33, 8:8 + 3 * 383]
            LB = A32[0:33, 1200:1328]
            GSB = [A32[:, 1400:1783], A32[:, 1800:2183]]
            t_gsb = [Tok(), Tok()]
            t_gd = Tok()
            t_lb = Tok()
            pr.dma("sp", RELB, relb_in[:, :], writes=[t_arena])
            pr.dma("sp", ONEH, oneh_in[:, :], writes=[t_arena])
            for hc in range(16):
                col = hc if hc < 4 else 4 + (hc - 4) % 4
                cfg = 0 if hc < 8 else (1 if hc < 12 else 2)
                copy_any("dve", LB, RELB[:, col:col + 1].to_broadcast([33, 128]), [t_arena], [t_lb])
                pj, t_pj = PJ.next()
                mm(pj[:, 0:383], LB, ONEH[:, cfg * 383:(cfg + 1) * 383], True, True, [t_lb, t_arena], [t_pj])
                copy_any("dve", GSB[hc % 2], pj[:, 0:383], [t_pj], [t_gsb[hc % 2]])
                pr.dma("sp", gd[hc:hc + 1, :].rearrange("o (p n) -> (o p) n", p=128), GSB[hc % 2], reads=[t_gsb[hc % 2]], writes=[t_gd])

            def skew_ap(hc, prev):
                return bass.AP(tensor=gd.ap().tensor, offset=hc * 128 * 383 + (255 if prev else 127), ap=[[382, 128], [1, 128]])

            ROPEC = A32[:, 2200:2202]
            pr.dma("sp", ROPEC, ropec_in[:, :], writes=[t_arena])
            t_cs = Tok()
            RC = 1024
            r_pi = A32[0:64, 4096:4096 + RC].bitcast(I32)
            r_a = A32[0:64, 5120:5120 + RC]
            r_b = A32[0:64, 6144:6144 + RC]
            r_ti = A32[0:64, 7168:7168 + RC].bitcast(I32)
            r_tf = A32[0:64, 8192:8192 + RC]
            r_r = A32[0:64, 9216:9216 + RC]
            r_m = A32[0:64, 10240:10240 + RC]
            r_o = [A32[0:64, 11264:11264 + RC], A32[0:64, 12288:12288 + RC]]
            tr_ = Tok()
            t_ro = [Tok(), Tok()]
            C1 = 6.28125
            C2 = 2.0 * math.pi - 6.28125
            for ch in range(T // RC):
                pr.dma("sp", r_pi, pos_in[0:1, ch * RC:(ch + 1) * RC].partition_broadcast(64), writes=[tr_])
                copy_any("dve", r_a, r_pi, [tr_], [tr_])
                ts("dve", r_a, r_a, ROPEC[0:64, 0:1], None, ALU.mult, None, [tr_, t_arena], [tr_])
                for which in (0, 1):
                    shift = math.pi / 2 if which == 0 else 0.0
                    ts("dve", r_b, r_a, shift, 1.0 / (2 * math.pi), ALU.add, ALU.mult, [tr_], [tr_])
                    copy_any("dve", r_ti, r_b, [tr_], [tr_])
                    copy_any("dve", r_tf, r_ti, [tr_], [tr_])
                    ts("dve", r_b, r_a, shift, None, ALU.add, None, [tr_], [tr_])
                    stt("dve", r_r, r_tf, -C1, r_b, ALU.mult, ALU.add, [tr_], [tr_])
                    stt("dve", r_r, r_tf, -C2, r_r, ALU.mult, ALU.add, [tr_], [tr_])
                    ts("dve", r_m, r_r, math.pi, None, ALU.is_gt, None, [tr_], [tr_])
                    stt("dve", r_r, r_m, -2 * math.pi, r_r, ALU.mult, ALU.add, [tr_], [tr_])
                    ts("dve", r_m, r_r, -math.pi, None, ALU.is_lt, None, [tr_], [tr_])
                    stt("dve", r_r, r_m, 2 * math.pi, r_r, ALU.mult, ALU.add, [tr_], [tr_])
                    ts("dve", r_r, r_r, -3.14159, 3.14159, ALU.max, ALU.min, [tr_], [tr_])
                    ro, tro = r_o[which], t_ro[which]
                    act(ro, r_r, AF.Sin, [tr_], [tro])
                    if which == 1:
                        ts("dve", ro, ro, ROPEC[0:64, 1:2], None, ALU.mult, None, [tro, t_arena], [tro])
                    pr.dma("sp", csd[which, :, ch * RC:(ch + 1) * RC], ro, reads=[tro], writes=[t_cs])
            return skew_ap, t_gd, t_cs

        def norm_block(tb, xb, t_xb, final=False):
            stt_, t_st = ST.next()
            hb, t_hb = HB.next()
            act(hb[:], xb[:], AF.Square, [t_xb], [t_hb, t_st], accum_out=stt_[:, 0:1])
            act(stt_[:, 1:2], stt_[:, 0:1], AF.Ln, [t_st, t_const], [t_st], scale=1.0 / D, bias=EPS[:, 0:1])
            act(stt_[:, 2:3], stt_[:, 1:2], AF.Exp, [t_st], [t_st], scale=-0.5)
            if final:
                if not raw:
                    stt("dve", xb[:], xb[:], stt_[:, 2:3], NW[:], ALU.mult, ALU.mult, [t_xb, t_st, t_NW], [t_xb])
                return pr.dma("pool", out[tb * 128:(tb + 1) * 128, :], xb[:], reads=[t_xb])
            stt("dve", hb[:], xb[:], stt_[:, 2:3], NW[:], ALU.mult, ALU.mult, [t_xb, t_st, t_NW], [t_hb])
            for kc in range(8):
                pr.op("pe", lambda e, kc=kc: e.transpose(out=TR[:, kc * 128:(kc + 1) * 128], in_=hb[:, kc * 128:(kc + 1) * 128],
                                                         identity=IDENT[:]),
                      reads=[t_hb, t_const], writes=[t_TR])
            copy_any("act", HT[:, :, tb * 128:(tb + 1) * 128], TR[:].rearrange("p (k t) -> p k t", k=8), [t_TR], [t_HT[tb]])
            return None

        def load_nw(row):
            pr.dma("sp", NW[:], nw_in[row:row + 1, :].partition_broadcast(128), writes=[t_NW])

        def epilogue(num, den, src_reads, ti, wz_lhsT, wz_tok, row0, sink_col=None):
            pj, t_pj = PJ.next()
            for kc in range(8):
                mm(pj[0:64, :], wz_lhsT(kc), HT[:, kc, ti * 512:(ti + 1) * 512], kc == 0, kc == 7,
                   [wz_tok] + t_HT[ti * 4:ti * 4 + 4], [t_pj])
            ea, t_ea = EPA.next()
            eb, t_eb = EPB.next()
            rd, t_rd = RD.next()
            act(ea[:], pj[0:64, :], AF.Exp, [t_pj], [t_ea], scale=-1.0)
            if sink_col is not None:
                ts("dve", rd[64:65, :], den, ES[64:65, sink_col:sink_col + 1], None, ALU.add, None, src_reads + [t_const], [t_rd])
            else:
                copy_any("dve", rd[64:65, :], den, src_reads, [t_rd])
            bc, t_bc = PJ.next()
            mm(bc[0:64, :], ONES[64:65, 0:64], rd[64:65, :], True, True, [t_rd, t_const], [t_bc])
            stt("dve", eb[:], ea[:], 1.0, bc[0:64, :], ALU.add, ALU.mult, [t_ea, t_bc], [t_eb])
            act(eb[:], eb[:], AF.Ln, [t_eb], [t_eb])
            act(eb[:], eb[:], AF.Exp, [t_eb], [t_eb], scale=-1.0)
            tt("dve", ea[:], pj[0:64, :], eb[:], ALU.mult, [t_pj, t_eb, t_ea], [t_ea])
            go, t_go = GOUT.next()
            tt("dve", go[:], num, ea[:], ALU.mult, src_reads + [t_ea], [t_go])
            pr.dma("pool", go_own[ti][row0:row0 + 64, :], go[:], reads=[t_go], writes=[t_go_own[ti]])

        t_go_own = [Tok() for _ in range(NT)]
        t_go_full = [[Tok() for _ in range(NT)] for _ in range(2)]
        t_xs = [Tok() for _ in range(NB)]

        def ab_layer(li):
            i = li // 2
            AB16 = ARENA.bitcast(BF16)
            o = 0
            WH = AB16[:, o:o + 8 * 1728].rearrange("p (k n) -> p k n", k=8); o += 8 * 1728
            VA = AB16[:, o:o + 32 * 65].rearrange("p (m e) -> p m e", m=32); o += 32 * 65
            VS = AB16[:, o:o + 3 * 32 * 65].rearrange("p (c m e) -> p c m e", c=3, m=32); o += 3 * 32 * 65
            VST = [AB16[:, o + k * 260:o + (k + 1) * 260].rearrange("p (h e) -> p h e", h=4) for k in range(2)]; o += 520
            o32 = (o + 1) // 2
            U = ARENA[0:65, o32:o32 + T]; o32 += T
            BT = ARENA[:, o32:o32 + 3 * 256].rearrange("p (c n) -> p c n", c=3); o32 += 768
            assert o32 <= 16384, o32
            vst = Rot(VST)
            t_WH, t_VN, t_VS, t_U, t_BT, t_vd = Tok(), Tok(), Tok(), Tok(), Tok(), Tok()

            load_w(WH, abw_in[i], 1728, t_WH)
            pr.op("pool", lambda e: e.memset(VA[:, :, 64:65], 1.0), writes=[t_VN])
            pr.op("pool", lambda e: e.memset(VS[:, :, :, 64:65], 1.0), writes=[t_VS])
            for (v_, tv_) in vst.items:
                pr.op("pool", lambda e, v_=v_: e.memset(v_[:, :, 64:65], 1.0), writes=[tv_])

            for blk in range(NB):
                pj, t_pj = PJ.next()
                for kc in range(8):
                    mm(pj[:, 0:320], HT[:, kc, blk * 128:(blk + 1) * 128], WH[:, kc, 1408:1728], kc == 0, kc == 7,
                       [t_HT[blk], t_WH], [t_pj])
                copy_any("dve", VA[:, blk, 0:64], pj[:, 0:64], [t_pj], [t_VN])
                v_, tv_ = vst.next()
                copy_any("dve", v_[:, :, 0:64], pj[:, 64:320].rearrange("p (h e) -> p h e", h=4), [t_pj], [tv_])
                pr.dma("pool", vd[blk * 128:(blk + 1) * 128, :], v_, reads=[tv_], writes=[t_vd])

            if stage < 3:
                return
            def proj_fm(dst, col0, alt, nc_=128):
                for ti in range(NT):
                    pj, t_pj = PJ.next()
                    for kc in range(8):
                        mm(pj[0:nc_, :], WH[:, kc, col0:col0 + nc_], HT[:, kc, ti * 512:(ti + 1) * 512], kc == 0, kc == 7,
                           [t_WH] + t_HT[ti * 4:ti * 4 + 4], [t_pj])
                    copy_any("act" if (ti + alt) % 2 == 0 else "dve", dst[0:nc_, ti * 512:(ti + 1) * 512], pj[0:nc_, :], [t_pj],
                             [t_QT if dst is QT else t_KT])

            def banded(d, vget, bt, consume, pb=0):
                L = T // d
                tl = min(512, L)
                steps = []
                for r in range(d):
                    for t0 in range(0, L, tl):
                        m0 = t0 // 128
                        nb_t = tl // 128
                        groups = []
                        mm_ = m0
                        while mm_ < m0 + nb_t:
                            if mm_ == 0 or mm_ == m0 + nb_t - 1:
                                groups.append([mm_]); mm_ += 1
                            else:
                                groups.append([mm_, mm_ + 1]); mm_ += 2
                        for gx, g in enumerate(groups):
                            steps.append(dict(r=r, t0=t0, m0=m0, g=g, firstg=(gx == 0), lastg=(gx == len(groups) - 1)))

                def front(stp):
                    r, g = stp["r"], stp["g"]
                    sbk, t_s = SBK.next()
                    stmp, t_stmp = STMP.next()
                    pt, t_pt = PT.next()
                    stp["pt"] = (pt, t_pt)
                    ng = len(g)
                    first = (g[0] == 0)
                    for gi, m in enumerate(g):
                        qa = QT[pb:pb + 64, r + d * 128 * m: r + d * 128 * m + d * 127 + 1: d]
                        for half in (0, 1):
                            if m == 0 and half == 0:
                                continue
                            kb = m - 1 + half
                            ka = KT[pb:pb + 64, r + d * 128 * kb: r + d * 128 * kb + d * 127 + 1: d]
                            mm(sbk[:, gi * 256 + half * 128: gi * 256 + half * 128 + 128], ka, qa, True, True,
                               [t_QT, t_KT], [t_s])
                    c0 = 128 if first else 0
                    c1 = ng * 256
                    if first:
                        stt("dve", stmp[:, c0:c1], sbk[:, c0:c1], 0.125, bt[:, 128:256], ALU.mult, ALU.add,
                            [t_s, t_BT], [t_stmp])
                    else:
                        stt("dve", stmp[:, c0:c1].rearrange("p (g n) -> p g n", g=ng),
                            sbk[:, c0:c1].rearrange("p (g n) -> p g n", g=ng), 0.125,
                            bt.unsqueeze(1).to_broadcast([128, ng, 256]), ALU.mult, ALU.add, [t_s, t_BT], [t_stmp])
                    act(pt[:, c0:c1], stmp[:, c0:c1], AF.Exp, [t_stmp], [t_pt])

                cur = [None]

                def back(stp):
                    r, g, m0 = stp["r"], stp["g"], stp["m0"]
                    pt, t_pt = stp["pt"]
                    if stp["firstg"]:
                        cur[0] = OBK.next()
                    ob, t_ob = cur[0]
                    for gi, m in enumerate(g):
                        oc = (m - m0) * 128
                        for half in (0, 1):
                            if m == 0 and half == 0:
                                continue
                            kb = m - 1 + half
                            mm(ob[0:65, oc:oc + 128], vget(r, kb), pt[:, gi * 256 + half * 128: gi * 256 + half * 128 + 128],
                               (half == 0) or (m == 0), half == 1, [t_pt, t_VN, t_VS], [t_ob])
                    if stp["lastg"]:
                        consume(ob, t_ob, r + d * stp["t0"], tl, d)

                for ix in range(len(steps) + LAG):
                    if ix < len(steps):
                        front(steps[ix])
                    if ix - LAG >= 0:
                        back(steps[ix - LAG])

            def load_bt(hcs):
                for ci, hc in enumerate(hcs):
                    pr.dma("sp", BT[:, ci, 0:128], skew_ap(hc, True), reads=[t_gd], writes=[t_BT])
                    pr.dma("sp", BT[:, ci, 128:256], skew_ap(hc, False), reads=[t_gd], writes=[t_BT])

            proj_fm(KT, 512, 1)
            for j in range(4):
                if j % 2 == 0:
                    proj_fm(QT, (j // 2) * 128, 0)
                load_bt([j])

                def consume_a(ob, t_ob, tok0, ntok, stride, j=j):
                    ti = tok0 // 512
                    epilogue(ob[0:64, :], ob[64:65, :], [t_ob], ti, lambda kc, j=j: WH[:, kc, 256 + j * 64:256 + (j + 1) * 64], t_WH,
                             j * 64, sink_col=i * 4 + j)
                banded(1, lambda r, kb: VA[:, kb, :], BT[:, 0, :], consume_a, pb=64 * (j % 2))
                load_w(WO[:, :, j * 128:(j + 1) * 128], abwo_in[i][:, j * 128:(j + 1) * 128], 128, t_WO, engs=("pool",))

            if stage < 4:
                return
            for j in range(4):
                if j % 2 == 0:
                    proj_fm(QT, 640 + (j // 2) * 256, 0)
                    proj_fm(KT, 640 + (j // 2) * 256 + 128, 1)
                zc = 1152 + j * 64
                load_bt([4 + j, 8 + j, 12 + j])
                for ci, d in enumerate(B_CFG):
                    src = bass.AP(tensor=vd.tensor, offset=j * 65,
                                  ap=[[d * 260, 128], [260, d], [128 * d * 260, 32 // d], [1, 65]])
                    pr.dma("sp", VS[:, ci, :, :].rearrange("p (r m) e -> p r m e", r=d), src, reads=[t_vd], writes=[t_VS])
                for ci, d in enumerate(B_CFG):
                    def consume_b(ob, t_ob, tok0, ntok, stride, ci=ci):
                        ua = U[:, tok0: tok0 + stride * (ntok - 1) + 1: stride]
                        if ci == 0:
                            copy_any("dve", ua, ob[0:65, 0:ntok], [t_ob], [t_U])
                        else:
                            tt("dve", ua, ua, ob[0:65, 0:ntok], ALU.add, [t_ob, t_U], [t_U])
                    vget = lambda r, kb, d=d, ci=ci: VS[:, ci, r * (32 // d) + kb, :]
                    banded(d, vget, BT[:, ci, :], consume_b, pb=64 * (j % 2))
                for ti in range(NT):
                    epilogue(U[0:64, ti * 512:(ti + 1) * 512], U[64:65, ti * 512:(ti + 1) * 512], [t_U], ti,
                             lambda kc, zc=zc: WH[:, kc, zc:zc + 64], t_WH, 256 + j * 64)
                load_w(WO[:, :, (4 + j) * 128:(5 + j) * 128], abwo_in[i][:, (4 + j) * 128:(5 + j) * 128], 128, t_WO, engs=("pool",))

        def c_layer(li):
            i = li // 2
            C16 = ARENA.bitcast(BF16)
            o = 0
            WC = C16[:, o:o + 8 * 1024].rearrange("p (k n) -> p k n", k=8); o += 8 * 1024
            WQB = C16[:, o:o + 2 * 1536].rearrange("p (k n) -> p k n", k=2); o += 2 * 1536
            WKVB = C16[:, o:o + 1536].rearrange("p (k n) -> p k n", k=1); o += 1536
            LAT = C16[:, o:o + 3 * T].rearrange("p (c t) -> p c t", c=3); o += 3 * T
            VM = C16[:, o:o + 32 * 65].rearrange("p (m e) -> p m e", m=32); o += 32 * 65
            CNB = C16[:, o:o + 384]; o += 384
            o32 = (o + 1) // 2
            CS = [ARENA[0:64, o32 + k * 512: o32 + (k + 1) * 512] for k in range(4)]; o32 += 2048
            CNW = ARENA[:, o32:o32 + 384]; o32 += 384
            assert o32 <= 16384, o32
            t_WC, t_WQB, t_WKVB, t_LAT, t_KPE, t_VM, t_CNB, t_CNW = (Tok() for _ in range(8))
            csr = Rot([(CS[0], CS[1]), (CS[2], CS[3])])

            load_w(WC, cw_in[i], 1024, t_WC)
            load_w(WQB, cqb_in[i], 1536, t_WQB)
            load_w(WKVB, ckvb_in[i], 1536, t_WKVB)
            pr.dma("sp", CNW, cn_in[0:1, i * 384:(i + 1) * 384].partition_broadcast(128), writes=[t_CNW])
            pr.op("pool", lambda e: e.memset(VM[:, :, 64:65], 1.0), writes=[t_VM])

            for tb in range(NB):
                pj, t_pj = PJ.next()
                for kc in range(8):
                    mm(pj[:, 0:384], HT[:, kc, tb * 128:(tb + 1) * 128], WC[:, kc, 0:384], kc == 0, kc == 7,
                       [t_HT[tb], t_WC], [t_pj])
                stt_, t_st = ST.next()
                stmp, t_stmp = STMP.next()
                act(stmp[:, 0:256], pj[:, 0:256], AF.Square, [t_pj], [t_stmp, t_st], accum_out=stt_[:, 0:1])
                act(stmp[:, 256:384], pj[:, 256:384], AF.Square, [t_pj], [t_stmp, t_st], accum_out=stt_[:, 1:2])
                act(stt_[:, 2:3], stt_[:, 0:1], AF.Ln, [t_st, t_const], [t_st], scale=1.0 / 256, bias=EPS[:, 0:1])
                act(stt_[:, 3:4], stt_[:, 1:2], AF.Ln, [t_st, t_const], [t_st], scale=1.0 / 128, bias=EPS[:, 0:1])
                act(stt_[:, 4:6], stt_[:, 2:4], AF.Exp, [t_st], [t_st], scale=-0.5)
                stt("dve", CNB[:, 0:256], pj[:, 0:256], stt_[:, 4:5], CNW[:, 0:256], ALU.mult, ALU.mult,
                    [t_pj, t_st, t_CNW], [t_CNB])
                stt("dve", CNB[:, 256:384], pj[:, 256:384], stt_[:, 5:6], CNW[:, 256:384], ALU.mult, ALU.mult,
                    [t_pj, t_st, t_CNW], [t_CNB])
                for c3 in range(3):
                    pr.op("pe", lambda e, c3=c3: e.transpose(out=TR[:, c3 * 128:(c3 + 1) * 128], in_=CNB[:, c3 * 128:(c3 + 1) * 128],
                                                             identity=IDENT[:]),
                          reads=[t_CNB, t_const], writes=[t_TR])
                copy_any("act", LAT[:, :, tb * 128:(tb + 1) * 128], TR[:, 0:384].rearrange("p (c t) -> p c t", c=3), [t_TR], [t_LAT])

            def rope_tile(ti, pj_main, t_main, pj_sw, t_sw, dst, t_dst):
                (cc, ss), t_cs_t = csr.next()
                pr.dma("sp", cc, csd[0, :, ti * 512:(ti + 1) * 512], reads=[t_cs], writes=[t_cs_t])
                pr.dma("sp", ss, csd[1, :, ti * 512:(ti + 1) * 512], reads=[t_cs], writes=[t_cs_t])
                ea, t_ea = EPA.next()
                eb, t_eb = EPB.next()
                tt("dve", ea[:], pj_main[0:64, :], cc, ALU.mult, [t_main, t_cs_t], [t_ea])
                tt("dve", eb[:], pj_sw[0:64, :], ss, ALU.mult, [t_sw, t_cs_t], [t_eb])
                tt("pool", dst[0:64, ti * 512:(ti + 1) * 512], ea[:], eb[:], ALU.add, [t_ea, t_eb], [t_dst])

            for ti in range(NT):
                pa, t_pa = PJ.next()
                pb, t_pb = PJ.next()
                for kc in range(8):
                    mm(pa[0:64, :], WC[:, kc, 384:448], HT[:, kc, ti * 512:(ti + 1) * 512], kc == 0, kc == 7,
                       [t_WC] + t_HT[ti * 4:ti * 4 + 4], [t_pa])
                for kc in range(8):
                    mm(pb[0:64, :], WC[:, kc, 448:512], HT[:, kc, ti * 512:(ti + 1) * 512], kc == 0, kc == 7,
                       [t_WC] + t_HT[ti * 4:ti * 4 + 4], [t_pb])
                rope_tile(ti, pa, t_pa, pb, t_pb, KT, t_KT)

            scale = (64 + 32) ** -0.5
            for h in range(8):
                for ti in range(NT):
                    pj, t_pj = PJ.next()
                    mm(pj[:, :], WKVB[:, 0, h * 128:(h + 1) * 128], LAT[:, 2, ti * 512:(ti + 1) * 512], True, True,
                       [t_WKVB, t_LAT], [t_pj])
                    copy_any("act", KT[64:128, ti * 512:(ti + 1) * 512], pj[64:128, :], [t_pj], [t_KT])
                for b0 in range(0, NB, 8):
                    pj, t_pj = PJ.next()
                    for bb in range(8):
                        mm(pj[:, bb * 64:(bb + 1) * 64], LAT[:, 2, (b0 + bb) * 128:(b0 + bb + 1) * 128],
                           WKVB[:, 0, 1024 + h * 64:1024 + (h + 1) * 64], True, True, [t_WKVB, t_LAT], [t_pj])
                    copy_any("dve", VM[:, b0:b0 + 8, 0:64], pj[:, :].rearrange("p (m e) -> p m e", m=8), [t_pj], [t_VM])
                for ti in range(NT):
                    pa, t_pa = PJ.next()
                    pb, t_pb = PJ.next()
                    for c2 in range(2):
                        mm(pa[:, :], WQB[:, c2, h * 128:(h + 1) * 128], LAT[:, c2, ti * 512:(ti + 1) * 512], c2 == 0, c2 == 1,
                           [t_WQB, t_LAT], [t_pa])
                    for c2 in range(2):
                        mm(pb[0:64, :], WQB[:, c2, 1024 + h * 64:1024 + (h + 1) * 64], LAT[:, c2, ti * 512:(ti + 1) * 512],
                           c2 == 0, c2 == 1, [t_WQB, t_LAT], [t_pb])
                    copy_any("act", QT[64:128, ti * 512:(ti + 1) * 512], pa[64:128, :], [t_pa], [t_QT])
                    rope_tile(ti, pa, t_pa, pb, t_pb, QT, t_QT)
                steps = []
                for j in range(NT):
                    nkb = 4 * j + 4
                    for kb in range(nkb):
                        steps.append(dict(j=j, kb=kb, nkb=nkb))

                def front(stp):
                    j, kb = stp["j"], stp["kb"]
                    c0 = max(0, kb - 4 * j) * 128
                    sbk, t_s = SBK.next()
                    pt, t_pt = PT.next()
                    stp["pt"] = (pt, t_pt)
                    mm(sbk[:, c0:512], KT[:, kb * 128:(kb + 1) * 128], QT[:, j * 512 + c0:(j + 1) * 512], True, True,
                       [t_QT, t_KT], [t_s])
                    act(pt[:, c0:512], sbk[:, c0:512], AF.Exp, [t_s], [t_pt], scale=scale)
                    if kb >= 4 * j:
                        tt("pool", pt[:, c0:c0 + 128], pt[:, c0:c0 + 128], TRI[:], ALU.mult, [t_pt, t_const], [t_pt])

                cur = [None]

                def back(stp, h=h):
                    j, kb, nkb = stp["j"], stp["kb"], stp["nkb"]
                    c0 = max(0, kb - 4 * j) * 128
                    pt, t_pt = stp["pt"]
                    if kb == 0:
                        cur[0] = OBK.next()
                    ob, t_ob = cur[0]
                    mm(ob[0:65, c0:512], VM[:, kb, :], pt[:, c0:512], kb == 0, kb == nkb - 1, [t_pt, t_VM], [t_ob])
                    if kb == nkb - 1:
                        epilogue(ob[0:64, :], ob[64:65, :], [t_ob], j, lambda kc, h=h: WC[:, kc, 512 + h * 64:512 + (h + 1) * 64],
                                 t_WC, h * 64)

                for ix in range(len(steps) + LAG):
                    if ix < len(steps):
                        front(steps[ix])
                    if ix - LAG >= 0:
                        back(steps[ix - LAG])
                load_w(WO[:, :, h * 128:(h + 1) * 128], cwo_in[i][:, h * 128:(h + 1) * 128], 128, t_WO, engs=("pool",))

        def phase_o(li, last):
            gf = go_full[li % 2]
            tgf = t_go_full[li % 2]
            evs = []
            banks = Rot([None])
            banks.items = PJ.items + SBK.items + OBK.items
            blk = {}

            def front(tb):
                gi, t_gi = GI.next()
                pr.dma("sp", gi[:], gf[tb // 4][:, (tb % 4) * 128:(tb % 4 + 1) * 128].rearrange("(kc p) t -> p kc t", p=128),
                       reads=[tgf[tb // 4]], writes=[t_gi])
                xb, t_xb = XB.next()
                xw = [t_xb] + xb_extra.get(id(t_xb), [])
                if li == 0:
                    pr.dma("sp", xb[:], x_in[tb * 128:(tb + 1) * 128, :], writes=xw)
                else:
                    pr.dma("sp", xb[:], xs[tb * 128:(tb + 1) * 128, :], reads=[t_xs[tb]], writes=xw)
                for half in range(2):
                    pj, t_pj = banks.next()
                    for kc in range(8):
                        mm(pj[:, :], gi[:, kc, :], WO[:, kc, half * 512:(half + 1) * 512], kc == 0, kc == 7, [t_gi, t_WO], [t_pj])
                    tt("dve", xb[:, half * 512:(half + 1) * 512], xb[:, half * 512:(half + 1) * 512], pj[:, :], ALU.add,
                       [t_xb, t_pj], [t_xb])
                if not last:
                    pr.dma("pool", xs[tb * 128:(tb + 1) * 128, :], xb[:], reads=[t_xb], writes=[t_xs[tb]])
                blk[tb] = (xb, t_xb)

            def back(tb):
                xb, t_xb = blk.pop(tb)
                ev = norm_block(tb, xb, t_xb, final=last)
                if ev is not None:
                    evs.append(ev)

            for ix in range(NB + 1):
                if ix < NB:
                    front(ix)
                if ix >= 1:
                    back(ix - 1)
            return evs

        load_nw(0)
        for tb in range(NB if stage >= 1 else 0):
            xb, t_xb = XB.next()
            pr.dma("sp", xb[:], x_in[tb * 128:(tb + 1) * 128, :], writes=[t_xb] + xb_extra.get(id(t_xb), []))
            norm_block(tb, xb, t_xb)
        skew_ap, t_gd, t_cs = build_tables()
        final_evs = []
        for li in range(nlayers if stage >= 2 else 0):
            pr.barrier()
            if li % 2 == 0:
                ab_layer(li)
            else:
                c_layer(li)
            last = (li == nlayers - 1)
            if stage < 5:
                break
            gf = go_full[li % 2]
            for ti in range(NT):
                pr.collective(lambda e, gf=gf, ti=ti: e.collective_compute("AllGather", ALU.bypass, replica_groups=PAIRS,
                                                                           ins=[go_own[ti].ap().opt()], outs=[gf[ti].ap().opt()]),
                              reads=[t_go_own[ti]], writes=[t_go_full[li % 2][ti]])
            if stage < 6:
                break
            load_nw(4 if (last and nlayers == 4) else li + 1)
            final_evs = phase_o(li, last)
        pr.wait_all("pool", final_evs)
        pr.barrier()
        pr.emit_all()
        print("instructions:", pr.n_inst, {e: pr.cnt[e] for e in pr.ENG}, "sbuf left", nc.sbuf_bytes_remaining)
    return nc


def _t5_bucket(dist):
    dist = np.asarray(dist, dtype=np.int64)
    d = np.maximum(dist, 1).astype(np.float32)
    large = 16 + (np.log(d / np.float32(16)) / np.float32(math.log(2048 / 16)) * np.float32(16)).astype(np.int32)
    large = np.minimum(large, 31)
    return np.where(dist < 16, dist, large)


def _consts():
    oneh = np.zeros((33, 3, 383), np.float32)
    for ci, d in enumerate(B_CFG):
        for n in range(383):
            dist = n - 127
            if 0 <= dist <= 128:
                oneh[_t5_bucket(dist * d), ci, n] = 1.0
            else:
                oneh[32, ci, n] = NEG
    kk = np.arange(128)[:, None]
    qq = np.arange(128)[None, :]
    tri = (qq >= kk).astype(np.float32).astype(ml_dtypes.bfloat16)
    ident = np.eye(128, dtype=np.float32).astype(ml_dtypes.bfloat16)
    inv_freq = (10000.0 ** (-np.arange(0, 32, 2, dtype=np.float32) / np.float32(32))).astype(np.float32)
    ropec = np.zeros((128, 2), np.float32)
    ropec[0:16, 0] = inv_freq
    ropec[32:48, 0] = inv_freq
    ropec[0:16, 1] = -1.0
    ropec[32:48, 1] = 1.0
    return oneh.reshape(33, 3 * 383), tri, ident, ropec


def _core_inputs(c, inp, consts):
    b, hh = c // 2, c % 2
    oneh, tri, ident, ropec = consts
    f = np.float32
    m = {}
    m["x"] = np.ascontiguousarray(inp["x"][b], dtype=f)
    m["pos"] = np.ascontiguousarray(inp["positions"][b][None, :], dtype=np.int32)
    m["nw"] = np.ascontiguousarray(np.concatenate([inp["norm_w"], inp["final_norm"][None, :]], 0), dtype=f)
    rb = inp["rel_bias"]
    relb = np.ones((33, 8), f)
    relb[:32, 0:4] = rb[:, hh * 4:hh * 4 + 4]
    relb[:32, 4:8] = rb[:, 8 + hh * 4:8 + hh * 4 + 4]
    m["relb"] = relb
    m["oneh"] = oneh
    m["sinks"] = np.ascontiguousarray(inp["ab_sinks"][:, hh * 4:hh * 4 + 4].reshape(1, 8), dtype=f)
    m["ident"] = ident
    m["tri"] = tri
    m["ropec"] = ropec
    m["cn"] = np.ascontiguousarray(np.concatenate(
        [np.concatenate([inp["c_q_norm"][i], inp["c_kv_norm"][i]]) for i in range(2)])[None, :], dtype=f)
    for i in range(2):
        w = inp["ab_w_in"][i]
        qa, ka, va, qb, kb, vb, z = np.split(w, np.cumsum([512, 128, 128, 512, 512, 512]), axis=1)
        hs = [hh * 4 + j for j in range(4)]
        cols = [qa[:, h * 64:(h + 1) * 64] for h in hs]
        cols += [z[:, h * 64:(h + 1) * 64] for h in hs]
        cols += [ka[:, hh * 64:(hh + 1) * 64]] * 2
        for p in range(2):
            cols += [qb[:, h * 64:(h + 1) * 64] for h in hs[2 * p:2 * p + 2]]
            cols += [kb[:, h * 64:(h + 1) * 64] for h in hs[2 * p:2 * p + 2]]
        cols += [z[:, 512 + h * 64:512 + (h + 1) * 64] for h in hs]
        cols.append(va[:, hh * 64:(hh + 1) * 64])
        cols.append(vb[:, hh * 256:(hh + 1) * 256])
        m[f"abw{i}"] = np.ascontiguousarray(np.concatenate(cols, 1), dtype=f)
        assert m[f"abw{i}"].shape == (1024, 1728)
        wo = inp["ab_w_out"][i]
        m[f"abwo{i}"] = np.ascontiguousarray(np.concatenate([wo[0:256], wo[512:768], wo[256:512], wo[768:1024]], 0), dtype=f)
        w = inp["c_w_in"][i]
        kpe = w[:, 384:416]
        z16 = np.zeros((1024, 16), f)
        kp_pad = np.concatenate([kpe[:, 0:16], z16, kpe[:, 16:32], z16], 1)
        kp_sw = np.concatenate([kpe[:, 16:32], z16, kpe[:, 0:16], z16], 1)
        m[f"cw{i}"] = np.ascontiguousarray(np.concatenate([w[:, 0:384], kp_pad, kp_sw, w[:, 416 + hh * 512:416 + (hh + 1) * 512]], 1), dtype=f)
        wq = inp["c_w_qb"][i].reshape(256, 16, 96)
        z16q = np.zeros((256, 16), f)
        pads, sws = [], []
        for j in range(8):
            h = hh * 8 + j
            nope, pe = wq[:, h, 0:64], wq[:, h, 64:96]
            pads.append(np.concatenate([pe[:, 0:16], z16q, pe[:, 16:32], z16q, nope], 1))
            sws.append(np.concatenate([pe[:, 16:32], z16q, pe[:, 0:16], z16q], 1))
        m[f"cqb{i}"] = np.ascontiguousarray(np.concatenate(pads + sws, 1), dtype=f)
        wkv = inp["c_w_kvb"][i].reshape(128, 16, 128)
        z64 = np.zeros((128, 64), f)
        ks, vs = [], []
        for j in range(8):
            h = hh * 8 + j
            ks.append(np.concatenate([z64, wkv[:, h, 0:64]], 1))
            vs.append(wkv[:, h, 64:128])
        m[f"ckvb{i}"] = np.ascontiguousarray(np.concatenate(ks + vs, 1), dtype=f)
        m[f"cwo{i}"] = np.ascontiguousarray(inp["c_w_out"][i], dtype=f)
    return m


_NC_CACHE = {}


def kernel(x, positions, norm_w, rel_bias, ab_w_in, ab_sinks, ab_w_out, c_w_in, c_q_norm, c_w_qb,
           c_kv_norm, c_w_kvb, c_w_out, final_norm, _nlayers=4, _raw=False, _stage=99):
    inp = dict(x=x, positions=positions, norm_w=norm_w, rel_bias=rel_bias, ab_w_in=ab_w_in, ab_sinks=ab_sinks,
               ab_w_out=ab_w_out, c_w_in=c_w_in, c_q_norm=c_q_norm, c_w_qb=c_w_qb, c_kv_norm=c_kv_norm,
               c_w_kvb=c_w_kvb, c_w_out=c_w_out, final_norm=final_norm)
    inp = {k: np.asarray(v) for k, v in inp.items()}
    consts = _consts()
    in_maps = [_core_inputs(c, inp, consts) for c in range(8)]
    if (_nlayers, _raw, _stage) not in _NC_CACHE:
        _NC_CACHE[(_nlayers, _raw, _stage)] = build_program(_nlayers, _raw, _stage)
    nc = _NC_CACHE[(_nlayers, _raw, _stage)]
    res = run_bass_kernel_spmd(nc, in_maps, core_ids=list(range(8)))
    outs = []
    for b in range(4):
        o0 = res.results[2 * b]["out"]
        o1 = res.results[2 * b + 1]["out"]
        outs.append(np.concatenate([o0[:2048], o1[2048:]], 0))
    return np.stack(outs, 0).astype(np.float32)
```
